# Optimizing a Trainium2 kernel written in Bass

```python
import jax
import jax.numpy as jnp
from jax import lax
import numpy as np

D_MODEL = 1024
BATCH = 2
SEQ = 8192
DEPTH = 2
DEC_BATCH = 32
DEC_SEQ = 1
PAST_LEN = 8192
PAGE_SIZE = 128

HEAD_DIM = 64
N_A_LAYERS = DEPTH // 2
N_B_LAYERS = DEPTH - N_A_LAYERS
DIL_GROUPS = ((128, 1), (512, 4), (2048, 16))
N_DIL = len(DIL_GROUPS)
DIL_HEADS = 4
DIL_COLS = N_DIL * 3 * DIL_HEADS * HEAD_DIM
MEM_TOKENS = 256
MEM_HEADS = 4
MEM_COLS = MEM_HEADS * HEAD_DIM
NSA_HEADS = 12
NSA_KV_HEADS = 2
NSA_GROUP = NSA_HEADS // NSA_KV_HEADS
NSA_Q_COLS = NSA_HEADS * HEAD_DIM
NSA_GATE_COLS = NSA_HEADS * 3
CMP_BLOCK = 32
CMP_STRIDE = 16
CMP_HIDDEN = 2 * HEAD_DIM
SLC_BLOCK = 64
SLC_TOPK = 16
SWA_WINDOW = 512
FORCE_SCORE = 1.0e9
PEER_KEYS = 128
PEER_EXPERTS = PEER_KEYS * PEER_KEYS
PEER_HEADS = 8
PEER_KEY_DIM = 128
PEER_TOPK = 16
PEER_BLOCK = 128
Q_BLOCK = 128
ROPE_THETA = 10000.0
ALPHA = (2 * DEPTH) ** 0.25
BETA = (8 * DEPTH) ** -0.25
LN_EPS = 1e-5
NEG = -1.0e30
SCALE = HEAD_DIM ** -0.5

kernel_name = 'yoco_dilated_nsa_peer_decoder_step'


def layer_norm(x, g, b):
    xf = x.astype(jnp.float32)
    mu = jnp.mean(xf, axis=-1, keepdims=True)
    var = jnp.mean(jnp.square(xf - mu), axis=-1, keepdims=True)
    return ((xf - mu) * lax.rsqrt(var + LN_EPS) * g + b).astype(x.dtype)


def rope(x, pos):
    half = HEAD_DIM // 2
    inv = ROPE_THETA ** (-jnp.arange(half, dtype=jnp.float32) / half)
    ang = pos.astype(jnp.float32)[:, None] * inv[None, :]
    ang = ang.reshape((pos.shape[0],) + (1,) * (x.ndim - 3) + (half,))
    cos, sin = jnp.cos(ang), jnp.sin(ang)
    xf = x.astype(jnp.float32)
    x1, x2 = xf[..., :half], xf[..., half:]
    return jnp.concatenate([x1 * cos - x2 * sin, x2 * cos + x1 * sin], axis=-1).astype(x.dtype)


def masked_softmax(s, mask):
    p = jax.nn.softmax(jnp.where(mask, s, NEG), axis=-1)
    return jnp.where(mask, p, 0.0)


def map_query_blocks(fn, *qs):
    n, t = qs[0].shape[:2]
    nb = t // Q_BLOCK
    blocks = tuple(jnp.moveaxis(a.reshape((n, nb, Q_BLOCK) + a.shape[2:]), 1, 0) for a in qs)
    starts = jnp.arange(nb, dtype=jnp.int32) * Q_BLOCK
    out = lax.map(lambda args: fn(args[0], *args[1:]), (starts,) + blocks)
    out = jnp.moveaxis(out, 0, 1)
    return out.reshape((n, t) + out.shape[3:])


def dilated_group_attn(q, k, v, q_idx, window, dil):
    taps = window // dil + 1
    idx = q_idx[:, None] - dil * jnp.arange(taps)[None, :]
    valid = idx >= 0
    idx = jnp.maximum(idx, 0)
    kg = jnp.take(k, idx, axis=1)
    vg = jnp.take(v, idx, axis=1)
    s = jnp.einsum('nqhd,nqjhd->nqhj', q, kg).astype(jnp.float32) * SCALE
    s = jnp.where(valid[None, :, None, :], s, NEG)
    m = jnp.max(s, axis=-1, keepdims=True)
    lse = m + jnp.log(jnp.sum(jnp.exp(s - m), axis=-1, keepdims=True))
    p = jnp.exp(s - lse)
    o = jnp.einsum('nqhj,nqjhd->nqhd', p.astype(v.dtype), vg)
    return o, lse[..., 0]


def dilated_mixture(q, ks, vs, q_idx):
    outs, lses = [], []
    for g, (window, dil) in enumerate(DIL_GROUPS):
        o, lse = dilated_group_attn(q[:, :, g], ks[g], vs[g], q_idx[g], window, dil)
        outs.append(o)
        lses.append(lse)
    w = jax.nn.softmax(jnp.stack(lses), axis=0)[..., None]
    return jnp.sum(w * jnp.stack(outs).astype(jnp.float32), axis=0).astype(q.dtype)


def a_project(x, pos, w_in):
    n, t = x.shape[:2]
    proj = x @ w_in
    dil = proj[..., :DIL_COLS].reshape(n, t, N_DIL, 3, DIL_HEADS, HEAD_DIM)
    q = rope(dil[:, :, :, 0], pos)
    k = rope(dil[:, :, :, 1], pos)
    v = dil[:, :, :, 2]
    mq = proj[..., DIL_COLS:].reshape(n, t, MEM_HEADS, HEAD_DIM)
    return q, k, v, mq


def b_project(x, pos, w_in):
    n, t = x.shape[:2]
    proj = x @ w_in
    q = rope(proj[..., :NSA_Q_COLS].reshape(n, t, NSA_HEADS, HEAD_DIM), pos)
    gates = jax.nn.sigmoid(proj[..., NSA_Q_COLS:NSA_Q_COLS + NSA_GATE_COLS].astype(jnp.float32))
    gates = gates.reshape(n, t, NSA_KV_HEADS, NSA_GROUP, 3).astype(x.dtype)
    mq = proj[..., NSA_Q_COLS + NSA_GATE_COLS:].reshape(n, t, MEM_HEADS, HEAD_DIM)
    return q, gates, mq


def mem_attend(q, mem_kv):
    s = jnp.einsum('nthd,nmhd->nhtm', q, mem_kv[:, :, 0]).astype(jnp.float32) * SCALE
    p = jax.nn.softmax(s, axis=-1)
    return jnp.einsum('nhtm,nmhd->nthd', p.astype(q.dtype), mem_kv[:, :, 1])


def merge_out(o, mo, w_out):
    n, t = o.shape[:2]
    return jnp.concatenate([o.reshape(n, t, -1), mo.reshape(n, t, -1)], axis=-1) @ w_out


def nsa_rows(x, pos, w_kv):
    n, t = x.shape[:2]
    r = (x @ w_kv).reshape(n, t, 6, NSA_KV_HEADS, HEAD_DIM)
    return jnp.stack([r[:, :, 0], r[:, :, 1], rope(r[:, :, 2], pos), r[:, :, 3],
                      rope(r[:, :, 4], pos), r[:, :, 5]], axis=2)


def compress(rows, pe, w1, w2):
    n, t, h, dh = rows.shape
    c = (t - CMP_BLOCK) // CMP_STRIDE + 1
    idx = jnp.arange(c)[:, None] * CMP_STRIDE + jnp.arange(CMP_BLOCK)[None, :]
    blocks = jnp.take(rows, idx, axis=1) + pe[None, None, :, None, :]
    flat = jnp.moveaxis(blocks, 3, 2).reshape(n, c, h, CMP_BLOCK * dh)
    return jax.nn.gelu(flat @ w1) @ w2


def nsa_context(rows, cmp_pe, cmp_w1, cmp_w2):
    n, t = rows.shape[:2]
    c = (t - CMP_BLOCK) // CMP_STRIDE + 1
    cmp_pos = jnp.arange(c) * CMP_STRIDE + (CMP_BLOCK - 1)
    ck = rope(compress(rows[:, :, 0], cmp_pe[0], cmp_w1[0], cmp_w2[0]), cmp_pos)
    cv = compress(rows[:, :, 1], cmp_pe[1], cmp_w1[1], cmp_w2[1])
    nb = -(-t // SLC_BLOCK)
    sl = jnp.pad(rows[:, :, 2:4], ((0, 0), (0, nb * SLC_BLOCK - t), (0, 0), (0, 0), (0, 0)))
    sl = sl.reshape(n, nb, SLC_BLOCK, 2, NSA_KV_HEADS, HEAD_DIM).transpose(3, 0, 4, 1, 2, 5)
    return ck, cv, cmp_pos, sl[0], sl[1]


def nsa_attend(q, gates, q_pos, ctx, win_k, win_v, win_pos):
    ck, cv, cmp_pos, skb, svb = ctx
    n, tq = q.shape[:2]
    qg = q.reshape(n, tq, NSA_KV_HEADS, NSA_GROUP, HEAD_DIM)
    s = jnp.einsum('nqhgd,nchd->nhgqc', qg, ck).astype(jnp.float32) * SCALE
    p_c = masked_softmax(s, cmp_pos[None, :] <= q_pos[:, None])
    o_c = jnp.einsum('nhgqc,nchd->nqhgd', p_c.astype(cv.dtype), cv)
    c = ck.shape[1]
    nb = skb.shape[2]
    r_sel, r_cmp = SLC_BLOCK // CMP_STRIDE, CMP_BLOCK // CMP_STRIDE
    offs = jnp.array([m - j for m in range(r_sel) for j in range(r_cmp)], dtype=jnp.int32)
    cidx = jnp.arange(nb)[:, None] * r_sel + offs[None, :]
    cval = (cidx >= 0) & (cidx < c)
    imp = jnp.sum(p_c, axis=2)
    imp = jnp.sum(jnp.where(cval, jnp.take(imp, jnp.clip(cidx, 0, c - 1), axis=-1), 0.0), axis=-1)
    blk = jnp.arange(nb)[None, :]
    cur = (q_pos // SLC_BLOCK)[:, None]
    forced = (blk == 0) | (blk == cur) | (blk == cur - 1)
    score = jnp.where(blk <= cur, jnp.where(forced, FORCE_SCORE, imp), NEG)
    top_s, sel = lax.top_k(score, min(SLC_TOPK, nb))
    n_i = jnp.arange(n)[:, None, None, None]
    h_i = jnp.arange(NSA_KV_HEADS)[None, :, None, None]
    kg = skb[n_i, h_i, sel]
    vg = svb[n_i, h_i, sel]
    kpos = sel[..., None] * SLC_BLOCK + jnp.arange(SLC_BLOCK)
    m_s = (kpos <= q_pos[None, None, :, None, None]) & (top_s > 0.5 * NEG)[..., None]
    kk = sel.shape[-1] * SLC_BLOCK
    s = jnp.einsum('nqhgd,nhqkld->nhgqkl', qg, kg).astype(jnp.float32) * SCALE
    p_s = masked_softmax(s.reshape(n, NSA_KV_HEADS, NSA_GROUP, tq, kk),
                         m_s.reshape(n, NSA_KV_HEADS, 1, tq, kk))
    o_s = jnp.einsum('nhgqx,nhqxd->nqhgd', p_s.astype(vg.dtype),
                     vg.reshape(n, NSA_KV_HEADS, tq, kk, HEAD_DIM))
    s = jnp.einsum('nqhgd,nkhd->nhgqk', qg, win_k).astype(jnp.float32) * SCALE
    dist = q_pos[:, None] - win_pos[None, :]
    p_w = masked_softmax(s, (dist >= 0) & (dist <= SWA_WINDOW) & (win_pos[None, :] >= 0))
    o_w = jnp.einsum('nhgqk,nkhd->nqhgd', p_w.astype(win_v.dtype), win_v)
    o = gates[..., 0:1] * o_c + gates[..., 1:2] * o_s + gates[..., 2:3] * o_w
    return o.reshape(n, tq, NSA_HEADS, HEAD_DIM)


def peer(x, w_q, sub_keys, u_tab, v_tab):
    n, t, d = x.shape
    m = n * t
    blk = min(PEER_BLOCK, m)
    pad = (-m) % blk
    xb = jnp.pad(x.reshape(m, d), ((0, pad), (0, 0))).reshape(-1, blk, d)

    def one(xc):
        q = (xc @ w_q).reshape(blk, PEER_HEADS, 2, PEER_KEY_DIM // 2)
        s = jnp.einsum('mhcd,hckd->mhck', q, sub_keys).astype(jnp.float32)
        s1, i1 = lax.top_k(s[:, :, 0], PEER_TOPK)
        s2, i2 = lax.top_k(s[:, :, 1], PEER_TOPK)
        cand_s = (s1[..., :, None] + s2[..., None, :]).reshape(blk, PEER_HEADS, PEER_TOPK * PEER_TOPK)
        cand_i = (i1[..., :, None] * PEER_KEYS + i2[..., None, :]).reshape(blk, PEER_HEADS, PEER_TOPK * PEER_TOPK)
        top_s, top_j = lax.top_k(cand_s, PEER_TOPK)
        e_idx = jnp.take_along_axis(cand_i, top_j, axis=-1)
        g = jax.nn.softmax(top_s, axis=-1)
        act = jax.nn.gelu(jnp.einsum('md,mhkd->mhk', xc, u_tab[e_idx]).astype(jnp.float32))
        return jnp.einsum('mhk,mhkd->md', (g * act).astype(xc.dtype), v_tab[e_idx])

    out = lax.map(one, xb).reshape(-1, d)[:m]
    return out.reshape(n, t, d)


def setup_inputs(seed: int = 0) -> dict:
    key = jax.random.key(seed)
    ks = jax.random.split(key, 26)
    f32 = jnp.float32

    def nrm(k, shape, scale=1.0):
        return jax.random.normal(k, shape, f32) * scale

    n_pages = PAST_LEN // PAGE_SIZE
    n_used = DEC_BATCH * n_pages
    n_phys = n_used + max(1, n_used // 4)
    page_table = jax.random.permutation(ks[0], n_phys)[:n_used].reshape(DEC_BATCH, n_pages).astype(jnp.int32)
    dil = [nrm(ks[4 + g], (N_A_LAYERS, DEC_BATCH, min(w, PAST_LEN), 2, DIL_HEADS, HEAD_DIM))
           for g, (w, _) in enumerate(DIL_GROUPS)]
    a_scale = jnp.concatenate([jnp.ones((N_DIL, 3, DIL_HEADS * HEAD_DIM), f32).at[:, 2].set(BETA).reshape(-1),
                               jnp.ones((MEM_COLS,), f32)])
    mem_scale = jnp.concatenate([jnp.ones((MEM_COLS,), f32), jnp.full((MEM_COLS,), BETA, f32)])
    kv_scale = jnp.repeat(jnp.array([1.0, BETA, 1.0, BETA, 1.0, BETA], f32), NSA_KV_HEADS * HEAD_DIM)
    d_in = D_MODEL ** -0.5
    a_out = DIL_HEADS * HEAD_DIM + MEM_COLS
    b_out = NSA_Q_COLS + MEM_COLS
    return {
        'x_prompt': nrm(ks[1], (BATCH, SEQ, D_MODEL)),
        'x_sample': nrm(ks[2], (DEC_BATCH, DEC_SEQ, D_MODEL)),
        'mem_prompt': nrm(ks[3], (BATCH, MEM_TOKENS, D_MODEL)),
        'cache_dil_g0': dil[0],
        'cache_dil_g1': dil[1],
        'cache_dil_g2': dil[2],
        'cache_nsa_kv': nrm(ks[7], (n_phys, PAGE_SIZE, 4, NSA_KV_HEADS, HEAD_DIM)),
        'cache_nsa_win': nrm(ks[8], (DEC_BATCH, min(SWA_WINDOW, PAST_LEN), 2, NSA_KV_HEADS, HEAD_DIM)),
        'cache_mem_kv': nrm(ks[9], (DEPTH, DEC_BATCH, MEM_TOKENS, 2, MEM_HEADS, HEAD_DIM)),
        'page_table': page_table,
        'w_in_a': nrm(ks[10], (N_A_LAYERS, D_MODEL, DIL_COLS + MEM_COLS), d_in) * a_scale,
        'w_out_a': nrm(ks[11], (N_A_LAYERS, a_out, D_MODEL), BETA * a_out ** -0.5),
        'w_in_b': nrm(ks[12], (N_B_LAYERS, D_MODEL, NSA_Q_COLS + NSA_GATE_COLS + MEM_COLS), d_in),
        'w_out_b': nrm(ks[13], (N_B_LAYERS, b_out, D_MODEL), BETA * b_out ** -0.5),
        'w_mem_kv': nrm(ks[14], (DEPTH, D_MODEL, 2 * MEM_COLS), d_in) * mem_scale,
        'w_kv_b': nrm(ks[15], (D_MODEL, 6 * NSA_KV_HEADS * HEAD_DIM), d_in) * kv_scale,
        'cmp_pe': nrm(ks[16], (2, CMP_BLOCK, HEAD_DIM), 0.02),
        'cmp_w1': nrm(ks[17], (2, CMP_BLOCK * HEAD_DIM, CMP_HIDDEN), (CMP_BLOCK * HEAD_DIM) ** -0.5),
        'cmp_w2': nrm(ks[18], (2, CMP_HIDDEN, HEAD_DIM), CMP_HIDDEN ** -0.5),
        'ln_g': 1.0 + nrm(ks[19], (DEPTH, 2, D_MODEL), 0.02),
        'ln_b': nrm(ks[20], (DEPTH, 2, D_MODEL), 0.02),
        'peer_wq': nrm(ks[21], (DEPTH, D_MODEL, PEER_HEADS * PEER_KEY_DIM), d_in),
        'peer_keys': nrm(ks[22], (DEPTH, PEER_HEADS, 2, PEER_KEYS, PEER_KEY_DIM // 2), (PEER_KEY_DIM // 2) ** -0.5),
        'peer_u': nrm(ks[23], (DEPTH, PEER_EXPERTS, D_MODEL), d_in),
        'peer_v': nrm(ks[24], (DEPTH, PEER_EXPERTS, D_MODEL), BETA * PEER_HEADS ** -0.5),
    }


def reference(x_prompt, x_sample, mem_prompt, cache_dil_g0, cache_dil_g1, cache_dil_g2,
              cache_nsa_kv, cache_nsa_win, cache_mem_kv, page_table,
              w_in_a, w_out_a, w_in_b, w_out_b, w_mem_kv, w_kv_b, cmp_pe, cmp_w1, cmp_w2,
              ln_g, ln_b, peer_wq, peer_keys, peer_u, peer_v):
    pos_p = jnp.arange(SEQ, dtype=jnp.int32)
    pos_s = PAST_LEN + jnp.arange(DEC_SEQ, dtype=jnp.int32)
    n_pages = PAST_LEN // PAGE_SIZE
    dil_caches = (cache_dil_g0, cache_dil_g1, cache_dil_g2)
    xp, xs = x_prompt, x_sample
    dil_new_p = [[] for _ in DIL_GROUPS]
    dil_new_s = [[] for _ in DIL_GROUPS]
    mem_new_p = []
    for layer in range(DEPTH):
        mem_kv_p = (mem_prompt @ w_mem_kv[layer]).reshape(BATCH, MEM_TOKENS, 2, MEM_HEADS, HEAD_DIM)
        mem_kv_s = cache_mem_kv[layer]
        mem_new_p.append(mem_kv_p)
        if layer < N_A_LAYERS:
            q, k, v, mq = a_project(xp, pos_p, w_in_a[layer])
            ks = [k[:, :, g] for g in range(N_DIL)]
            vs = [v[:, :, g] for g in range(N_DIL)]
            o = map_query_blocks(
                lambda s0, qb: dilated_mixture(qb, ks, vs, [s0 + jnp.arange(Q_BLOCK)] * N_DIL), q)
            mix_p = merge_out(o, mem_attend(mq, mem_kv_p), w_out_a[layer])
            for g, (window, _) in enumerate(DIL_GROUPS):
                dil_new_p[g].append(jnp.stack([k[:, :, g], v[:, :, g]], axis=2)[:, SEQ - min(window, SEQ):])
            q, k, v, mq = a_project(xs, pos_s, w_in_a[layer])
            ks, vs, qi = [], [], []
            for g in range(N_DIL):
                buf = dil_caches[g][layer]
                full = jnp.concatenate([buf, jnp.stack([k[:, :, g], v[:, :, g]], axis=2)], axis=1)
                ks.append(full[:, :, 0])
                vs.append(full[:, :, 1])
                qi.append(buf.shape[1] + jnp.arange(DEC_SEQ))
                dil_new_s[g].append(full[:, DEC_SEQ:])
            o = dilated_mixture(q, ks, vs, qi)
            mix_s = merge_out(o, mem_attend(mq, mem_kv_s), w_out_a[layer])
        else:
            b = layer - N_A_LAYERS
            if b == 0:
                rows_p = nsa_rows(xp, pos_p, w_kv_b)
                rows_s = nsa_rows(xs, pos_s, w_kv_b)
                nsa_new_p = rows_p[:, :, :4]
                nsa_new_s = rows_s[:, :, :4]
                past = cache_nsa_kv[page_table].reshape(DEC_BATCH, n_pages * PAGE_SIZE, 4, NSA_KV_HEADS, HEAD_DIM)
                ctx_p = nsa_context(nsa_new_p, cmp_pe, cmp_w1, cmp_w2)
                ctx_s = nsa_context(jnp.concatenate([past, nsa_new_s], axis=1), cmp_pe, cmp_w1, cmp_w2)
                win_p = rows_p[:, :, 4:]
                win_s = jnp.concatenate([cache_nsa_win, rows_s[:, :, 4:]], axis=1)
                win_new_p = win_p[:, SEQ - min(SWA_WINDOW, SEQ):]
                win_new_s = win_s[:, DEC_SEQ:]
                win_pad_p = jnp.pad(win_p, ((0, 0), (SWA_WINDOW, 0), (0, 0), (0, 0), (0, 0)))
                win_pos_s = PAST_LEN - cache_nsa_win.shape[1] + jnp.arange(win_s.shape[1])
            q, gates, mq = b_project(xp, pos_p, w_in_b[b])

            def nsa_block(s0, qb, gb):
                wb = lax.dynamic_slice_in_dim(win_pad_p, s0, Q_BLOCK + SWA_WINDOW, axis=1)
                w_pos = s0 - SWA_WINDOW + jnp.arange(Q_BLOCK + SWA_WINDOW)
                return nsa_attend(qb, gb, s0 + jnp.arange(Q_BLOCK), ctx_p, wb[:, :, 0], wb[:, :, 1], w_pos)

            o = map_query_blocks(nsa_block, q, gates)
            mix_p = merge_out(o, mem_attend(mq, mem_kv_p), w_out_b[b])
            q, gates, mq = b_project(xs, pos_s, w_in_b[b])
            o = nsa_attend(q, gates, pos_s, ctx_s, win_s[:, :, 0], win_s[:, :, 1], win_pos_s)
            mix_s = merge_out(o, mem_attend(mq, mem_kv_s), w_out_b[b])
        xp = layer_norm(ALPHA * xp + mix_p, ln_g[layer, 0], ln_b[layer, 0])
        xs = layer_norm(ALPHA * xs + mix_s, ln_g[layer, 0], ln_b[layer, 0])
        xp = layer_norm(ALPHA * xp + peer(xp, peer_wq[layer], peer_keys[layer], peer_u[layer], peer_v[layer]),
                        ln_g[layer, 1], ln_b[layer, 1])
        xs = layer_norm(ALPHA * xs + peer(xs, peer_wq[layer], peer_keys[layer], peer_u[layer], peer_v[layer]),
                        ln_g[layer, 1], ln_b[layer, 1])
    return (xp, xs,
            jnp.stack(dil_new_p[0]), jnp.stack(dil_new_p[1]), jnp.stack(dil_new_p[2]),
            jnp.stack(dil_new_s[0]), jnp.stack(dil_new_s[1]), jnp.stack(dil_new_s[2]),
            nsa_new_p, nsa_new_s, win_new_p, win_new_s, jnp.stack(mem_new_p))
```

```python
import numpy as np
import ml_dtypes
import concourse.bass as bass
import concourse.mybir as mybir
from concourse.bass_utils import run_bass_kernel_spmd

F32 = mybir.dt.float32
BF16 = mybir.dt.bfloat16
I32 = mybir.dt.int32
ALU = mybir.AluOpType
AF = mybir.ActivationFunctionType
AX = mybir.AxisListType

D = 1024
HD = 64
PAST = 8192
NS = 4
DEPTH = 2
ALPHA = (2 * DEPTH) ** 0.25
LN_EPS = 1e-5
NEGM = -30000.0
DIL = ((128, 1), (512, 4), (2048, 16))
GELU_C = 0.7978845608028654


class Buf:
    __slots__ = ("w", "r", "name")

    def __init__(self, name=""):
        self.w = None
        self.r = {}
        self.name = name


class Eng:
    def __init__(self, nc, name, eng):
        self.name = name
        self.eng = eng
        self.sem = nc.alloc_semaphore("sem_" + name)
        self.cnt = 0
        self.waited = {}


class Ctx:
    NDQ = 8

    def __init__(self, nc):
        self.nc = nc
        self.E = {
            "pe": Eng(nc, "pe", nc.tensor),
            "act": Eng(nc, "act", nc.scalar),
            "dve": Eng(nc, "dve", nc.vector),
            "pool": Eng(nc, "pool", nc.gpsimd),
            "sp": Eng(nc, "sp", nc.sync),
        }
        self.dq = {}
        for q in ("sp", "pool", "act"):
            self.dq[q] = {"sems": [[nc.alloc_semaphore("dq_%s_%d" % (q, i)), 0] for i in range(self.NDQ)], "next": 0}
        self.semid = {}
        self.ninstr = 0

    def _sid(self, sem):
        return id(sem)

    def _wait(self, E, tok):
        sem, val, _ = tok
        k = id(sem)
        if E.waited.get(k, 0) >= val:
            return
        E.eng.wait_ge(sem, val)
        E.waited[k] = val

    def _deps(self, E, reads, writes, is_dma=False):
        for b in reads:
            if b.w is not None:
                if not (b.w[2] == E.name and E.name == "pe"):
                    self._wait(E, b.w)
        for b in writes:
            if b.w is not None:
                if not (b.w[2] == E.name and E.name == "pe"):
                    self._wait(E, b.w)
            for tok in b.r.values():
                if tok[2] == E.name and not is_dma:
                    continue
                self._wait(E, tok)

    def _mark(self, tok, reads, writes):
        for b in reads:
            b.r[id(tok[0])] = tok
        for b in writes:
            b.w = tok
            b.r = {}

    limit = None

    def op(self, eng, fn, r=(), w=()):
        if self.limit is not None and self.ninstr >= self.limit:
            return
        E = self.E[eng]
        self._deps(E, r, w)
        ins = fn(E.eng)
        E.cnt += 1
        ins.then_inc(E.sem, 1)
        self._mark((E.sem, E.cnt, eng), r, w)
        self.ninstr += 1

    def dma(self, q, out, in_, r=(), w=(), **kw):
        if self.limit is not None and self.ninstr >= self.limit:
            return
        E = self.E[q]
        Q = self.dq[q]
        slot = Q["next"]
        Q["next"] = (slot + 1) % self.NDQ
        ent = Q["sems"][slot]
        if ent[1] > 0:
            self._wait(E, (ent[0], 16 * ent[1], "dma"))
        self._deps(E, r, w, is_dma=True)
        ins = E.eng.dma_start(out=out, in_=in_, **kw)
        ent[1] += 1
        ins.then_inc(ent[0], 16)
        self._mark((ent[0], 16 * ent[1], "dma"), r, w)
        self.ninstr += 1

    def finish(self):
        E = self.E["sp"]
        for q in self.dq.values():
            for sem, cnt in q["sems"]:
                if cnt > 0:
                    self._wait(E, (sem, 16 * cnt, "dma"))
        for e in self.E.values():
            if e.cnt > 0 and e.name != "sp":
                self._wait(E, (e.sem, e.cnt, e.name))


class TB:
    def __init__(self, t, name):
        self.t = t
        self.b = Buf(name)

    def __getitem__(self, k):
        return self.t[k]


class Prog:
    def __init__(self, cfg):
        self.cfg = cfg
        self.nc = bass.Bass("TRN2", target_bir_lowering=False)
        self.cx = Ctx(self.nc)
        self.ins = {}
        self.outs = {}
        self._n = 0

    def din(self, name, shape, dt=F32):
        t = self.nc.dram_tensor(name, list(shape), dt, kind="ExternalInput")
        self.ins[name] = t
        return t.ap()

    def dout(self, name, shape, dt=F32):
        t = self.nc.dram_tensor(name, list(shape), dt, kind="ExternalOutput")
        self.outs[name] = t
        return t.ap()

    def dscr(self, name, shape, dt=F32):
        return self.nc.dram_tensor(name, list(shape), dt).ap()

    def sb(self, shape, dt=F32, name=None):
        self._n += 1
        name = name or ("sb%d" % self._n)
        return TB(self.nc.alloc_sbuf_tensor(name, list(shape), dt), name)

    def ps(self, shape, dt=F32, name=None):
        self._n += 1
        name = name or ("ps%d" % self._n)
        return TB(self.nc.alloc_psum_tensor(name, list(shape), dt), name)

    def phase_begin(self):
        import contextlib
        if not hasattr(self, "stacks"):
            self.stacks = []
        self.stacks.append(contextlib.ExitStack())
        self.stack = self.stacks[-1]

    def phase_end(self):
        self.barrier()
        self.stacks.pop().close()
        self.stack = self.stacks[-1] if self.stacks else None

    def sbp(self, shape, dt=F32, name=None):
        self._n += 1
        name = name or ("sb%d" % self._n)
        t = self.stack.enter_context(self.nc.sbuf_tensor(name, list(shape), dt))
        return TB(t, name)

    def psp(self, shape, dt=F32, name=None):
        self._n += 1
        name = name or ("ps%d" % self._n)
        t = self.stack.enter_context(self.nc.psum_tensor(name, list(shape), dt))
        return TB(t, name)

    def barrier(self):
        cx = self.cx
        for E in cx.E.values():
            for q in cx.dq.values():
                for sem, cnt in q["sems"]:
                    if cnt > 0:
                        cx._wait(E, (sem, 16 * cnt, "dma"))
            for e in cx.E.values():
                if e.cnt > 0 and e is not E:
                    cx._wait(E, (e.sem, e.cnt, e.name))

    def mm(self, outb, out, lhsT, rhs, r, start=True, stop=True):
        self.cx.op("pe", lambda e: e.matmul(out, lhsT=lhsT, rhs=rhs, start=start, stop=stop), r=r, w=[outb])

    def tr(self, outb, out, in_, ident, r):
        self.cx.op("pe", lambda e: e.transpose(out, in_, ident), r=r, w=[outb])

    def copy(self, eng, outb, out, in_, r):
        if eng == "act":
            self.cx.op("act", lambda e: e.activation(out=out, in_=in_, func=AF.Copy), r=r, w=[outb])
        else:
            self.cx.op(eng, lambda e: e.tensor_copy(out=out, in_=in_), r=r, w=[outb])

    def tt(self, eng, outb, out, a, b, op, r):
        self.cx.op(eng, lambda e: e.tensor_tensor(out=out, in0=a, in1=b, op=op), r=r, w=[outb])

    def ts(self, eng, outb, out, a, s1, s2, op0, op1, r):
        if op1 is None:
            self.cx.op(eng, lambda e: e.tensor_scalar(out=out, in0=a, scalar1=s1, scalar2=None, op0=op0), r=r, w=[outb])
        else:
            self.cx.op(eng, lambda e: e.tensor_scalar(out=out, in0=a, scalar1=s1, scalar2=s2, op0=op0, op1=op1), r=r, w=[outb])

    def stt(self, outb, out, a, sc, b, op0, op1, r):
        self.cx.op("dve", lambda e: e.scalar_tensor_tensor(out=out, in0=a, scalar=sc, in1=b, op0=op0, op1=op1), r=r, w=[outb])

    def act(self, outb, out, in_, func, r, **kw):
        self.cx.op("act", lambda e: e.activation(out=out, in_=in_, func=func, **kw), r=r, w=[outb])

    def memset(self, eng, outb, out, val):
        shp = list(out.shape)
        if len(shp) > 2:
            names = " ".join("a%d" % i for i in range(len(shp) - 1))
            try:
                out = out.rearrange("p %s -> p (%s)" % (names, names))
            except Exception:
                self.cx.op("dve", lambda e: e.memset(out, val), r=(), w=[outb])
                return
        n = out.shape[1]
        for c0 in range(0, n, 2048):
            c1 = min(n, c0 + 2048)
            self.cx.op("dve", lambda e, c0=c0, c1=c1: e.memset(out[:, c0:c1], val), r=(), w=[outb])

    _rr = 0

    def evac(self, outb, out, in_, r):
        self._rr ^= 1
        self.copy("act" if (self._rr and not getattr(self, "evac_dve", False)) else "dve", outb, out, in_, r)

    def evac8(self, dst, src_ps, n=8, p0=0, p1=128):
        for b0 in range(0, n, 4):
            b1 = min(n, b0 + 4)
            self.evac(dst.b, dst[p0:p1, b0:b1, :], src_ps[p0:p1, b0 * 128:b1 * 128].rearrange("p (k t) -> p k t", k=b1 - b0), r=[src_ps.b])

    def load_w_bf16(self, dst, src, rows, cols, stg):
        kc_n = max(1, rows // 128)
        pr = min(rows, 128)
        sw = stg[0].t.shape[1]
        i = 0
        for kc in range(kc_n):
            for c0 in range(0, cols, sw):
                c1 = min(cols, c0 + sw)
                s = stg[i % len(stg)]
                i += 1
                self.cx.dma("sp", s[0:pr, 0:c1 - c0], src[kc * 128:kc * 128 + pr, c0:c1], r=(), w=[s.b])
                self.evac(dst.b, dst[0:pr, kc, c0:c1], s[0:pr, 0:c1 - c0], r=[s.b])

    def ln_tile(self, z, out, g, bt, tmp):
        st, mv, rs = tmp
        for hf in range(2):
            self.cx.op("dve", lambda e, hf=hf: e.bn_stats(out=st[:, hf, :], in_=z[:, hf * 512:(hf + 1) * 512]), r=[z.b], w=[st.b])
        self.cx.op("dve", lambda e: e.bn_aggr(out=mv[:, :], in_=st[:, :, :]), r=[st.b], w=[mv.b])
        self.ts("dve", rs.b, rs[:, :], mv[:, 1:2], LN_EPS, None, ALU.add, None, r=[mv.b])
        self.act(rs.b, rs[:, :], rs[:, :], AF.Ln, r=[rs.b])
        self.act(rs.b, rs[:, :], rs[:, :], AF.Exp, r=[rs.b], scale=-0.5)
        self.ts("dve", out.b, out[:, :], z[:, :], mv[:, 0:1], rs[:, 0:1], ALU.subtract, ALU.mult, r=[z.b, mv.b, rs.b])
        self.tt("pool", out.b, out[:, :], out[:, :], g[:, :], ALU.mult, r=[out.b, g.b])
        self.tt("pool", out.b, out[:, :], out[:, :], bt[:, :], ALU.add, r=[out.b, bt.b])


MASK_NAMES = ["g0d", "g0f", "g1d", "g1m", "g1f", "g2d", "g2m", "g2f", "eye"]


def host_masks():
    ki = np.arange(128)[:, None]
    qi = np.arange(128)[None, :]
    out = []
    for (w, dil) in DIL:
        dd = ((qi - ki) % dil) == 0
        out.append((qi >= ki) & dd)
        if dil > 1:
            out.append(dd)
        out.append((qi <= ki) & dd)
    out.append(qi == ki)
    m = np.stack(out)
    return np.where(m, 0.0, NEGM).astype(ml_dtypes.bfloat16)


def rope_tables(T):
    half = HD // 2
    inv = (10000.0 ** (-np.arange(half, dtype=np.float32) / half)).astype(np.float32)
    pos = np.concatenate([np.arange(T), np.full(128, PAST)]).astype(np.float32)
    ang = pos[:, None] * inv[None, :]
    return np.cos(ang).astype(np.float32), np.sin(ang).astype(np.float32)


class Model(Prog):
    def __init__(self, T, debug=False, small=()):
        super().__init__(None)
        self.small = set(small)
        self.T = T
        self.NT = T // 128
        self.TT = T + 128
        self.debug = debug
        self.declare()

    def declare(self):
        T, TT = self.T, self.TT
        d = self.din
        self.xp = d("xp", [T, D])
        self.xs = d("xs", [128, D])
        self.memp = d("memp", [256, D])
        self.cdil = [d("cdil%d" % g, [NS, DIL[g][0], 2, 4, HD]) for g in range(3)]
        self.pool = d("pool", [8 if "pool" in self.small else 2560, 128, 4, 2, HD])
        self.cwin = d("cwin", [NS, 512, 2, 2, HD])
        self.cmem = d("cmem", [2, NS, 256, 2, 4, HD])
        self.ptab = d("ptab", [NS, 64], I32)
        self.w_in_a = d("w_in_a", [D, 2560])
        self.w_out_a = d("w_out_a", [512, D])
        self.w_in_b = d("w_in_b", [D, 1060])
        self.w_out_b = d("w_out_b", [1024, D])
        self.w_mem_kv = d("w_mem_kv", [2, D, 512])
        self.w_kv_b = d("w_kv_b", [D, 768])
        self.cmp_pe = d("cmp_pe", [2, 128, 16])
        self.cmp_w1 = d("cmp_w1", [2, 2048, 128])
        self.cmp_w2 = d("cmp_w2", [2, 128, 64])
        self.cmp_w2r = d("cmp_w2r", [2, 128, 64])
        self.ln_g = d("ln_g", [2, 2, D])
        self.ln_b = d("ln_b", [2, 2, D])
        self.peer_wq = d("peer_wq", [2, D, 1024])
        self.peer_keysT = d("peer_keysT", [2, 128, 8, 128])
        ne = 2 if "peer" in self.small else 128
        self.peer_uT = d("peer_uT", [2, ne, 128, 8, 128])
        self.peer_v = d("peer_v", [2, ne * 128, D])
        self.c_cos = d("c_cos", [TT, 32])
        self.c_sin = d("c_sin", [TT, 32])
        self.c_masks = d("c_masks", [9, 128, 128], BF16)
        self.c_ident = d("c_ident", [128, 128])
        self.c_cosC = d("c_cosC", [128, 512])
        self.c_sinC = d("c_sinC", [128, 512])
        self.c_mrel = d("c_mrel", [128, 1024], BF16)
        self.c_mtc = d("c_mtc", [128, 2304], BF16)
        self.c_kp = d("c_kp", [128, 264])
        self.c_ad = d("c_ad", [128, 264])
        self.c_selg = d("c_selg", [36, 36, 64])
        self.c_riota = d("c_riota", [128, 1])
        o = self.dout
        self.y_p = o("y_p", [T, D])
        self.y_s = o("y_s", [128, D])
        self.dil_p = [o("dil_p%d" % g, [min(DIL[g][0], T), 2, 4, HD]) for g in range(3)]
        self.dil_s = [o("dil_s%d" % g, [NS, DIL[g][0], 2, 4, HD]) for g in range(3)]
        self.nsa_kv_p = o("nsa_kv_p", [T, 4, 2, HD])
        self.nsa_kv_s = o("nsa_kv_s", [128, 4, 2, HD])
        self.win_p = o("win_p", [min(512, T), 2, 2, HD])
        self.win_s = o("win_s", [NS, 512, 2, 2, HD])
        self.mem_p = o("mem_p", [2, 256, 2, 4, HD])
        self.x1 = self.dscr("x1", [TT, D])
        self.x1T = self.dscr("x1T", [8, 128, TT], BF16)
        self.x1b = [Buf("x1_%d" % t) for t in range(self.NT + 1)]
        if self.debug:
            self.dbg_x1 = o("dbg_x1", [TT, D])

    def consts(self):
        self.ident = self.sb([128, 128], F32, "ident")
        self.cx.dma("sp", self.ident[:, :], self.c_ident[:, :], w=[self.ident.b])
        self.identb = self.sb([128, 128], BF16, "identb")
        self.copy("dve", self.identb.b, self.identb[:, :], self.ident[:, :], r=[self.ident.b])
        self.masks = self.sb([128, 9, 128], BF16, "masks")
        self.cx.dma("sp", self.masks[:, :, :], self.c_masks.rearrange("m k q -> k m q"), w=[self.masks.b])
        self.sel64 = self.sb([128, 64], F32, "sel64")
        self.memset("dve", self.sel64.b, self.sel64[:, :], 0.0)
        self.memset("dve", self.sel64.b, self.sel64[64:65, :], 1.0)

    def bcast_row(self, dst, src_row):
        C = src_row.shape[-1]
        self.cx.dma("sp", dst[:, :], src_row.broadcast_to([128, C]), w=[dst.b])

    def mem_kv(self, layer, wst, memKT, memV, xin, pT, pP, xT):
        wm = self.sbp([128, 8, 512], BF16)
        self.load_w_bf16(wm, self.w_mem_kv[layer], D, 512, wst)
        for mt in range(2):
            self.cx.dma("sp", xin[:, :], self.memp[mt * 128:(mt + 1) * 128, :], w=[xin.b])
            for kc in range(8):
                self.tr(pT.b, pT[:, kc * 128:(kc + 1) * 128], xin[:, kc * 128:(kc + 1) * 128], self.ident[:, :], r=[xin.b, self.ident.b])
            self.evac8(xT, pT)
            for kc in range(8):
                self.mm(pP.b, pP[:, 0:512], xT[:, kc, :], wm[:, kc, :], r=[xT.b, wm.b], start=(kc == 0), stop=(kc == 7))
            mk = self.sbp([128, 512], F32)
            self.evac(mk.b, mk[:, :], pP[:, 0:512], r=[pP.b])
            self.cx.dma("pool", self.mem_p[layer, mt * 128:(mt + 1) * 128].rearrange("m a h d -> m (a h d)"), mk[:, :], r=[mk.b])
            self.mem_tile(mk, mt, memKT, memV, pT)

    def mem_tile(self, mk, mt, memKT, memV, pT, col=None):
        for pr in range(2):
            self.tr(pT.b, pT[:, pr * 128:(pr + 1) * 128], mk[:, pr * 128:(pr + 1) * 128], self.ident[:, :], r=[mk.b, self.ident.b])
        self.evac(memKT.b, memKT[:, :, mt * 128:(mt + 1) * 128], pT[:, 0:256].rearrange("p (k t) -> p k t", k=2), r=[pT.b])
        self.evac(memV.b, memV[:, mt, :, 0:64], mk[:, 256:512].rearrange("p (h d) -> p h d", h=4), r=[mk.b])

    def layer0_attn(self):
        P, cx, NT = self, self.cx, self.NT
        P.phase_begin()
        wst = [P.sbp([128, 1280], F32) for _ in range(2)]
        win = P.sbp([128, 8, 2560], BF16)
        P.load_w_bf16(win, self.w_in_a, D, 2560, wst)
        wout = P.sbp([64, 8, 1024], BF16)
        for hh in range(8):
            s = wst[hh % 2]
            cx.dma("sp", s[0:64, 0:1024], self.w_out_a[hh * 64:(hh + 1) * 64, :], w=[s.b])
            P.evac(wout.b, wout[:, hh, :], s[0:64, 0:1024], r=[s.b])
        lng, lnb = P.sbp([128, D]), P.sbp([128, D])
        P.bcast_row(lng, self.ln_g[0, 0:1, :])
        P.bcast_row(lnb, self.ln_b[0, 0:1, :])
        pU, pS, pP, pT = P.psp([128, 1024]), P.psp([128, 1024]), P.psp([128, 1024]), P.psp([128, 1024])
        Sb = [Buf("S%d" % i) for i in range(8)]
        NTg = [w // 128 + 1 for (w, _) in DIL]
        kt = [P.sbp([128, 2, NTg[g] * 128], BF16) for g in range(3)]
        vr = [P.sbp([128, NTg[g], 4, 65], BF16) for g in range(3)]
        for g in range(3):
            P.memset("pool", vr[g].b, vr[g][:, :, :, :], 1.0)
        memKT, memV = P.sbp([128, 2, 256], BF16), P.sbp([128, 2, 4, 65], BF16)
        memKTs, memVs = P.sbp([128, 2, 256], BF16), P.sbp([128, 2, 4, 65], BF16)
        P.memset("pool", memV.b, memV[:, :, :, :], 1.0)
        P.memset("pool", memVs.b, memVs[:, :, :, :], 1.0)
        KsT, Vs = P.sbp([128, 2, 128], BF16), P.sbp([128, 4, 65], BF16)
        ktS, vS = P.sbp([128, 3, 2, 128], BF16), P.sbp([128, 3, 4, 65], BF16)
        P.memset("pool", Vs.b, Vs[:, :, :], 1.0)
        P.memset("pool", vS.b, vS[:, :, :, :], 1.0)
        xin = [P.sbp([128, D]) for _ in range(2)]
        xT = P.sbp([128, 8, 128], BF16)
        pr = [P.sbp([128, 2560])]
        cs, sn = P.sbp([128, 32]), P.sbp([128, 32])
        qkr = [P.sbp([128, 3, 8, 64])]
        tm = [P.sbp([128, 3, 8, 32]) for _ in range(4)]
        qT = P.sbp([128, 2, 3, 2, 128], BF16)
        mqT = P.sbp([128, 2, 2, 128], BF16)
        P.memset("pool", qT.b, qT[:, :, :, :, :], 0.0)
        P.memset("pool", mqT.b, mqT[:, :, :, :], 0.0)
        PT = [P.sbp([128, 128], BF16) for _ in range(6)]
        rden = P.sbp([128, 1024])
        P.memset("pool", rden.b, rden[:, :], 0.0)
        Usb = P.sbp([64, 1024])
        oT = P.sbp([64, 8, 128], BF16)
        z = [P.sbp([128, D]) for _ in range(2)]
        x1Tt = [P.sbp([128, 8, 128], BF16) for _ in range(2)]
        lnt = (P.sbp([128, 2, 6]), P.sbp([128, 2]), P.sbp([128, 1]))
        kin, vin = P.sbp([128, 256]), P.sbp([128, 256])
        mks = P.sbp([128, 512])
        ident = self.ident

        P.mem_kv(0, wst, memKT, memV, xin[1], pT, pP, xT)

        for g in range(3):
            W = DIL[g][0]
            for s in range(NS):
                cx.dma("pool", self.dil_s[g][s, 0:W - 1].rearrange("r a h d -> r (a h d)"),
                       self.cdil[g][s, 1:W].rearrange("r a h d -> r (a h d)"))

        def load_x(t):
            src = self.xs[:, :] if t == NT else self.xp[t * 128:(t + 1) * 128, :]
            cx.dma("sp", xin[t % 2][:, :], src, w=[xin[t % 2].b])

        sctr = [0]
        pctr = [0]

        def attn_unit(kT_ap, q_ap, mask_idx, v_ap, u_ap, ncol, rd, flags):
            si = sctr[0] % 8
            sctr[0] += 1
            sb_, s_ap = Sb[si], pS[:, si * 128:si * 128 + ncol]
            P.mm(sb_, s_ap, kT_ap, q_ap, r=rd, start=True, stop=(mask_idx is None))
            if mask_idx is not None:
                P.mm(sb_, s_ap, self.identb[:, :], self.masks[:, mask_idx, 0:ncol], r=[self.identb.b, self.masks.b], start=False, stop=True)
            pt = PT[pctr[0] % len(PT)]
            pctr[0] += 1
            P.act(pt.b, pt[:, 0:ncol], s_ap, AF.Exp, r=[sb_], scale=0.125)
            P.mm(pU.b, u_ap, v_ap, pt[:, 0:ncol], r=[pt.b] + rd, start=flags[0], stop=flags[1])

        load_x(0)
        for t in range(NT + 1):
            samp = (t == NT)
            xi, prt, qk, zt, x1T_ = xin[t % 2], pr[0], qkr[0], z[t % 2], x1Tt[t % 2]
            if t + 1 <= NT:
                load_x(t + 1)
            cx.dma("sp", cs[:, :], self.c_cos[t * 128:(t + 1) * 128, :], w=[cs.b])
            cx.dma("sp", sn[:, :], self.c_sin[t * 128:(t + 1) * 128, :], w=[sn.b])
            for kc in range(8):
                P.tr(pT.b, pT[:, kc * 128:(kc + 1) * 128], xi[:, kc * 128:(kc + 1) * 128], ident[:, :], r=[xi.b, ident.b])
            P.evac8(xT, pT)
            for cb in range(5):
                hb = pP[:, (cb % 2) * 512:(cb % 2) * 512 + 512]
                for kc in range(8):
                    P.mm(pP.b, hb, xT[:, kc, :], win[:, kc, cb * 512:(cb + 1) * 512], r=[xT.b, win.b], start=(kc == 0), stop=(kc == 7))
                P.evac(prt.b, prt[:, cb * 512:(cb + 1) * 512], hb, r=[pP.b])
            v6 = prt[:, 0:2304].rearrange("p (g r h x i) -> p g r h x i", g=3, r=3, h=4, x=2)
            x1v = v6[:, :, 0:2, :, 0, :].rearrange("p g r h i -> p g (r h) i")
            x2v = v6[:, :, 0:2, :, 1, :].rearrange("p g r h i -> p g (r h) i")
            cb_ = cs[:, :].unsqueeze(1).unsqueeze(1).broadcast_to([128, 3, 8, 32])
            sb_ = sn[:, :].unsqueeze(1).unsqueeze(1).broadcast_to([128, 3, 8, 32])
            P.tt("dve", tm[0].b, tm[0][:, :, :, :], x1v, cb_, ALU.mult, r=[prt.b, cs.b])
            P.tt("pool", tm[1].b, tm[1][:, :, :, :], x2v, sb_, ALU.mult, r=[prt.b, sn.b])
            P.tt("dve", tm[2].b, tm[2][:, :, :, :], x1v, sb_, ALU.mult, r=[prt.b, sn.b])
            P.tt("pool", tm[3].b, tm[3][:, :, :, :], x2v, cb_, ALU.mult, r=[prt.b, cs.b])
            P.tt("dve", qk.b, qk[:, :, :, 0:32], tm[0][:, :, :, :], tm[1][:, :, :, :], ALU.subtract, r=[tm[0].b, tm[1].b])
            P.tt("dve", qk.b, qk[:, :, :, 32:64], tm[3][:, :, :, :], tm[2][:, :, :, :], ALU.add, r=[tm[2].b, tm[3].b])
            if not samp:
                for g in range(3):
                    nW = min(DIL[g][0], self.T) // 128
                    if t >= NT - nW:
                        r0 = 128 * (t - (NT - nW))
                        cx.dma("pool", self.dil_p[g][r0:r0 + 128, 0].rearrange("r h d -> r (h d)"), qk[:, g, 4:8, :].rearrange("p h d -> p (h d)"), r=[qk.b])
                        cx.dma("pool", self.dil_p[g][r0:r0 + 128, 1].rearrange("r h d -> r (h d)"), prt[:, g * 768 + 512:g * 768 + 768], r=[prt.b])
            else:
                for g in range(3):
                    W = DIL[g][0]
                    cx.dma("pool", self.dil_s[g][0:NS, W - 1, 0].rearrange("s h d -> s (h d)"), qk[0:NS, g, 4:8, :].rearrange("p h d -> p (h d)"), r=[qk.b])
                    cx.dma("pool", self.dil_s[g][0:NS, W - 1, 1].rearrange("s h d -> s (h d)"), prt[0:NS, g * 768 + 512:g * 768 + 768], r=[prt.b])
            for g in range(3):
                for pi in range(4):
                    P.tr(pT.b, pT[:, pi * 128:(pi + 1) * 128], qk[:, g, 2 * pi:2 * pi + 2, :].rearrange("p h d -> p (h d)"), ident[:, :], r=[qk.b, ident.b])
                P.evac(qT.b, qT[0:64, 0, g, :, :], pT[0:64, 0:256].rearrange("p (k t) -> p k t", k=2), r=[pT.b])
                P.evac(qT.b, qT[64:128, 1, g, :, :], pT[64:128, 0:256].rearrange("p (k t) -> p k t", k=2), r=[pT.b])
                if samp:
                    P.evac(ktS.b, ktS[:, g, :, :], pT[:, 256:512].rearrange("p (k t) -> p k t", k=2), r=[pT.b])
                    P.evac(vS.b, vS[:, g, :, 0:64], prt[:, g * 768 + 512:g * 768 + 768].rearrange("p (h d) -> p h d", h=4), r=[prt.b])
                else:
                    slot = t % NTg[g]
                    P.evac(kt[g].b, kt[g][:, :, slot * 128:(slot + 1) * 128], pT[:, 256:512].rearrange("p (k t) -> p k t", k=2), r=[pT.b])
                    P.evac(vr[g].b, vr[g][:, slot, :, 0:64], prt[:, g * 768 + 512:g * 768 + 768].rearrange("p (h d) -> p h d", h=4), r=[prt.b])
            for pi in range(2):
                P.tr(pT.b, pT[:, 512 + pi * 128:512 + (pi + 1) * 128], prt[:, 2304 + pi * 128:2304 + (pi + 1) * 128], ident[:, :], r=[prt.b, ident.b])
            P.evac(mqT.b, mqT[0:64, 0, :, :], pT[0:64, 512:768].rearrange("p (k t) -> p k t", k=2), r=[pT.b])
            P.evac(mqT.b, mqT[64:128, 1, :, :], pT[64:128, 512:768].rearrange("p (k t) -> p k t", k=2), r=[pT.b])

            unitsA, unitsB = [], []
            MI = [[0, None, 1], [2, 3, 4], [5, 6, 7]]
            if not samp:
                for g in range(3):
                    nb = DIL[g][0] // 128
                    for o in range(min(t, nb) + 1):
                        slot = (t - o) % NTg[g]
                        mi = MI[g][0] if o == 0 else (MI[g][2] if o == nb else MI[g][1])
                        for h in range(4):
                            unitsA.append((kt[g][:, h // 2, slot * 128:(slot + 1) * 128], qT[:, h % 2, g, h // 2, :], mi,
                                           vr[g][:, slot, h, :], pU[0:65, h * 128:(h + 1) * 128], 128, [kt[g].b, qT.b, vr[g].b]))
                for mt in range(2):
                    for h in range(4):
                        unitsB.append((memKT[:, h // 2, mt * 128:(mt + 1) * 128], mqT[:, h % 2, h // 2, :], None,
                                       memV[:, mt, h, :], pU[0:65, 512 + h * 128:512 + (h + 1) * 128], 128, [memKT.b, mqT.b, memV.b]))
                for i, u in enumerate(unitsA):
                    attn_unit(*u, flags=(i == 0, i == len(unitsA) - 1))
                for i, u in enumerate(unitsB):
                    attn_unit(*u, flags=(i == 0, i == len(unitsB) - 1))
            else:
                nA = 12 + NS * 12
                ia = 0
                for g in range(3):
                    for h in range(4):
                        attn_unit(ktS[:, g, h // 2, :], qT[:, h % 2, g, h // 2, :], 8, vS[:, g, h, :],
                                  pU[0:65, h * 128:(h + 1) * 128], 128, [ktS.b, qT.b, vS.b], flags=(ia == 0, False))
                        ia += 1
                for s in range(NS):
                    for g in range(3):
                        W, dil = DIL[g]
                        cx.dma("sp", kin[:, :], self.cdil[g][s, 0:W:dil, 0].rearrange("r h d -> r (h d)"), w=[kin.b])
                        cx.dma("sp", vin[:, :], self.cdil[g][s, 0:W:dil, 1].rearrange("r h d -> r (h d)"), w=[vin.b])
                        for pi in range(2):
                            P.tr(pT.b, pT[:, pi * 128:(pi + 1) * 128], kin[:, pi * 128:(pi + 1) * 128], ident[:, :], r=[kin.b, ident.b])
                        P.evac(KsT.b, KsT[:, :, :], pT[:, 0:256].rearrange("p (k t) -> p k t", k=2), r=[pT.b])
                        P.evac(Vs.b, Vs[:, :, 0:64], vin[:, :].rearrange("p (h d) -> p h d", h=4), r=[vin.b])
                        for h in range(4):
                            ia += 1
                            attn_unit(KsT[:, h // 2, :], qT[:, h % 2, g, h // 2, s:s + 1], None, Vs[:, h, :],
                                      pU[0:65, h * 128 + s:h * 128 + s + 1], 1, [KsT.b, qT.b, Vs.b], flags=(False, ia == nA))
                ib = 0
                for s in range(NS):
                    for mt in range(2):
                        cx.dma("sp", mks[:, :], self.cmem[0, s, mt * 128:(mt + 1) * 128].rearrange("m a h d -> m (a h d)"), w=[mks.b])
                        P.mem_tile(mks, mt, memKTs, memVs, pT)
                    for mt in range(2):
                        for h in range(4):
                            ib += 1
                            attn_unit(memKTs[:, h // 2, mt * 128:(mt + 1) * 128], mqT[:, h % 2, h // 2, s:s + 1], None,
                                      memVs[:, mt, h, :], pU[0:65, 512 + h * 128 + s:512 + h * 128 + s + 1], 1, [memKTs.b, mqT.b, memVs.b],
                                      flags=(ib == 1, ib == NS * 8))
            for hf in range(2):
                hs = slice(hf * 512, (hf + 1) * 512)
                P.ts("dve", rden.b, rden[64:65, hs], pU[64:65, hs], 1e-20, None, ALU.max, None, r=[pU.b])
                cx.op("dve", lambda e, hs=hs: e.reciprocal(out=rden[64:65, hs], in_=rden[64:65, hs]), r=[rden.b], w=[rden.b])
                P.mm(pT.b, pT[0:64, hs], self.sel64[:, :], rden[:, hs], r=[self.sel64.b, rden.b])
                P.copy("act", Usb.b, Usb[:, hs], pU[0:64, hs], r=[pU.b])
                P.tt("dve", oT.b, oT[:, 4 * hf:4 * hf + 4, :].rearrange("p h t -> p (h t)"), Usb[:, hs], pT[0:64, hs], ALU.mult, r=[Usb.b, pT.b])
            if samp:
                P.memset("dve", oT.b, oT[:, :, NS:128], 0.0)
            for hf in range(2):
                for hh in range(8):
                    P.mm(pP.b, pP[:, hf * 512:(hf + 1) * 512], oT[:, hh, :], wout[:, hh, hf * 512:(hf + 1) * 512], r=[oT.b, wout.b], start=(hh == 0), stop=(hh == 7))
            for hf in range(2):
                hs = slice(hf * 512, (hf + 1) * 512)
                P.stt(zt.b, zt[:, hs], xi[:, hs], ALPHA, pP[:, hs], ALU.mult, ALU.add, r=[xi.b, pP.b])
            P.ln_tile(zt, zt, lng, lnb, lnt)
            cx.dma("pool", self.x1[t * 128:(t + 1) * 128, :], zt[:, :], r=[zt.b], w=[self.x1b[t]])
            if self.debug:
                cx.dma("pool", self.dbg_x1[t * 128:(t + 1) * 128, :], zt[:, :], r=[zt.b])
            for kc in range(8):
                P.tr(pT.b, pT[:, kc * 128:(kc + 1) * 128], zt[:, kc * 128:(kc + 1) * 128], ident[:, :], r=[zt.b, ident.b])
            P.evac8(x1T_, pT)
            cx.dma("pool", self.x1T[:, :, t * 128:(t + 1) * 128].rearrange("k p t -> p k t"), x1T_[:, :, :], r=[x1T_.b], w=[self.x1b[t]])
        P.phase_end()

    def blocks(self):
        nt = self.NT + 1
        BT = min(13, nt)
        return [list(range(b0, min(nt, b0 + BT))) for b0 in range(0, nt, BT)], BT

    def peer_p1(self, L):
        P, cx, NT = self, self.cx, self.NT
        blks, BT = self.blocks()
        if not hasattr(self, "wgt"):
            self.wgt = [self.dscr("wgt%d" % b_, [128, 128, BT * 128], BF16) for b_ in range(len(blks))]
            self.wgtb = [Buf("wgt%d" % t) for t in range(NT + 1)]
        P.phase_begin()
        wq = P.sbp([128, 8, 1024], F32)
        for kc in range(8):
            cx.dma("sp", wq[:, kc, :], self.peer_wq[L, kc * 128:(kc + 1) * 128, :], w=[wq.b])
        kT = P.sbp([128, 8, 128], F32)
        cx.dma("sp", kT[:, :, :], self.peer_keysT[L], w=[kT.b])
        pQ, pS = P.psp([128, 1024]), P.psp([128, 2048])
        pW = [P.psp([128, 1024], BF16) for _ in range(2)]
        xin = [P.sbp([128, D]) for _ in range(2)]
        xT = [P.sbp([128, 8, 128], F32)]
        qpz = P.sbp([128, 2, 8, 128], F32)
        P.memset("pool", qpz.b, qpz[:, :, :, :], 0.0)
        ssb = P.sbp([128, 8, 2, 128])
        work = P.sbp([128, 8, 2, 128])
        tops = P.sbp([128, 8, 2, 16])
        cand = P.sbp([128, 8, 256])
        cw = P.sbp([128, 8, 256])
        v24 = P.sbp([128, 8, 24])
        thr, nthr, Z, lnZ, off, nlz = [P.sbp([128, 8]) for _ in range(6)]
        e16 = P.sbp([128, 16])
        bsb = P.sbp([128, 8, 128])
        pen = P.sbp([128, 8, 128])
        Tm = [P.sbp([128, 32, 128]) for _ in range(2)]
        Eb = [P.sbp([128, 4096], BF16) for _ in range(2)]
        Wh = [P.sbp([128, 4096], BF16) for _ in range(2)]
        Wacc = [P.sbp([128, 4096], BF16) for _ in range(2)]
        WT = [P.sbp([128, 32, 128], BF16) for _ in range(2)]

        def load(t):
            cx.dma("sp", xin[t % 2][:, :], self.x1[t * 128:(t + 1) * 128, :], r=[self.x1b[t]], w=[xin[t % 2].b])

        load(0)
        k = 0
        for t in range(NT + 1):
            if t + 1 <= NT:
                load(t + 1)
            x = xT[0]
            for kc in range(8):
                P.tr(pS.b, pS[:, kc * 128:(kc + 1) * 128], xin[t % 2][:, kc * 128:(kc + 1) * 128], self.ident[:, :], r=[xin[t % 2].b, self.ident.b])
            P.evac8(x, pS)
            for h in range(8):
                for kc in range(8):
                    P.mm(pQ.b, pQ[:, h * 128:(h + 1) * 128], wq[:, kc, h * 128:(h + 1) * 128], x[:, kc, :], r=[wq.b, x.b], start=(kc == 0), stop=(kc == 7))
            for b0 in (0, 4):
                P.evac(qpz.b, qpz[0:64, 0, b0:b0 + 4, :], pQ[0:64, b0 * 128:(b0 + 4) * 128].rearrange("p (k t) -> p k t", k=4), r=[pQ.b])
                P.evac(qpz.b, qpz[64:128, 1, b0:b0 + 4, :], pQ[64:128, b0 * 128:(b0 + 4) * 128].rearrange("p (k t) -> p k t", k=4), r=[pQ.b])
            for h in range(8):
                for c in range(2):
                    P.mm(pS.b, pS[:, (2 * h + c) * 128:(2 * h + c + 1) * 128], qpz[:, c, h, :], kT[:, h, :], r=[qpz.b, kT.b])
            for bk in range(4):
                P.evac(ssb.b, ssb[:, 2 * bk:2 * bk + 2, :, :].rearrange("p h c k -> p (h c k)"), pS[:, bk * 512:(bk + 1) * 512], r=[pS.b])
            for h in range(8):
                for c in range(2):
                    cx.op("dve", lambda e, h=h, c=c: e.max(out=tops[:, h, c, 0:8], in_=ssb[:, h, c, :]), r=[ssb.b], w=[tops.b])
                    cx.op("dve", lambda e, h=h, c=c: e.match_replace(out=work[:, h, c, :], in_to_replace=tops[:, h, c, 0:8], in_values=ssb[:, h, c, :], imm_value=-1e30), r=[ssb.b, tops.b], w=[work.b])
                    cx.op("dve", lambda e, h=h, c=c: e.max(out=tops[:, h, c, 8:16], in_=work[:, h, c, :]), r=[work.b], w=[tops.b])
            P.tt("dve", cand.b, cand[:, :, :].rearrange("p h (a b) -> p h a b", a=16),
                 tops[:, :, 0, :].unsqueeze(3).broadcast_to([128, 8, 16, 16]), tops[:, :, 1, :].unsqueeze(2).broadcast_to([128, 8, 16, 16]), ALU.add, r=[tops.b])
            for h in range(8):
                cx.op("dve", lambda e, h=h: e.max(out=v24[:, h, 0:8], in_=cand[:, h, :]), r=[cand.b], w=[v24.b])
                cx.op("dve", lambda e, h=h: e.match_replace(out=cw[:, h, :], in_to_replace=v24[:, h, 0:8], in_values=cand[:, h, :], imm_value=-1e30), r=[cand.b, v24.b], w=[cw.b])
                cx.op("dve", lambda e, h=h: e.max(out=v24[:, h, 8:16], in_=cw[:, h, :]), r=[cw.b], w=[v24.b])
                cx.op("dve", lambda e, h=h: e.match_replace(out=cw[:, h, :], in_to_replace=v24[:, h, 8:16], in_values=cw[:, h, :], imm_value=-1e30), r=[cw.b, v24.b], w=[cw.b])
                cx.op("dve", lambda e, h=h: e.max(out=v24[:, h, 16:24], in_=cw[:, h, :]), r=[cw.b], w=[v24.b])
            P.tt("dve", thr.b, thr[:, :], v24[:, :, 15], v24[:, :, 16], ALU.add, r=[v24.b])
            P.ts("dve", thr.b, thr[:, :], thr[:, :], 0.5, None, ALU.mult, None, r=[thr.b])
            P.ts("dve", nthr.b, nthr[:, :], thr[:, :], -1.0, None, ALU.mult, None, r=[thr.b])
            for h in range(8):
                P.act(e16.b, e16[:, :], v24[:, h, 0:16], AF.Exp, r=[v24.b, nthr.b], bias=nthr[:, h:h + 1], scale=1.0, accum_out=Z[:, h:h + 1])
            P.act(lnZ.b, lnZ[:, :], Z[:, :], AF.Ln, r=[Z.b, e16.b])
            P.tt("dve", off.b, off[:, :], thr[:, :], lnZ[:, :], ALU.add, r=[thr.b, lnZ.b])
            P.ts("dve", nlz.b, nlz[:, :], lnZ[:, :], -1.0, None, ALU.mult, None, r=[lnZ.b])
            P.tt("dve", bsb.b, bsb[:, :, :], ssb[:, :, 0, :], off[:, :].unsqueeze(2).broadcast_to([128, 8, 128]), ALU.subtract, r=[ssb.b, off.b])
            for c in range(2):
                P.tt("dve", pen.b, pen[:, :, :], ssb[:, :, c, :], tops[:, :, c, 15:16].broadcast_to([128, 8, 128]), ALU.is_ge, r=[ssb.b, tops.b])
                P.ts("dve", pen.b, pen[:, :, :], pen[:, :, :], 1.0, 1.0e30, ALU.subtract, ALU.mult, r=[pen.b])
                if c == 0:
                    P.tt("dve", bsb.b, bsb[:, :, :], bsb[:, :, :], pen[:, :, :], ALU.add, r=[bsb.b, pen.b])
                else:
                    P.tt("dve", ssb.b, ssb[:, :, 1, :], ssb[:, :, 1, :], pen[:, :, :], ALU.add, r=[ssb.b, pen.b])
            blk = [bi for bi, bl in enumerate(blks) if t in bl][0]
            pos = t - blks[blk][0]
            for ic in range(4):
                wa = Wacc[ic % 2]
                for h in range(8):
                    k += 1
                    tm_, eb_, wh_ = Tm[k % 2], Eb[k % 2], Wh[k % 2]
                    P.tt("pool", tm_.b, tm_[:, :, :], ssb[:, h, 1, :].unsqueeze(1).broadcast_to([128, 32, 128]),
                         bsb[:, h, ic * 32:(ic + 1) * 32].unsqueeze(2).broadcast_to([128, 32, 128]), ALU.add, r=[ssb.b, bsb.b])
                    tf = tm_[:, :, :].rearrange("p i j -> p (i j)")
                    P.act(eb_.b, eb_[:, :], tf, AF.Exp, r=[tm_.b])
                    if h == 0:
                        P.stt(wa.b, wa[:, :], tf, nlz[:, h:h + 1], eb_[:, :], ALU.is_ge, ALU.mult, r=[tm_.b, eb_.b, nlz.b])
                    else:
                        P.stt(wh_.b, wh_[:, :], tf, nlz[:, h:h + 1], eb_[:, :], ALU.is_ge, ALU.mult, r=[tm_.b, eb_.b, nlz.b])
                        P.tt("pool", wa.b, wa[:, :], wa[:, :], wh_[:, :], ALU.add, r=[wa.b, wh_.b])
                wt = WT[ic % 2]
                for i8 in range(4):
                    pw = pW[i8 % 2]
                    for j in range(8):
                        i = i8 * 8 + j
                        P.tr(pw.b, pw[:, j * 128:(j + 1) * 128], wa[:, i * 128:(i + 1) * 128], self.identb[:, :], r=[wa.b, self.identb.b])
                    P.evac(wt.b, wt[:, i8 * 8:(i8 + 1) * 8, :], pw[:, :].rearrange("p (k t) -> p k t", k=8), r=[pw.b])
                cx.dma("pool", self.wgt[blk][ic * 32:(ic + 1) * 32, :, pos * 128:(pos + 1) * 128].rearrange("i j t -> j i t"), wt[:, :, :], r=[wt.b], w=[self.wgtb[t]])
        P.phase_end()

    def peer_p2(self, L, final):
        P, cx, NT = self, self.cx, self.NT
        blks, BT = self.blocks()
        NCH = self.peer_uT.shape[1]
        G = min(8, NCH)
        P.phase_begin()
        lng, lnb = P.sbp([128, D]), P.sbp([128, D])
        P.bcast_row(lng, self.ln_g[L, 1:2, :])
        P.bcast_row(lnb, self.ln_b[L, 1:2, :])
        acc = P.sbp([128, BT, 1024])
        xTb = P.sbp([128, 8, BT * 128], BF16)
        A = P.sbp([128, G, BT * 128], BF16)
        vbf = P.sbp([128, G, 1024], BF16)
        ust = [P.sbp([128, 8, 128]) for _ in range(2)]
        ubf = [P.sbp([128, 8, 128], BF16) for _ in range(2)]
        vst = [P.sbp([128, 1024]) for _ in range(2)]
        actb = [P.sbp([128, BT * 128], BF16) for _ in range(2)]
        wT = [P.sbp([128, BT * 128], BF16) for _ in range(2)]
        lnt = (P.sbp([128, 2, 6]), P.sbp([128, 2]), P.sbp([128, 1]))
        xo = [P.sbp([128, 8, 128], BF16) for _ in range(2)]
        pH = P.psp([128, 2048])
        pO = P.psp([128, 2048])
        Ob = [Buf("po%d" % i) for i in range(4)]
        oc = 0
        for bi, tiles in enumerate(blks):
            nb = len(tiles)
            ntok = nb * 128
            t0 = tiles[0]
            for tt, t in enumerate(tiles):
                cx.dma("sp", acc[:, tt, :], self.x1[t * 128:(t + 1) * 128, :], r=[self.x1b[t]], w=[acc.b])
            cx.dma("sp", xTb[:, :, 0:ntok], self.x1T[:, :, t0 * 128:t0 * 128 + ntok].rearrange("k p t -> p k t"), r=[self.x1b[t] for t in tiles], w=[xTb.b])
            for tt in range(nb):
                P.act(acc.b, acc[:, tt, :], acc[:, tt, :], AF.Copy, r=[acc.b], scale=ALPHA)
            ncb = (ntok + 511) // 512
            cw_ = ntok // ncb
            assert cw_ * ncb == ntok and cw_ <= 512
            for g0 in range(0, NCH, G):
                for gi in range(G):
                    i = g0 + gi
                    us, ub, vs_, ab, wt = ust[i % 2], ubf[i % 2], vst[i % 2], actb[i % 2], wT[i % 2]
                    cx.dma("sp", us[:, :, :], self.peer_uT[L, i], w=[us.b])
                    cx.dma("sp", vs_[:, :], self.peer_v[L, i * 128:(i + 1) * 128, :], w=[vs_.b])
                    cx.dma("sp", wt[:, 0:ntok], self.wgt[bi][i, :, 0:ntok], r=[self.wgtb[t] for t in tiles], w=[wt.b])
                    P.copy("pool", ub.b, ub[:, :, :], us[:, :, :], r=[us.b])
                    P.copy("pool", vbf.b, vbf[:, gi, :], vs_[:, :], r=[vs_.b])
                    for cb in range(ncb):
                        for kc in range(8):
                            P.mm(pH.b, pH[:, cb * 512:cb * 512 + cw_], ub[:, kc, :], xTb[:, kc, cb * cw_:(cb + 1) * cw_], r=[ub.b, xTb.b], start=(kc == 0), stop=(kc == 7))
                    for cb in range(ncb):
                        P.act(ab.b, ab[:, cb * cw_:(cb + 1) * cw_], pH[:, cb * 512:cb * 512 + cw_], AF.Gelu_apprx_tanh, r=[pH.b])
                    P.tt("dve", A.b, A[:, gi, 0:ntok], ab[:, 0:ntok], wt[:, 0:ntok], ALU.mult, r=[ab.b, wt.b])
                for tt in range(nb):
                    for hf in range(2):
                        ob = Ob[oc % 4]
                        po = pO[:, (oc % 4) * 512:(oc % 4 + 1) * 512]
                        oc += 1
                        for gi in range(G):
                            P.mm(ob, po, A[:, gi, tt * 128:(tt + 1) * 128], vbf[:, gi, hf * 512:(hf + 1) * 512], r=[A.b, vbf.b], start=(gi == 0), stop=(gi == G - 1))
                        P.tt("dve", acc.b, acc[:, tt, hf * 512:(hf + 1) * 512], acc[:, tt, hf * 512:(hf + 1) * 512], po, ALU.add, r=[acc.b, ob])
            for tt, t in enumerate(tiles):
                class _V:
                    def __init__(s, ap, b):
                        s.ap, s.b = ap, b

                    def __getitem__(s, k):
                        return s.ap[k]
                zv = _V(acc[:, tt, :], acc.b)
                P.ln_tile(zv, zv, lng, lnb, lnt)
                if final:
                    dst = self.y_s[:, :] if t == NT else self.y_p[t * 128:(t + 1) * 128, :]
                    cx.dma("pool", dst, acc[:, tt, :], r=[acc.b])
                else:
                    cx.dma("pool", self.x1[t * 128:(t + 1) * 128, :], acc[:, tt, :], r=[acc.b], w=[self.x1b[t]])
                    x_o = xo[t % 2]
                    for kc in range(8):
                        P.tr(pH.b, pH[:, kc * 128:(kc + 1) * 128], acc[:, tt, kc * 128:(kc + 1) * 128], self.ident[:, :], r=[acc.b, self.ident.b])
                    P.evac8(x_o, pH)
                    cx.dma("pool", self.x1T[:, :, t * 128:(t + 1) * 128].rearrange("k p t -> p k t"), x_o[:, :, :], r=[x_o.b], w=[self.x1b[t]])
                if self.debug:
                    cx.dma("pool", self.dbg_x1[t * 128:(t + 1) * 128, :], acc[:, tt, :], r=[acc.b])
        P.phase_end()


def host_l1_tables():
    half = HD // 2
    inv = (10000.0 ** (-np.arange(half, dtype=np.float64) / half))
    cpos = np.arange(512) * 16 + 31
    d = np.arange(128) % 64
    ang = cpos[None, :] * inv[d % 32][:, None]
    cosC = np.cos(ang.astype(np.float32)).astype(np.float32)
    sn = np.sin(ang.astype(np.float32)).astype(np.float32)
    sinC = np.where((d < 32)[:, None], -sn, sn).astype(np.float32)
    qi = np.arange(128)[:, None]
    u = np.arange(1024)[None, :] - 512
    mrel = np.where(16 * u + 31 <= qi, 0.0, NEGM).astype(ml_dtypes.bfloat16)
    ci = np.arange(128)[:, None]
    x = np.arange(2304)[None, :]
    mtc = np.where(16 * ci + 31 <= x - 128, 0.0, NEGM).astype(ml_dtypes.bfloat16)
    bp = np.arange(264)[None, :] - 128
    cur = (qi >= 64).astype(np.int64)
    valid = bp <= cur
    forced = (bp == cur) | (bp == cur - 1)
    kp = (valid & ~forced).astype(np.float32)
    ad = np.where(~valid, -1.0e30, np.where(forced, 1.0e9, 0.0)).astype(np.float32)
    selg = np.zeros((36, 36, 64), np.float32)
    for r in range(36):
        selg[r, r, :] = 1.0
    return dict(c_cosC=cosC, c_sinC=sinC, c_mrel=mrel, c_mtc=mtc, c_kp=kp, c_ad=ad, c_selg=selg,
                c_riota=np.arange(128, dtype=np.float32).reshape(128, 1))


NKT = 65
NBK = 136


def l1_context(self):
    P = self
    C = type("C", (), {})()
    C.skT = P.sbp([128, NKT * 128], BF16)
    C.svx = P.sbp([128, NKT, 2, 65], BF16)
    C.ckT = P.sbp([128, 512], BF16)
    C.cvx = P.sbp([128, 4, 2, 65], BF16)
    C.wkT = P.sbp([128, 5 * 128], BF16)
    C.wvx = P.sbp([128, 5, 2, 65], BF16)
    for tb in (C.svx, C.cvx, C.wvx):
        P.memset("pool", tb.b, tb[:, :, :, :], 1.0)
    P.memset("pool", C.skT.b, C.skT[:, :], 0.0)
    C.wkS = P.dscr("wkS", [NKT, 128, 128], BF16)
    C.wvS = P.dscr("wvS", [NKT, 128, 130], BF16)
    C.wsb = [Buf("ws%d" % i) for i in range(NKT)]
    C.rowsS = P.sbp([128, 768])
    C.qzS = P.sbp([128, 2, 6, 128], BF16)
    C.mqzS = P.sbp([128, 2, 2, 128], BF16)
    C.gTS = P.sbp([36, 128])
    C.oTS = P.sbp([64, 16, 128], BF16)
    C.xinS = P.sbp([128, D])
    P.memset("pool", C.oTS.b, C.oTS[:, :, :], 0.0)
    return C


def l1_kstage(self, C, unit):
    P, cx, NT, T = self, self.cx, self.NT, self.T
    P.phase_begin()
    ident = self.ident
    C.ckraw = P.sbp([128, NKT * 128], BF16)
    C.cvraw = P.sbp([128, NKT * 128], BF16)
    for tb in (C.ckraw, C.cvraw):
        P.memset("pool", tb.b, tb[:, :], 0.0)
    pT, pP, pH, pK = P.psp([128, 1024]), P.psp([128, 1024]), P.psp([128, 512]), P.psp([128, 1024])
    st = [P.sbp([128, 1024]) for _ in range(2)]
    w1z = P.sbp([128, 2, 2, 32, 128], BF16)
    P.memset("pool", w1z.b, w1z[:, :, :, :, :].rearrange("p a b l k -> p (a b l k)"), 0.0)
    for kind in range(2):
        for h in range(2):
            for l0 in range(0, 32, 8):
                s_ = st[(l0 // 8) % 2]
                cx.dma("sp", s_[h * 64:(h + 1) * 64, :].rearrange("p (l k) -> p l k", l=8),
                       self.cmp_w1[kind, l0 * 64:(l0 + 8) * 64, :].rearrange("(l d) k -> d l k", d=64), w=[s_.b])
                P.evac(w1z.b, w1z[h * 64:(h + 1) * 64, kind, h, l0:l0 + 8, :], s_[h * 64:(h + 1) * 64, :].rearrange("p (l k) -> p l k", l=8), r=[s_.b])
    w1n = P.sbp([128, 2, 16, 128], BF16)
    pe = P.sbp([128, 2, 16], BF16)
    pst = P.sbp([128, 16])
    for kind in range(2):
        for c0 in (0, 8):
            s_ = st[(c0 // 8) % 2]
            cx.dma("sp", s_[:, :].rearrange("p (c k) -> p c k", c=8), self.cmp_w1[kind, c0 * 128:(c0 + 8) * 128, :].rearrange("(c p) k -> p c k", p=128), w=[s_.b])
            P.evac(w1n.b, w1n[:, kind, c0:c0 + 8, :], s_[:, :].rearrange("p (c k) -> p c k", c=8), r=[s_.b])
        cx.dma("sp", pst[:, :], self.cmp_pe[kind], w=[pst.b])
        P.evac(pe.b, pe[:, kind, :], pst[:, :], r=[pst.b])
    w2z = P.sbp([128, 2, 2, 128], BF16)
    P.memset("pool", w2z.b, w2z[:, :, :, :], 0.0)
    w2v = P.sbp([128, 64], BF16)
    w2s = P.sbp([128, 64])
    for ri, src in enumerate((self.cmp_w2[0], self.cmp_w2r[0])):
        cx.dma("sp", w2s[:, :], src, w=[w2s.b])
        for h in range(2):
            P.evac(w2z.b, w2z[:, ri, h, h * 64:(h + 1) * 64], w2s[:, :], r=[w2s.b])
    cx.dma("sp", w2s[:, :], self.cmp_w2[1], w=[w2s.b])
    P.evac(w2v.b, w2v[:, :], w2s[:, :], r=[w2s.b])
    cosC, sinC = P.sbp([128, 512]), P.sbp([128, 512])
    cx.dma("sp", cosC[:, :], self.c_cosC[:, :], w=[cosC.b])
    cx.dma("sp", sinC[:, :], self.c_sinC[:, :], w=[sinC.b])
    rows = [P.sbp([128, 768]) for _ in range(2)]
    self.marks.append(("kw", cx.ninstr))

    def store_tile(t, rw, win):
        kinds = [0, 1, 2] + ([4] if win else [])
        for j, kd in enumerate(kinds):
            P.tr(pT.b, pT[:, j * 128:(j + 1) * 128], rw[:, kd * 128:(kd + 1) * 128], ident[:, :], r=[rw.b, ident.b])
        sl = slice(t * 128, (t + 1) * 128)
        P.copy("dve", C.ckraw.b, C.ckraw[:, sl], pT[:, 0:128], r=[pT.b])
        P.copy("dve", C.cvraw.b, C.cvraw[:, sl], pT[:, 128:256], r=[pT.b])
        P.copy("dve", C.skT.b, C.skT[:, sl], pT[:, 256:384], r=[pT.b])
        P.evac(C.svx.b, C.svx[:, t, :, 0:64], rw[:, 384:512].rearrange("p (h d) -> p h d", h=2), r=[rw.b])
        if win:
            ws = t % 5
            P.evac(C.wkT.b, C.wkT[:, ws * 128:(ws + 1) * 128], pT[:, 384:512], r=[pT.b])
            P.evac(C.wvx.b, C.wvx[:, ws, :, 0:64], rw[:, 640:768].rearrange("p (h d) -> p h d", h=2), r=[rw.b])
            if unit == "p":
                cx.dma("pool", C.wkS[t], C.wkT[:, ws * 128:(ws + 1) * 128], r=[C.wkT.b], w=[C.wsb[t]])
                cx.dma("pool", C.wvS[t].rearrange("p (h d) -> p h d", h=2), C.wvx[:, ws, :, :], r=[C.wvx.b], w=[C.wsb[t]])

    if unit == "p":
        wkv = P.sbp([128, 8, 768], BF16)
        P.load_w_bf16(wkv, self.w_kv_b, D, 768, st)
        xT = [P.sbp([128, 8, 128], BF16) for _ in range(2)]
        cs, sn = P.sbp([128, 32]), P.sbp([128, 32])
        tm = [P.sbp([128, 2, 2, 32]) for _ in range(4)]
        for t in range(NT + 1):
            x, rw = xT[t % 2], rows[t % 2]
            if t == NT:
                rw = C.rowsS
            cx.dma("sp", x[:, :, :], self.x1T[:, :, t * 128:(t + 1) * 128].rearrange("k p t -> p k t"), r=[self.x1b[t]], w=[x.b])
            cx.dma("sp", cs[:, :], self.c_cos[t * 128:(t + 1) * 128, :], w=[cs.b])
            cx.dma("sp", sn[:, :], self.c_sin[t * 128:(t + 1) * 128, :], w=[sn.b])
            for cb, (c0, c1) in enumerate(((0, 512), (512, 768))):
                hb = pP[:, cb * 512:cb * 512 + (c1 - c0)]
                for kc in range(8):
                    P.mm(pP.b, hb, x[:, kc, :], wkv[:, kc, c0:c1], r=[x.b, wkv.b], start=(kc == 0), stop=(kc == 7))
                P.evac(rw.b, rw[:, c0:c1], hb, r=[pP.b])
            v5 = rw[:, :].rearrange("p (k h x i) -> p k h x i", k=6, h=2, x=2)
            x1v, x2v = v5[:, 2:6:2, :, 0, :], v5[:, 2:6:2, :, 1, :]
            cb_ = cs[:, :].unsqueeze(1).unsqueeze(1).broadcast_to([128, 2, 2, 32])
            sb_ = sn[:, :].unsqueeze(1).unsqueeze(1).broadcast_to([128, 2, 2, 32])
            P.tt("dve", tm[0].b, tm[0][:, :, :, :], x1v, cb_, ALU.mult, r=[rw.b, cs.b])
            P.tt("pool", tm[1].b, tm[1][:, :, :, :], x2v, sb_, ALU.mult, r=[rw.b, sn.b])
            P.tt("dve", tm[2].b, tm[2][:, :, :, :], x1v, sb_, ALU.mult, r=[rw.b, sn.b])
            P.tt("pool", tm[3].b, tm[3][:, :, :, :], x2v, cb_, ALU.mult, r=[rw.b, cs.b])
            P.tt("dve", rw.b, x1v, tm[0][:, :, :, :], tm[1][:, :, :, :], ALU.subtract, r=[tm[0].b, tm[1].b])
            P.tt("dve", rw.b, x2v, tm[3][:, :, :, :], tm[2][:, :, :, :], ALU.add, r=[tm[2].b, tm[3].b])
            if t < NT:
                cx.dma("pool", self.nsa_kv_p[t * 128:(t + 1) * 128].rearrange("t k h d -> t (k h d)"), rw[:, 0:512], r=[rw.b])
                nW = min(512, T) // 128
                if t >= NT - nW:
                    r0 = 128 * (t - (NT - nW))
                    cx.dma("pool", self.win_p[r0:r0 + 128].rearrange("t k h d -> t (k h d)"), rw[:, 512:768], r=[rw.b])
                store_tile(t, rw, True)
            else:
                cx.dma("pool", self.nsa_kv_s[:, :].rearrange("t k h d -> t (k h d)"), rw[:, 0:512], r=[rw.b], w=[C.rowsS.b])
                for s in range(NS):
                    cx.dma("pool", self.win_s[s, 0:511].rearrange("t k h d -> t (k h d)"), self.cwin[s, 1:512].rearrange("t k h d -> t (k h d)"))
                cx.dma("pool", self.win_s[0:NS, 511].rearrange("s k h d -> s (k h d)"), rw[0:NS, 512:768], r=[rw.b])
    else:
        s = unit
        pti = P.sbp([128, 64], I32)
        ptf = P.sbp([128, 64])
        idx = P.sbp([128, 64], mybir.dt.uint32)
        rio = P.sbp([128, 1])
        cx.dma("sp", pti[:, :], self.ptab[s:s + 1, :].broadcast_to([128, 64]), w=[pti.b])
        cx.dma("sp", rio[:, :], self.c_riota[:, :], w=[rio.b])
        P.copy("dve", ptf.b, ptf[:, :], pti[:, :], r=[pti.b])
        P.ts("dve", ptf.b, ptf[:, :], ptf[:, :], 128.0, rio[:, 0:1], ALU.mult, ALU.add, r=[ptf.b, rio.b])
        P.copy("dve", idx.b, idx[:, :], ptf[:, :], r=[ptf.b])
        poolrows = self.pool.rearrange("g r k h d -> (g r) (k h d)")
        for t in range(NKT):
            rw = rows[t % 2]
            win = t >= NKT - 5
            if t < NKT - 1 and (cx.limit is not None and cx.ninstr >= cx.limit):
                pass
            elif t < NKT - 1:
                E = cx.E["pool"]
                cx._deps(E, [idx.b], [rw.b], is_dma=True)
                Q = cx.dq["pool"]
                slot = Q["next"]
                Q["next"] = (slot + 1) % cx.NDQ
                ent = Q["sems"][slot]
                if ent[1] > 0:
                    cx._wait(E, (ent[0], 16 * ent[1], "dma"))
                ins = self.nc.gpsimd.indirect_dma_start(out=rw[:, 0:512], out_offset=None, in_=poolrows,
                                                        in_offset=bass.IndirectOffsetOnAxis(ap=idx[:, t:t + 1], axis=0))
                ent[1] += 1
                ins.then_inc(ent[0], 16)
                cx._mark((ent[0], 16 * ent[1], "dma"), [idx.b], [rw.b])
                cx.ninstr += 1
                if win:
                    cx.dma("sp", rw[:, 512:768], self.cwin[s, (t - (NKT - 5)) * 128:(t - (NKT - 5) + 1) * 128].rearrange("t k h d -> t (k h d)"), w=[rw.b])
            else:
                P.memset("dve", rw.b, rw[:, :], 0.0)
                cx.dma("sp", rw[0:1, :], C.rowsS[s:s + 1, :], r=[C.rowsS.b], w=[rw.b])
            store_tile(t, rw, win)
    self.marks.append(("kt", cx.ninstr))
    bias = P.sbp([128, 1])
    Hs = [P.sbp([128, 512], BF16) for _ in range(2)]
    ck, ckr = P.sbp([128, 512]), P.sbp([128, 512])
    for kind in range(2):
        raw = C.ckraw if kind == 0 else C.cvraw
        for ch in range(16):
            P.mm(pK.b, pK[:, 0:1], w1n[:, kind, ch, :], pe[:, kind, ch:ch + 1], r=[w1n.b, pe.b], start=(ch == 0), stop=(ch == 15))
        P.copy("dve", bias.b, bias[:, :], pK[:, 0:1], r=[pK.b])
        for h in range(2):
            for l in range(32):
                P.mm(pH.b, pH[:, :], w1z[:, kind, h, l, :], raw[:, l:l + 16 * 511 + 1:16], r=[w1z.b, raw.b], start=(l == 0), stop=(l == 31))
            P.act(Hs[h].b, Hs[h][:, :], pH[:, :], AF.Gelu_apprx_tanh, r=[pH.b, bias.b], bias=bias[:, 0:1], scale=1.0)
        if kind == 0:
            for ri in range(2):
                for h in range(2):
                    P.mm(pK.b, pK[:, ri * 512:(ri + 1) * 512], w2z[:, ri, h, :], Hs[h][:, :], r=[w2z.b, Hs[h].b], start=(h == 0), stop=(h == 1))
            P.tt("dve", ck.b, ck[:, :], pK[:, 0:512], cosC[:, :], ALU.mult, r=[pK.b, cosC.b])
            P.tt("dve", ckr.b, ckr[:, :], pK[:, 512:1024], sinC[:, :], ALU.mult, r=[pK.b, sinC.b])
            P.tt("dve", C.ckT.b, C.ckT[:, :], ck[:, :], ckr[:, :], ALU.add, r=[ck.b, ckr.b])
        else:
            for h in range(2):
                for ct in range(4):
                    j = h * 4 + ct
                    P.mm(pK.b, pK[:, j * 64:(j + 1) * 64], Hs[h][:, ct * 128:(ct + 1) * 128], w2v[:, :], r=[Hs[h].b, w2v.b])
            for h in range(2):
                P.evac(C.cvx.b, C.cvx[:, :, h, 0:64], pK[:, h * 256:(h + 1) * 256].rearrange("p (c d) -> p c d", c=4), r=[pK.b])
    P.phase_end()


Model.l1_context = l1_context
Model.l1_kstage = l1_kstage


def l1_astage(self, C, unit):
    P, cx, NT, T = self, self.cx, self.NT, self.T
    P.phase_begin()
    ident, identb, masks = self.ident, self.identb, self.masks
    st = [P.sbp([128, 1024]) for _ in range(2)]
    pU, pM = P.psp([128, 1024]), P.psp([128, 1024])
    pS = [P.psp([128, 1024]) for _ in range(2)]
    memKT, memV = P.sbp([128, 2, 256], BF16), P.sbp([128, 2, 4, 65], BF16)
    P.memset("pool", memV.b, memV[:, :, :, :], 1.0)
    if unit == "p":
        P.phase_begin()
        tmpx, tmpT = P.sbp([128, D]), P.sbp([128, 8, 128], BF16)
        P.mem_kv(1, st, memKT, memV, tmpx, pS[0], pM, tmpT)
        P.phase_end()
    wout = P.sbp([64, 16, 1024], BF16)
    for hh in range(16):
        s_ = st[hh % 2]
        cx.dma("sp", s_[0:64, 0:1024], self.w_out_b[hh * 64:(hh + 1) * 64, :], w=[s_.b])
        P.evac(wout.b, wout[:, hh, :], s_[0:64, 0:1024], r=[s_.b])
    lng, lnb = P.sbp([128, D]), P.sbp([128, D])
    P.bcast_row(lng, self.ln_g[1, 0:1, :])
    P.bcast_row(lnb, self.ln_b[1, 0:1, :])
    mrel, mtc = P.sbp([128, 1024], BF16), P.sbp([128, 2304], BF16)
    kp, ad = P.sbp([128, 264]), P.sbp([128, 264])
    selg = P.sbp([36, 36, 64])
    for dst, src in ((mrel, self.c_mrel), (mtc, self.c_mtc), (kp, self.c_kp), (ad, self.c_ad)):
        cx.dma("sp", dst[:, :], src[:, :], w=[dst.b])
    cx.dma("sp", selg[:, :, :], self.c_selg[:, :, :], w=[selg.b])
    PT = [P.sbp([128, 768], BF16) for _ in range(3)]
    Pc = P.sbp([128, 512])
    den, rdn, thr = P.sbp([128, 1]), P.sbp([128, 1]), P.sbp([128, 1])
    impp = P.sbp([128, 520])
    P.memset("pool", impp.b, impp[:, :], 0.0)
    rA, rB = P.sbp([128, 128]), P.sbp([128, 128])
    score, work = P.sbp([128, NBK]), P.sbp([128, NBK])
    P.memset("pool", score.b, score[:, :], 0.0)
    m16 = P.sbp([128, 16])
    selb = P.sbp([128, NBK], BF16)
    selE = [P.sbp([128, 128], BF16) for _ in range(2)]
    rd = P.sbp([128, 768])
    P.memset("pool", rd.b, rd[:, :], 0.0)
    Usb, t1, t2 = P.sbp([64, 768]), P.sbp([64, 768]), P.sbp([64, 768])
    oacc = P.sbp([64, 2, 768])
    oT = P.sbp([64, 16, 128], BF16)
    zt = P.sbp([128, D])
    x1T_ = P.sbp([128, 8, 128], BF16)
    lnt = (P.sbp([128, 2, 6]), P.sbp([128, 2]), P.sbp([128, 1]))
    mks = P.sbp([128, 512])
    sct, pct = [0], [0]

    def attend(tq, ncol, qz, mqz, gT, qsel, ocols, memK, memVv):
        np_ = ncol
        halves = [(0, 3), (3, 6)] if ncol == 128 else [(0, 6)]
        N = 6 * ncol

        def bc(ap2, nh):
            return ap2.unsqueeze(1).broadcast_to([ap2.shape[0], nh, ncol])

        def s_unit(k_ap, kvh, extra, v_ap, first, last, rd_):
            ps = pS[sct[0] % 2]
            sct[0] += 1
            pt = PT[pct[0] % 3]
            pct[0] += 1
            for hi, (i0, i1) in enumerate(halves):
                nh = i1 - i0
                out = ps[:, hi * 512:hi * 512 + nh * ncol]
                P.mm(ps.b, out, k_ap, qz[:, kvh, i0:i1, :], r=rd_ + [C.qb], start=True, stop=(len(extra) == 0))
                for ei, (l_ap, r_fn, rb) in enumerate(extra):
                    P.mm(ps.b, out, l_ap, r_fn(nh), r=rb, start=False, stop=(ei == len(extra) - 1))
                P.act(pt.b, pt[:, i0 * ncol:i1 * ncol], out, AF.Exp, r=[ps.b], scale=0.125)
            for hi, (i0, i1) in enumerate(halves):
                nh = i1 - i0
                P.mm(pU.b, pU[0:65, hi * 512:hi * 512 + nh * ncol], v_ap, pt[:, i0 * ncol:i1 * ncol], r=[pt.b] + rd_,
                     start=(first and hi == 0) or (first and hi == 1), stop=last)

        def finish_branch(kvh, r, firstb):
            for hi, (i0, i1) in enumerate(halves):
                nh = i1 - i0
                w = nh * ncol
                us = slice(hi * 512, hi * 512 + w)
                fs = slice(i0 * ncol, i1 * ncol)
                P.ts("dve", rd.b, rd[64:65, fs], pU[64:65, us], 1e-20, None, ALU.max, None, r=[pU.b])
                cx.op("dve", lambda e, fs=fs: e.reciprocal(out=rd[64:65, fs], in_=rd[64:65, fs]), r=[rd.b], w=[rd.b])
                P.mm(pM.b, pM[0:64, 0:w], self.sel64[:, :], rd[:, fs], r=[self.sel64.b, rd.b])
                P.copy("act", Usb.b, Usb[:, fs], pU[0:64, us], r=[pU.b])
                P.tt("dve", t1.b, t1[:, fs], Usb[:, fs], pM[0:64, 0:w], ALU.mult, r=[Usb.b, pM.b])
                for i in range(i0, i1):
                    row = (kvh * 6 + i) * 3 + r
                    P.mm(pM.b, pM[0:64, 512 + (i - i0) * ncol:512 + (i - i0 + 1) * ncol], selg[:, row, :], gT, r=[selg.b, C.gb])
                if firstb:
                    P.tt("dve", oacc.b, oacc[:, kvh, fs], t1[:, fs], pM[0:64, 512:512 + w], ALU.mult, r=[t1.b, pM.b])
                else:
                    P.tt("dve", t2.b, t2[:, fs], t1[:, fs], pM[0:64, 512:512 + w], ALU.mult, r=[t1.b, pM.b])
                    P.tt("dve", oacc.b, oacc[:, kvh, fs], oacc[:, kvh, fs], t2[:, fs], ALU.add, r=[oacc.b, t2.b])

        s0 = 128 * tq
        for kvh in range(2):
            for i in range(6):
                P.mm(pM.b, pM[0:np_, 0:512], qz[:, kvh, i, :], C.ckT[:, :], r=[C.qb, C.ckT.b], start=True, stop=False)
                P.mm(pM.b, pM[0:np_, 0:512], identb[0:np_, 0:np_], mrel[0:np_, 512 - 8 * tq:1024 - 8 * tq], r=[identb.b, mrel.b], start=False, stop=True)
                P.act(Pc.b, Pc[0:np_, :], pM[0:np_, 0:512], AF.Exp, r=[pM.b], scale=0.125, accum_out=den[0:np_, :])
                P.ts("dve", rdn.b, rdn[0:np_, :], den[0:np_, :], 1e-20, None, ALU.max, None, r=[den.b, Pc.b])
                cx.op("dve", lambda e: e.reciprocal(out=rdn[0:np_, :], in_=rdn[0:np_, :]), r=[rdn.b], w=[rdn.b])
                if i == 0:
                    P.ts("dve", impp.b, impp[0:np_, 1:513], Pc[0:np_, :], rdn[0:np_, 0:1], None, ALU.mult, None, r=[Pc.b, rdn.b])
                else:
                    P.stt(impp.b, impp[0:np_, 1:513], Pc[0:np_, :], rdn[0:np_, 0:1], impp[0:np_, 1:513], ALU.mult, ALU.add, r=[Pc.b, rdn.b, impp.b])
            cx.op("dve", lambda e: e.tensor_reduce(out=rA[0:np_, :], in_=impp[0:np_, 1:513].rearrange("p (b m) -> p b m", m=4), axis=AX.X, op=ALU.add), r=[impp.b], w=[rA.b])
            cx.op("dve", lambda e: e.tensor_reduce(out=rB[0:np_, :], in_=impp[0:np_, 0:512].rearrange("p (b m) -> p b m", m=4), axis=AX.X, op=ALU.add), r=[impp.b], w=[rB.b])
            P.tt("dve", score.b, score[0:np_, 0:128], rA[0:np_, :], rB[0:np_, :], ALU.add, r=[rA.b, rB.b])
            P.memset("dve", score.b, score[0:np_, 128:NBK], 0.0)
            o_ = 128 - 2 * tq
            P.tt("dve", score.b, score[0:np_, :], score[0:np_, :], kp[0:np_, o_:o_ + NBK], ALU.mult, r=[score.b, kp.b])
            P.tt("dve", score.b, score[0:np_, :], score[0:np_, :], ad[0:np_, o_:o_ + NBK], ALU.add, r=[score.b, ad.b])
            P.memset("dve", score.b, score[0:np_, 0:1], 1.0e9)
            cx.op("dve", lambda e: e.max(out=m16[0:np_, 0:8], in_=score[0:np_, :]), r=[score.b], w=[m16.b])
            cx.op("dve", lambda e: e.match_replace(out=work[0:np_, :], in_to_replace=m16[0:np_, 0:8], in_values=score[0:np_, :], imm_value=-1e30), r=[score.b, m16.b], w=[work.b])
            cx.op("dve", lambda e: e.max(out=m16[0:np_, 8:16], in_=work[0:np_, :]), r=[work.b], w=[m16.b])
            P.ts("dve", thr.b, thr[0:np_, :], m16[0:np_, 15:16], -1e29, None, ALU.max, None, r=[m16.b])
            P.ts("dve", work.b, work[0:np_, :], score[0:np_, :], thr[0:np_, 0:1], 1.0, ALU.is_ge, ALU.subtract, r=[score.b, thr.b])
            P.ts("dve", selb.b, selb[0:np_, :], work[0:np_, :], -NEGM, None, ALU.mult, None, r=[work.b])
            cts = [ct for ct in range(4) if s0 - 2048 * ct >= 0]
            for j, ct in enumerate(cts):
                dl = s0 - 2048 * ct
                extra = []
                if dl < 2176:
                    extra.append((identb[:, :], (lambda nh, dl=dl: bc(mtc[:, dl + 128:dl + 128 + ncol], nh)), [identb.b, mtc.b]))
                s_unit(C.ckT[:, ct * 128:(ct + 1) * 128], kvh, extra, C.cvx[:, ct, kvh, :], j == 0, j == len(cts) - 1, [C.ckT.b, C.cvx.b])
            finish_branch(kvh, 0, True)
            for kt in range(tq + 1):
                se = selE[(kt + kvh) % 2]
                P.copy("dve", se.b, se[0:np_, :].rearrange("p (a b) -> p a b", a=2), selb[0:np_, 2 * kt:2 * kt + 2].unsqueeze(2).broadcast_to([np_, 2, 64]), r=[selb.b])
                extra = [(se[0:np_, :], (lambda nh: bc(qsel, nh)), [se.b, identb.b])]
                if kt == tq:
                    extra.append((identb[:, :], (lambda nh: bc(masks[:, 0, 0:ncol], nh)), [identb.b, masks.b]))
                s_unit(C.skT[:, kt * 128:(kt + 1) * 128], kvh, extra, C.svx[:, kt, kvh, :], kt == 0, kt == tq, [C.skT.b, C.svx.b])
            finish_branch(kvh, 1, False)
            kts = list(range(max(0, tq - 4), tq + 1))
            for j, kt in enumerate(kts):
                extra = []
                if kt == tq:
                    extra.append((identb[:, :], (lambda nh: bc(masks[:, 0, 0:ncol], nh)), [identb.b, masks.b]))
                elif kt == tq - 4:
                    extra.append((identb[:, :], (lambda nh: bc(masks[:, 1, 0:ncol], nh)), [identb.b, masks.b]))
                ws = kt % 5
                s_unit(C.wkT[:, ws * 128:(ws + 1) * 128], kvh, extra, C.wvx[:, ws, kvh, :], j == 0, j == len(kts) - 1, [C.wkT.b, C.wvx.b])
            finish_branch(kvh, 2, False)
            P.copy("act", ocols[0], ocols[1](kvh), oacc[:, kvh, 0:N].rearrange("p (i t) -> p i t", i=6), r=[oacc.b])
        first = True
        for mt in range(2):
            for h in range(4):
                ps = pS[sct[0] % 2]
                sct[0] += 1
                pt = PT[pct[0] % 3]
                pct[0] += 1
                P.mm(ps.b, ps[:, 0:ncol], memK[:, h // 2, mt * 128:(mt + 1) * 128], mqz[:, h % 2, h // 2, :], r=[C.qb])
                P.act(pt.b, pt[:, 0:ncol], ps[:, 0:ncol], AF.Exp, r=[ps.b], scale=0.125)
                P.mm(pU.b, pU[0:65, h * ncol:(h + 1) * ncol], memVv[:, mt, h, :], pt[:, 0:ncol], r=[pt.b], start=first, stop=(mt == 1 and h == 3))
                first = False
        w = 4 * ncol
        P.ts("dve", rd.b, rd[64:65, 0:w], pU[64:65, 0:w], 1e-20, None, ALU.max, None, r=[pU.b])
        cx.op("dve", lambda e: e.reciprocal(out=rd[64:65, 0:w], in_=rd[64:65, 0:w]), r=[rd.b], w=[rd.b])
        P.mm(pM.b, pM[0:64, 0:w], self.sel64[:, :], rd[:, 0:w], r=[self.sel64.b, rd.b])
        P.copy("act", Usb.b, Usb[:, 0:w], pU[0:64, 0:w], r=[pU.b])
        P.tt("dve", ocols[0], ocols[2], Usb[:, 0:w].rearrange("p (h t) -> p h t", h=4), pM[0:64, 0:w].rearrange("p (h t) -> p h t", h=4), ALU.mult, r=[Usb.b, pM.b])

    def epilogue(t, oT_, xin):
        for hf in range(2):
            for hh in range(16):
                P.mm(pM.b, pM[:, hf * 512:(hf + 1) * 512], oT_[:, hh, :], wout[:, hh, hf * 512:(hf + 1) * 512], r=[oT_.b, wout.b], start=(hh == 0), stop=(hh == 15))
        for hf in range(2):
            hs = slice(hf * 512, (hf + 1) * 512)
            P.stt(zt.b, zt[:, hs], xin[:, hs], ALPHA, pM[:, hs], ALU.mult, ALU.add, r=[xin.b, pM.b])
        P.ln_tile(zt, zt, lng, lnb, lnt)
        cx.dma("pool", self.x1[t * 128:(t + 1) * 128, :], zt[:, :], r=[zt.b], w=[self.x1b[t]])
        if self.debug:
            cx.dma("pool", self.dbg_x1[t * 128:(t + 1) * 128, :], zt[:, :], r=[zt.b])
        for kc in range(8):
            P.tr(pM.b, pM[:, kc * 128:(kc + 1) * 128], zt[:, kc * 128:(kc + 1) * 128], ident[:, :], r=[zt.b, ident.b])
        P.evac8(x1T_, pM)
        cx.dma("pool", self.x1T[:, :, t * 128:(t + 1) * 128].rearrange("k p t -> p k t"), x1T_[:, :, :], r=[x1T_.b], w=[self.x1b[t]])

    if unit == "p":
        win = P.sbp([128, 8, 1060], BF16)
        P.load_w_bf16(win, self.w_in_b, D, 1060, st)
        xin = [P.sbp([128, D])]
        xT = P.sbp([128, 8, 128], BF16)
        pr = P.sbp([128, 1060])
        cs, sn = P.sbp([128, 32]), P.sbp([128, 32])
        tm = [P.sbp([128, 12, 32]) for _ in range(4)]
        qb = P.sbp([128, 6, 2, 64])
        gates = P.sbp([128, 36])
        qz = P.sbp([128, 2, 6, 128], BF16)
        mqz = P.sbp([128, 2, 2, 128], BF16)
        gT = P.sbp([36, 128])
        P.memset("pool", qz.b, qz[:, :, :, :], 0.0)
        P.memset("pool", mqz.b, mqz[:, :, :, :], 0.0)
        P.memset("pool", C.qzS.b, C.qzS[:, :, :, :], 0.0)
        P.memset("pool", C.mqzS.b, C.mqzS[:, :, :, :], 0.0)
        for t in range(NT + 1):
            samp = (t == NT)
            xi = C.xinS if samp else xin[0]
            qz_, mqz_, gT_ = (C.qzS, C.mqzS, C.gTS) if samp else (qz, mqz, gT)
            cx.dma("sp", xi[:, :], self.x1[t * 128:(t + 1) * 128, :], r=[self.x1b[t]], w=[xi.b])
            cx.dma("sp", xT[:, :, :], self.x1T[:, :, t * 128:(t + 1) * 128].rearrange("k p t -> p k t"), r=[self.x1b[t]], w=[xT.b])
            cx.dma("sp", cs[:, :], self.c_cos[t * 128:(t + 1) * 128, :], w=[cs.b])
            cx.dma("sp", sn[:, :], self.c_sin[t * 128:(t + 1) * 128, :], w=[sn.b])
            for cb, (c0, c1) in enumerate(((0, 512), (512, 1024), (1024, 1060))):
                hb = pM[:, (cb % 2) * 512:(cb % 2) * 512 + (c1 - c0)]
                for kc in range(8):
                    P.mm(pM.b, hb, xT[:, kc, :], win[:, kc, c0:c1], r=[xT.b, win.b], start=(kc == 0), stop=(kc == 7))
                P.evac(pr.b, pr[:, c0:c1], hb, r=[pM.b])
            v4 = pr[:, 0:768].rearrange("p (h x i) -> p h x i", h=12, x=2)
            x1v, x2v = v4[:, :, 0, :], v4[:, :, 1, :]
            cb_ = cs[:, :].unsqueeze(1).broadcast_to([128, 12, 32])
            sb_ = sn[:, :].unsqueeze(1).broadcast_to([128, 12, 32])
            P.tt("dve", tm[0].b, tm[0][:, :, :], x1v, cb_, ALU.mult, r=[pr.b, cs.b])
            P.tt("pool", tm[1].b, tm[1][:, :, :], x2v, sb_, ALU.mult, r=[pr.b, sn.b])
            P.tt("dve", tm[2].b, tm[2][:, :, :], x1v, sb_, ALU.mult, r=[pr.b, sn.b])
            P.tt("pool", tm[3].b, tm[3][:, :, :], x2v, cb_, ALU.mult, r=[pr.b, cs.b])
            qv = qb[:, :, :, :].rearrange("p i k (x j) -> p k i x j", x=2)
            for kv in range(2):
                P.tt("dve", qb.b, qv[:, kv, :, 0, :], tm[0][:, kv * 6:(kv + 1) * 6, :], tm[1][:, kv * 6:(kv + 1) * 6, :], ALU.subtract, r=[tm[0].b, tm[1].b])
                P.tt("dve", qb.b, qv[:, kv, :, 1, :], tm[3][:, kv * 6:(kv + 1) * 6, :], tm[2][:, kv * 6:(kv + 1) * 6, :], ALU.add, r=[tm[2].b, tm[3].b])
            P.act(gates.b, gates[:, :], pr[:, 768:804], AF.Sigmoid, r=[pr.b])
            for b0 in (0, 3):
                for i in range(b0, b0 + 3):
                    P.tr(pM.b, pM[:, (i - b0) * 128:(i - b0 + 1) * 128], qb[:, i, :, :].rearrange("p k d -> p (k d)"), ident[:, :], r=[qb.b, ident.b])
                P.evac(qz_.b, qz_[0:64, 0, b0:b0 + 3, :], pM[0:64, 0:384].rearrange("p (k t) -> p k t", k=3), r=[pM.b])
                P.evac(qz_.b, qz_[64:128, 1, b0:b0 + 3, :], pM[64:128, 0:384].rearrange("p (k t) -> p k t", k=3), r=[pM.b])
            for pi in range(2):
                P.tr(pM.b, pM[:, 512 + pi * 128:512 + (pi + 1) * 128], pr[:, 804 + pi * 128:804 + (pi + 1) * 128], ident[:, :], r=[pr.b, ident.b])
            P.evac(mqz_.b, mqz_[0:64, 0, :, :], pM[0:64, 512:768].rearrange("p (k t) -> p k t", k=2), r=[pM.b])
            P.evac(mqz_.b, mqz_[64:128, 1, :, :], pM[64:128, 512:768].rearrange("p (k t) -> p k t", k=2), r=[pM.b])
            P.tr(pM.b, pM[0:36, 0:128], gates[:, :], ident[:, :], r=[gates.b, ident.b])
            P.copy("dve", gT_.b, gT_[:, :], pM[0:36, 0:128], r=[pM.b])
            if samp:
                break
            C.qb, C.gb = qz.b, gT.b
            ws = t % 5
            cx.dma("sp", C.wkT[:, ws * 128:(ws + 1) * 128], C.wkS[t], r=[C.wsb[t]], w=[C.wkT.b])
            cx.dma("sp", C.wvx[:, ws, :, :], C.wvS[t].rearrange("p (h d) -> p h d", h=2), r=[C.wsb[t]], w=[C.wvx.b])
            attend(t, 128, qz, mqz, gT[:, :], identb[:, 0:128], (oT.b, lambda kvh: oT[:, kvh * 6:(kvh + 1) * 6, :], oT[:, 12:16, :]), memKT, memV)
            epilogue(t, oT, xi)
    else:
        s = unit
        memKTs, memVs = memKT, memV
        for mt in range(2):
            cx.dma("sp", mks[:, :], self.cmem[1, s, mt * 128:(mt + 1) * 128].rearrange("m a h d -> m (a h d)"), w=[mks.b])
            P.mem_tile(mks, mt, memKTs, memVs, pM)
        C.qb, C.gb = C.qzS.b, C.gTS.b
        qzs = type("V", (), {"__getitem__": lambda _, k: C.qzS[k[0], k[1], k[2], s:s + 1]})()
        mqzs = type("V", (), {"__getitem__": lambda _, k: C.mqzS[k[0], k[1], k[2], s:s + 1]})()
        attend(NKT - 1, 1, qzs, mqzs, C.gTS[:, s:s + 1], identb[0:1, 0:1],
               (C.oTS.b, lambda kvh: C.oTS[:, kvh * 6:(kvh + 1) * 6, s:s + 1], C.oTS[:, 12:16, s:s + 1]), memKTs, memVs)
        if s == NS - 1:
            epilogue(NT, C.oTS, C.xinS)
    P.phase_end()


Model.l1_astage = l1_astage


def layer1(self):
    self.evac_dve = True
    self.marks = [("start", self.cx.ninstr)]
    self.phase_begin()
    C = self.l1_context()
    self.l1_kstage(C, "p")
    self.marks.append(("k_p", self.cx.ninstr))
    self.l1_astage(C, "p")
    self.marks.append(("a_p", self.cx.ninstr))
    for s in range(NS):
        self.l1_kstage(C, s)
        self.marks.append(("k_%d" % s, self.cx.ninstr))
        self.l1_astage(C, s)
        self.marks.append(("a_%d" % s, self.cx.ninstr))
    self.phase_end()
    self.evac_dve = False


Model.layer1 = layer1


def _core_inputs(inp, T, core, tabs):
    b = core % 2
    ss = slice(NS * core, NS * core + NS)
    d = {}
    d["xp"] = np.ascontiguousarray(inp["x_prompt"][b, :T])
    xs = np.zeros((128, D), np.float32)
    xs[:NS] = inp["x_sample"][ss, 0]
    d["xs"] = xs
    d["memp"] = np.ascontiguousarray(inp["mem_prompt"][b])
    for g in range(3):
        d["cdil%d" % g] = np.ascontiguousarray(inp["cache_dil_g%d" % g][0, ss])
    d["pool"] = inp["cache_nsa_kv"]
    d["cwin"] = np.ascontiguousarray(inp["cache_nsa_win"][ss])
    d["cmem"] = np.ascontiguousarray(inp["cache_mem_kv"][:, ss])
    d["ptab"] = np.ascontiguousarray(inp["page_table"][ss]).astype(np.int32)
    d.update(tabs)
    return d


def _shared_inputs(inp, T):
    d = {}
    d["w_in_a"] = np.ascontiguousarray(inp["w_in_a"][0])
    d["w_out_a"] = np.ascontiguousarray(inp["w_out_a"][0])
    d["w_in_b"] = np.ascontiguousarray(inp["w_in_b"][0])
    d["w_out_b"] = np.ascontiguousarray(inp["w_out_b"][0])
    d["w_mem_kv"] = np.ascontiguousarray(inp["w_mem_kv"])
    d["w_kv_b"] = np.ascontiguousarray(inp["w_kv_b"])
    d["cmp_pe"] = np.ascontiguousarray(inp["cmp_pe"].reshape(2, 16, 128).transpose(0, 2, 1))
    d["cmp_w1"] = np.ascontiguousarray(inp["cmp_w1"])
    w2 = inp["cmp_w2"]
    d["cmp_w2"] = np.ascontiguousarray(w2)
    d["cmp_w2r"] = np.ascontiguousarray(np.concatenate([w2[..., 32:], w2[..., :32]], -1))
    d["ln_g"] = np.ascontiguousarray(inp["ln_g"])
    d["ln_b"] = np.ascontiguousarray(inp["ln_b"])
    d["peer_wq"] = np.ascontiguousarray(inp["peer_wq"])
    d["peer_keysT"] = np.ascontiguousarray(inp["peer_keys"].transpose(0, 2, 4, 1, 3).reshape(2, 128, 8, 128))
    u = inp["peer_u"].reshape(2, 128, 128, 8, 128)
    d["peer_uT"] = np.ascontiguousarray(u.transpose(0, 1, 4, 3, 2))
    d["peer_v"] = np.ascontiguousarray(inp["peer_v"])
    c, s_ = rope_tables(T)
    d["c_cos"], d["c_sin"] = c, s_
    d["c_masks"] = host_masks()
    d["c_ident"] = np.eye(128, dtype=np.float32)
    d.update(host_l1_tables())
    return d


def build_model(T):
    m = Model(T)
    m.consts()
    m.layer0_attn()
    m.peer_p1(0)
    m.peer_p2(0, False)
    m.layer1()
    m.peer_p1(1)
    m.peer_p2(1, True)
    m.cx.finish()
    return m


def kernel(**inputs):
    inp = {k: np.asarray(v) for k, v in inputs.items()}
    T = inp["x_prompt"].shape[1]
    m = build_model(T)
    shared = _shared_inputs(inp, T)
    in_maps = [_core_inputs(inp, T, c, shared) for c in range(8)]
    res = run_bass_kernel_spmd(m.nc, in_maps, core_ids=list(range(8))).results
    B = inp["x_prompt"].shape[0]
    y_p = np.stack([res[b]["y_p"] for b in range(B)])
    y_s = np.concatenate([res[c]["y_s"][:NS] for c in range(8)])[:, None, :]
    outs = [y_p, y_s]
    for g in range(3):
        outs.append(np.stack([res[b]["dil_p%d" % g] for b in range(B)])[None])
    for g in range(3):
        outs.append(np.concatenate([res[c]["dil_s%d" % g] for c in range(8)])[None])
    outs.append(np.stack([res[b]["nsa_kv_p"] for b in range(B)]))
    outs.append(np.concatenate([res[c]["nsa_kv_s"][:NS] for c in range(8)])[:, None])
    outs.append(np.stack([res[b]["win_p"] for b in range(B)]))
    outs.append(np.concatenate([res[c]["win_s"] for c in range(8)]))
    outs.append(np.stack([res[b]["mem_p"] for b in range(B)], axis=1))
    return tuple(np.ascontiguousarray(o, dtype=np.float32) for o in outs)
```

```python
import numpy as np
import ml_dtypes
import concourse.bass as bass
import concourse.mybir as mybir
from concourse.bass_utils import run_bass_kernel_spmd

F32 = mybir.dt.float32
BF16 = mybir.dt.bfloat16
I32 = mybir.dt.int32
ALU = mybir.AluOpType
AF = mybir.ActivationFunctionType
AX = mybir.AxisListType

D = 1024
HD = 64
PAST = 8192
NS = 4
DEPTH = 2
ALPHA = (2 * DEPTH) ** 0.25
LN_EPS = 1e-5
NEGM = -30000.0
DIL = ((128, 1), (512, 4), (2048, 16))
GELU_C = 0.7978845608028654


class Buf:
    __slots__ = ("w", "r", "name")

    def __init__(self, name=""):
        self.w = None
        self.r = {}
        self.name = name


class Eng:
    def __init__(self, nc, name, eng):
        self.name = name
        self.eng = eng
        self.sem = nc.alloc_semaphore("sem_" + name)
        self.cnt = 0
        self.waited = {}


class Ctx:
    NDQ = 8

    def __init__(self, nc):
        self.nc = nc
        self.E = {
            "pe": Eng(nc, "pe", nc.tensor),
            "act": Eng(nc, "act", nc.scalar),
            "dve": Eng(nc, "dve", nc.vector),
            "pool": Eng(nc, "pool", nc.gpsimd),
            "sp": Eng(nc, "sp", nc.sync),
        }
        self.dq = {}
        for q in ("sp", "pool", "act"):
            self.dq[q] = {"sems": [[nc.alloc_semaphore("dq_%s_%d" % (q, i)), 0] for i in range(self.NDQ)], "next": 0}
        self.semid = {}
        self.ninstr = 0

    def _sid(self, sem):
        return id(sem)

    def _wait(self, E, tok):
        sem, val, _ = tok
        k = id(sem)
        if E.waited.get(k, 0) >= val:
            return
        E.eng.wait_ge(sem, val)
        E.waited[k] = val

    def _deps(self, E, reads, writes, is_dma=False):
        for b in reads:
            if b.w is not None:
                if not (b.w[2] == E.name and E.name == "pe"):
                    self._wait(E, b.w)
        for b in writes:
            if b.w is not None:
                if not (b.w[2] == E.name and E.name == "pe"):
                    self._wait(E, b.w)
            for tok in b.r.values():
                if tok[2] == E.name and not is_dma:
                    continue
                self._wait(E, tok)

    def _mark(self, tok, reads, writes):
        for b in reads:
            b.r[id(tok[0])] = tok
        for b in writes:
            b.w = tok
            b.r = {}

    limit = None

    def op(self, eng, fn, r=(), w=()):
        if self.limit is not None and self.ninstr >= self.limit:
            return
        E = self.E[eng]
        self._deps(E, r, w)
        ins = fn(E.eng)
        E.cnt += 1
        ins.then_inc(E.sem, 1)
        self._mark((E.sem, E.cnt, eng), r, w)
        self.ninstr += 1

    def dma(self, q, out, in_, r=(), w=(), **kw):
        if self.limit is not None and self.ninstr >= self.limit:
            return
        E = self.E[q]
        Q = self.dq[q]
        slot = Q["next"]
        Q["next"] = (slot + 1) % self.NDQ
        ent = Q["sems"][slot]
        if ent[1] > 0:
            self._wait(E, (ent[0], 16 * ent[1], "dma"))
        self._deps(E, r, w, is_dma=True)
        ins = E.eng.dma_start(out=out, in_=in_, **kw)
        ent[1] += 1
        ins.then_inc(ent[0], 16)
        self._mark((ent[0], 16 * ent[1], "dma"), r, w)
        self.ninstr += 1

    def finish(self):
        E = self.E["sp"]
        for q in self.dq.values():
            for sem, cnt in q["sems"]:
                if cnt > 0:
                    self._wait(E, (sem, 16 * cnt, "dma"))
        for e in self.E.values():
            if e.cnt > 0 and e.name != "sp":
                self._wait(E, (e.sem, e.cnt, e.name))


class TB:
    def __init__(self, t, name):
        self.t = t
        self.b = Buf(name)

    def __getitem__(self, k):
        return self.t[k]


class Prog:
    def __init__(self, cfg):
        self.cfg = cfg
        self.nc = bass.Bass("TRN2", target_bir_lowering=False)
        self.cx = Ctx(self.nc)
        self.ins = {}
        self.outs = {}
        self._n = 0

    def din(self, name, shape, dt=F32):
        t = self.nc.dram_tensor(name, list(shape), dt, kind="ExternalInput")
        self.ins[name] = t
        return t.ap()

    def dout(self, name, shape, dt=F32):
        t = self.nc.dram_tensor(name, list(shape), dt, kind="ExternalOutput")
        self.outs[name] = t
        return t.ap()

    def dscr(self, name, shape, dt=F32):
        return self.nc.dram_tensor(name, list(shape), dt).ap()

    def sb(self, shape, dt=F32, name=None):
        self._n += 1
        name = name or ("sb%d" % self._n)
        return TB(self.nc.alloc_sbuf_tensor(name, list(shape), dt), name)

    def ps(self, shape, dt=F32, name=None):
        self._n += 1
        name = name or ("ps%d" % self._n)
        return TB(self.nc.alloc_psum_tensor(name, list(shape), dt), name)

    def phase_begin(self):
        import contextlib
        if not hasattr(self, "stacks"):
            self.stacks = []
        self.stacks.append(contextlib.ExitStack())
        self.stack = self.stacks[-1]

    def phase_end(self):
        self.barrier()
        self.stacks.pop().close()
        self.stack = self.stacks[-1] if self.stacks else None

    def sbp(self, shape, dt=F32, name=None):
        self._n += 1
        name = name or ("sb%d" % self._n)
        t = self.stack.enter_context(self.nc.sbuf_tensor(name, list(shape), dt))
        return TB(t, name)

    def psp(self, shape, dt=F32, name=None):
        self._n += 1
        name = name or ("ps%d" % self._n)
        t = self.stack.enter_context(self.nc.psum_tensor(name, list(shape), dt))
        return TB(t, name)

    def barrier(self):
        cx = self.cx
        for E in cx.E.values():
            for q in cx.dq.values():
                for sem, cnt in q["sems"]:
                    if cnt > 0:
                        cx._wait(E, (sem, 16 * cnt, "dma"))
            for e in cx.E.values():
                if e.cnt > 0 and e is not E:
                    cx._wait(E, (e.sem, e.cnt, e.name))

    def mm(self, outb, out, lhsT, rhs, r, start=True, stop=True):
        self.cx.op("pe", lambda e: e.matmul(out, lhsT=lhsT, rhs=rhs, start=start, stop=stop), r=r, w=[outb])

    def tr(self, outb, out, in_, ident, r):
        self.cx.op("pe", lambda e: e.transpose(out, in_, ident), r=r, w=[outb])

    def copy(self, eng, outb, out, in_, r):
        if eng == "act":
            self.cx.op("act", lambda e: e.activation(out=out, in_=in_, func=AF.Copy), r=r, w=[outb])
        else:
            self.cx.op(eng, lambda e: e.tensor_copy(out=out, in_=in_), r=r, w=[outb])

    def tt(self, eng, outb, out, a, b, op, r):
        self.cx.op(eng, lambda e: e.tensor_tensor(out=out, in0=a, in1=b, op=op), r=r, w=[outb])

    def ts(self, eng, outb, out, a, s1, s2, op0, op1, r):
        if op1 is None:
            self.cx.op(eng, lambda e: e.tensor_scalar(out=out, in0=a, scalar1=s1, scalar2=None, op0=op0), r=r, w=[outb])
        else:
            self.cx.op(eng, lambda e: e.tensor_scalar(out=out, in0=a, scalar1=s1, scalar2=s2, op0=op0, op1=op1), r=r, w=[outb])

    def stt(self, outb, out, a, sc, b, op0, op1, r):
        self.cx.op("dve", lambda e: e.scalar_tensor_tensor(out=out, in0=a, scalar=sc, in1=b, op0=op0, op1=op1), r=r, w=[outb])

    def act(self, outb, out, in_, func, r, **kw):
        self.cx.op("act", lambda e: e.activation(out=out, in_=in_, func=func, **kw), r=r, w=[outb])

    def memset(self, eng, outb, out, val):
        shp = list(out.shape)
        if len(shp) > 2:
            names = " ".join("a%d" % i for i in range(len(shp) - 1))
            try:
                out = out.rearrange("p %s -> p (%s)" % (names, names))
            except Exception:
                self.cx.op("dve", lambda e: e.memset(out, val), r=(), w=[outb])
                return
        n = out.shape[1]
        for c0 in range(0, n, 2048):
            c1 = min(n, c0 + 2048)
            self.cx.op("dve", lambda e, c0=c0, c1=c1: e.memset(out[:, c0:c1], val), r=(), w=[outb])

    _rr = 0

    def evac(self, outb, out, in_, r):
        self._rr ^= 1
        self.copy("act" if (self._rr and not getattr(self, "evac_dve", False)) else "dve", outb, out, in_, r)

    def evac8(self, dst, src_ps, n=8, p0=0, p1=128):
        for b0 in range(0, n, 4):
            b1 = min(n, b0 + 4)
            self.evac(dst.b, dst[p0:p1, b0:b1, :], src_ps[p0:p1, b0 * 128:b1 * 128].rearrange("p (k t) -> p k t", k=b1 - b0), r=[src_ps.b])

    def load_w_bf16(self, dst, src, rows, cols, stg):
        kc_n = max(1, rows // 128)
        pr = min(rows, 128)
        sw = stg[0].t.shape[1]
        i = 0
        for kc in range(kc_n):
            for c0 in range(0, cols, sw):
                c1 = min(cols, c0 + sw)
                s = stg[i % len(stg)]
                i += 1
                self.cx.dma("sp", s[0:pr, 0:c1 - c0], src[kc * 128:kc * 128 + pr, c0:c1], r=(), w=[s.b])
                self.evac(dst.b, dst[0:pr, kc, c0:c1], s[0:pr, 0:c1 - c0], r=[s.b])

    def ln_tile(self, z, out, g, bt, tmp):
        st, mv, rs = tmp
        for hf in range(2):
            self.cx.op("dve", lambda e, hf=hf: e.bn_stats(out=st[:, hf, :], in_=z[:, hf * 512:(hf + 1) * 512]), r=[z.b], w=[st.b])
        self.cx.op("dve", lambda e: e.bn_aggr(out=mv[:, :], in_=st[:, :, :]), r=[st.b], w=[mv.b])
        self.ts("dve", rs.b, rs[:, :], mv[:, 1:2], LN_EPS, None, ALU.add, None, r=[mv.b])
        self.act(rs.b, rs[:, :], rs[:, :], AF.Ln, r=[rs.b])
        self.act(rs.b, rs[:, :], rs[:, :], AF.Exp, r=[rs.b], scale=-0.5)
        self.ts("dve", out.b, out[:, :], z[:, :], mv[:, 0:1], rs[:, 0:1], ALU.subtract, ALU.mult, r=[z.b, mv.b, rs.b])
        self.tt("pool", out.b, out[:, :], out[:, :], g[:, :], ALU.mult, r=[out.b, g.b])
        self.tt("pool", out.b, out[:, :], out[:, :], bt[:, :], ALU.add, r=[out.b, bt.b])


MASK_NAMES = ["g0d", "g0f", "g1d", "g1m", "g1f", "g2d", "g2m", "g2f", "eye"]


def host_masks():
    ki = np.arange(128)[:, None]
    qi = np.arange(128)[None, :]
    out = []
    for (w, dil) in DIL:
        dd = ((qi - ki) % dil) == 0
        out.append((qi >= ki) & dd)
        if dil > 1:
            out.append(dd)
        out.append((qi <= ki) & dd)
    out.append(qi == ki)
    m = np.stack(out)
    return np.where(m, 0.0, NEGM).astype(ml_dtypes.bfloat16)


def rope_tables(T):
    half = HD // 2
    inv = (10000.0 ** (-np.arange(half, dtype=np.float32) / half)).astype(np.float32)
    pos = np.concatenate([np.arange(T), np.full(128, PAST)]).astype(np.float32)
    ang = pos[:, None] * inv[None, :]
    return np.cos(ang).astype(np.float32), np.sin(ang).astype(np.float32)


class Model(Prog):
    def __init__(self, T, debug=False, small=()):
        super().__init__(None)
        self.small = set(small)
        self.T = T
        self.NT = T // 128
        self.TT = T + 128
        self.debug = debug
        self.declare()

    def declare(self):
        T, TT = self.T, self.TT
        d = self.din
        self.xp = d("xp", [T, D])
        self.xs = d("xs", [128, D])
        self.memp = d("memp", [256, D])
        self.cdil = [d("cdil%d" % g, [NS, DIL[g][0], 2, 4, HD]) for g in range(3)]
        self.pool = d("pool", [8 if "pool" in self.small else 2560, 128, 4, 2, HD])
        self.cwin = d("cwin", [NS, 512, 2, 2, HD])
        self.cmem = d("cmem", [2, NS, 256, 2, 4, HD])
        self.ptab = d("ptab", [NS, 64], I32)
        self.w_in_a = d("w_in_a", [D, 2560])
        self.w_out_a = d("w_out_a", [512, D])
        self.w_in_b = d("w_in_b", [D, 1060])
        self.w_out_b = d("w_out_b", [1024, D])
        self.w_mem_kv = d("w_mem_kv", [2, D, 512])
        self.w_kv_b = d("w_kv_b", [D, 768])
        self.cmp_pe = d("cmp_pe", [2, 128, 16])
        self.cmp_w1 = d("cmp_w1", [2, 2048, 128])
        self.cmp_w2 = d("cmp_w2", [2, 128, 64])
        self.cmp_w2r = d("cmp_w2r", [2, 128, 64])
        self.ln_g = d("ln_g", [2, 2, D])
        self.ln_b = d("ln_b", [2, 2, D])
        self.peer_wq = d("peer_wq", [2, D, 1024])
        self.peer_keysT = d("peer_keysT", [2, 128, 8, 128])
        ne = 2 if "peer" in self.small else 128
        self.peer_uT = d("peer_uT", [2, ne, 128, 8, 128])
        self.peer_v = d("peer_v", [2, ne * 128, D])
        self.c_cos = d("c_cos", [TT, 32])
        self.c_sin = d("c_sin", [TT, 32])
        self.c_masks = d("c_masks", [9, 128, 128], BF16)
        self.c_ident = d("c_ident", [128, 128])
        self.c_cosC = d("c_cosC", [128, 512])
        self.c_sinC = d("c_sinC", [128, 512])
        self.c_mrel = d("c_mrel", [128, 1024], BF16)
        self.c_mtc = d("c_mtc", [128, 2304], BF16)
        self.c_kp = d("c_kp", [128, 264])
        self.c_ad = d("c_ad", [128, 264])
        self.c_selg = d("c_selg", [36, 36, 64])
        self.c_riota = d("c_riota", [128, 1])
        o = self.dout
        self.y_p = o("y_p", [T, D])
        self.y_s = o("y_s", [128, D])
        self.dil_p = [o("dil_p%d" % g, [min(DIL[g][0], T), 2, 4, HD]) for g in range(3)]
        self.dil_s = [o("dil_s%d" % g, [NS, DIL[g][0], 2, 4, HD]) for g in range(3)]
        self.nsa_kv_p = o("nsa_kv_p", [T, 4, 2, HD])
        self.nsa_kv_s = o("nsa_kv_s", [128, 4, 2, HD])
        self.win_p = o("win_p", [min(512, T), 2, 2, HD])
        self.win_s = o("win_s", [NS, 512, 2, 2, HD])
        self.mem_p = o("mem_p", [2, 256, 2, 4, HD])
        self.x1 = self.dscr("x1", [TT, D])
        self.x1T = self.dscr("x1T", [8, 128, TT], BF16)
        self.x1b = [Buf("x1_%d" % t) for t in range(self.NT + 1)]
        if self.debug:
            self.dbg_x1 = o("dbg_x1", [TT, D])

    def consts(self):
        self.ident = self.sb([128, 128], F32, "ident")
        self.cx.dma("sp", self.ident[:, :], self.c_ident[:, :], w=[self.ident.b])
        self.identb = self.sb([128, 128], BF16, "identb")
        self.copy("dve", self.identb.b, self.identb[:, :], self.ident[:, :], r=[self.ident.b])
        self.masks = self.sb([128, 9, 128], BF16, "masks")
        self.cx.dma("sp", self.masks[:, :, :], self.c_masks.rearrange("m k q -> k m q"), w=[self.masks.b])
        self.sel64 = self.sb([128, 64], F32, "sel64")
        self.memset("dve", self.sel64.b, self.sel64[:, :], 0.0)
        self.memset("dve", self.sel64.b, self.sel64[64:65, :], 1.0)

    def bcast_row(self, dst, src_row):
        C = src_row.shape[-1]
        self.cx.dma("sp", dst[:, :], src_row.broadcast_to([128, C]), w=[dst.b])

    def mem_kv(self, layer, wst, memKT, memV, xin, pT, pP, xT):
        wm = self.sbp([128, 8, 512], BF16)
        self.load_w_bf16(wm, self.w_mem_kv[layer], D, 512, wst)
        for mt in range(2):
            self.cx.dma("sp", xin[:, :], self.memp[mt * 128:(mt + 1) * 128, :], w=[xin.b])
            for kc in range(8):
                self.tr(pT.b, pT[:, kc * 128:(kc + 1) * 128], xin[:, kc * 128:(kc + 1) * 128], self.ident[:, :], r=[xin.b, self.ident.b])
            self.evac8(xT, pT)
            for kc in range(8):
                self.mm(pP.b, pP[:, 0:512], xT[:, kc, :], wm[:, kc, :], r=[xT.b, wm.b], start=(kc == 0), stop=(kc == 7))
            mk = self.sbp([128, 512], F32)
            self.evac(mk.b, mk[:, :], pP[:, 0:512], r=[pP.b])
            self.cx.dma("pool", self.mem_p[layer, mt * 128:(mt + 1) * 128].rearrange("m a h d -> m (a h d)"), mk[:, :], r=[mk.b])
            self.mem_tile(mk, mt, memKT, memV, pT)

    def mem_tile(self, mk, mt, memKT, memV, pT, col=None):
        for pr in range(2):
            self.tr(pT.b, pT[:, pr * 128:(pr + 1) * 128], mk[:, pr * 128:(pr + 1) * 128], self.ident[:, :], r=[mk.b, self.ident.b])
        self.evac(memKT.b, memKT[:, :, mt * 128:(mt + 1) * 128], pT[:, 0:256].rearrange("p (k t) -> p k t", k=2), r=[pT.b])
        self.evac(memV.b, memV[:, mt, :, 0:64], mk[:, 256:512].rearrange("p (h d) -> p h d", h=4), r=[mk.b])

    def layer0_attn(self):
        P, cx, NT = self, self.cx, self.NT
        P.phase_begin()
        wst = [P.sbp([128, 1280], F32) for _ in range(2)]
        win = P.sbp([128, 8, 2560], BF16)
        P.load_w_bf16(win, self.w_in_a, D, 2560, wst)
        wout = P.sbp([64, 8, 1024], BF16)
        for hh in range(8):
            s = wst[hh % 2]
            cx.dma("sp", s[0:64, 0:1024], self.w_out_a[hh * 64:(hh + 1) * 64, :], w=[s.b])
            P.evac(wout.b, wout[:, hh, :], s[0:64, 0:1024], r=[s.b])
        lng, lnb = P.sbp([128, D]), P.sbp([128, D])
        P.bcast_row(lng, self.ln_g[0, 0:1, :])
        P.bcast_row(lnb, self.ln_b[0, 0:1, :])
        pU, pS, pP, pT = P.psp([128, 1024]), P.psp([128, 1024]), P.psp([128, 1024]), P.psp([128, 1024])
        Sb = [Buf("S%d" % i) for i in range(8)]
        NTg = [w // 128 + 1 for (w, _) in DIL]
        kt = [P.sbp([128, 2, NTg[g] * 128], BF16) for g in range(3)]
        vr = [P.sbp([128, NTg[g], 4, 65], BF16) for g in range(3)]
        for g in range(3):
            P.memset("pool", vr[g].b, vr[g][:, :, :, :], 1.0)
        memKT, memV = P.sbp([128, 2, 256], BF16), P.sbp([128, 2, 4, 65], BF16)
        memKTs, memVs = P.sbp([128, 2, 256], BF16), P.sbp([128, 2, 4, 65], BF16)
        P.memset("pool", memV.b, memV[:, :, :, :], 1.0)
        P.memset("pool", memVs.b, memVs[:, :, :, :], 1.0)
        KsT, Vs = P.sbp([128, 2, 128], BF16), P.sbp([128, 4, 65], BF16)
        ktS, vS = P.sbp([128, 3, 2, 128], BF16), P.sbp([128, 3, 4, 65], BF16)
        P.memset("pool", Vs.b, Vs[:, :, :], 1.0)
        P.memset("pool", vS.b, vS[:, :, :, :], 1.0)
        xin = [P.sbp([128, D]) for _ in range(2)]
        xT = P.sbp([128, 8, 128], BF16)
        pr = [P.sbp([128, 2560])]
        cs, sn = P.sbp([128, 32]), P.sbp([128, 32])
        qkr = [P.sbp([128, 3, 8, 64])]
        tm = [P.sbp([128, 3, 8, 32]) for _ in range(4)]
        qT = P.sbp([128, 2, 3, 2, 128], BF16)
        mqT = P.sbp([128, 2, 2, 128], BF16)
        P.memset("pool", qT.b, qT[:, :, :, :, :], 0.0)
        P.memset("pool", mqT.b, mqT[:, :, :, :], 0.0)
        PT = [P.sbp([128, 128], BF16) for _ in range(6)]
        rden = P.sbp([128, 1024])
        P.memset("pool", rden.b, rden[:, :], 0.0)
        Usb = P.sbp([64, 1024])
        oT = P.sbp([64, 8, 128], BF16)
        z = [P.sbp([128, D]) for _ in range(2)]
        x1Tt = [P.sbp([128, 8, 128], BF16) for _ in range(2)]
        lnt = (P.sbp([128, 2, 6]), P.sbp([128, 2]), P.sbp([128, 1]))
        kin, vin = P.sbp([128, 256]), P.sbp([128, 256])
        mks = P.sbp([128, 512])
        ident = self.ident

        P.mem_kv(0, wst, memKT, memV, xin[1], pT, pP, xT)

        for g in range(3):
            W = DIL[g][0]
            for s in range(NS):
                cx.dma("pool", self.dil_s[g][s, 0:W - 1].rearrange("r a h d -> r (a h d)"),
                       self.cdil[g][s, 1:W].rearrange("r a h d -> r (a h d)"))

        def load_x(t):
            src = self.xs[:, :] if t == NT else self.xp[t * 128:(t + 1) * 128, :]
            cx.dma("sp", xin[t % 2][:, :], src, w=[xin[t % 2].b])

        sctr = [0]
        pctr = [0]

        def attn_unit(kT_ap, q_ap, mask_idx, v_ap, u_ap, ncol, rd, flags):
            si = sctr[0] % 8
            sctr[0] += 1
            sb_, s_ap = Sb[si], pS[:, si * 128:si * 128 + ncol]
            P.mm(sb_, s_ap, kT_ap, q_ap, r=rd, start=True, stop=(mask_idx is None))
            if mask_idx is not None:
                P.mm(sb_, s_ap, self.identb[:, :], self.masks[:, mask_idx, 0:ncol], r=[self.identb.b, self.masks.b], start=False, stop=True)
            pt = PT[pctr[0] % len(PT)]
            pctr[0] += 1
            P.act(pt.b, pt[:, 0:ncol], s_ap, AF.Exp, r=[sb_], scale=0.125)
            P.mm(pU.b, u_ap, v_ap, pt[:, 0:ncol], r=[pt.b] + rd, start=flags[0], stop=flags[1])

        load_x(0)
        for t in range(NT + 1):
            samp = (t == NT)
            xi, prt, qk, zt, x1T_ = xin[t % 2], pr[0], qkr[0], z[t % 2], x1Tt[t % 2]
            if t + 1 <= NT:
                load_x(t + 1)
            cx.dma("sp", cs[:, :], self.c_cos[t * 128:(t + 1) * 128, :], w=[cs.b])
            cx.dma("sp", sn[:, :], self.c_sin[t * 128:(t + 1) * 128, :], w=[sn.b])
            for kc in range(8):
                P.tr(pT.b, pT[:, kc * 128:(kc + 1) * 128], xi[:, kc * 128:(kc + 1) * 128], ident[:, :], r=[xi.b, ident.b])
            P.evac8(xT, pT)
            for cb in range(5):
                hb = pP[:, (cb % 2) * 512:(cb % 2) * 512 + 512]
                for kc in range(8):
                    P.mm(pP.b, hb, xT[:, kc, :], win[:, kc, cb * 512:(cb + 1) * 512], r=[xT.b, win.b], start=(kc == 0), stop=(kc == 7))
                P.evac(prt.b, prt[:, cb * 512:(cb + 1) * 512], hb, r=[pP.b])
            v6 = prt[:, 0:2304].rearrange("p (g r h x i) -> p g r h x i", g=3, r=3, h=4, x=2)
            x1v = v6[:, :, 0:2, :, 0, :].rearrange("p g r h i -> p g (r h) i")
            x2v = v6[:, :, 0:2, :, 1, :].rearrange("p g r h i -> p g (r h) i")
            cb_ = cs[:, :].unsqueeze(1).unsqueeze(1).broadcast_to([128, 3, 8, 32])
            sb_ = sn[:, :].unsqueeze(1).unsqueeze(1).broadcast_to([128, 3, 8, 32])
            P.tt("dve", tm[0].b, tm[0][:, :, :, :], x1v, cb_, ALU.mult, r=[prt.b, cs.b])
            P.tt("pool", tm[1].b, tm[1][:, :, :, :], x2v, sb_, ALU.mult, r=[prt.b, sn.b])
            P.tt("dve", tm[2].b, tm[2][:, :, :, :], x1v, sb_, ALU.mult, r=[prt.b, sn.b])
            P.tt("pool", tm[3].b, tm[3][:, :, :, :], x2v, cb_, ALU.mult, r=[prt.b, cs.b])
            P.tt("dve", qk.b, qk[:, :, :, 0:32], tm[0][:, :, :, :], tm[1][:, :, :, :], ALU.subtract, r=[tm[0].b, tm[1].b])
            P.tt("dve", qk.b, qk[:, :, :, 32:64], tm[3][:, :, :, :], tm[2][:, :, :, :], ALU.add, r=[tm[2].b, tm[3].b])
            if not samp:
                for g in range(3):
                    nW = min(DIL[g][0], self.T) // 128
                    if t >= NT - nW:
                        r0 = 128 * (t - (NT - nW))
                        cx.dma("pool", self.dil_p[g][r0:r0 + 128, 0].rearrange("r h d -> r (h d)"), qk[:, g, 4:8, :].rearrange("p h d -> p (h d)"), r=[qk.b])
                        cx.dma("pool", self.dil_p[g][r0:r0 + 128, 1].rearrange("r h d -> r (h d)"), prt[:, g * 768 + 512:g * 768 + 768], r=[prt.b])
            else:
                for g in range(3):
                    W = DIL[g][0]
                    cx.dma("pool", self.dil_s[g][0:NS, W - 1, 0].rearrange("s h d -> s (h d)"), qk[0:NS, g, 4:8, :].rearrange("p h d -> p (h d)"), r=[qk.b])
                    cx.dma("pool", self.dil_s[g][0:NS, W - 1, 1].rearrange("s h d -> s (h d)"), prt[0:NS, g * 768 + 512:g * 768 + 768], r=[prt.b])
            for g in range(3):
                for pi in range(4):
                    P.tr(pT.b, pT[:, pi * 128:(pi + 1) * 128], qk[:, g, 2 * pi:2 * pi + 2, :].rearrange("p h d -> p (h d)"), ident[:, :], r=[qk.b, ident.b])
                P.evac(qT.b, qT[0:64, 0, g, :, :], pT[0:64, 0:256].rearrange("p (k t) -> p k t", k=2), r=[pT.b])
                P.evac(qT.b, qT[64:128, 1, g, :, :], pT[64:128, 0:256].rearrange("p (k t) -> p k t", k=2), r=[pT.b])
                if samp:
                    P.evac(ktS.b, ktS[:, g, :, :], pT[:, 256:512].rearrange("p (k t) -> p k t", k=2), r=[pT.b])
                    P.evac(vS.b, vS[:, g, :, 0:64], prt[:, g * 768 + 512:g * 768 + 768].rearrange("p (h d) -> p h d", h=4), r=[prt.b])
                else:
                    slot = t % NTg[g]
                    P.evac(kt[g].b, kt[g][:, :, slot * 128:(slot + 1) * 128], pT[:, 256:512].rearrange("p (k t) -> p k t", k=2), r=[pT.b])
                    P.evac(vr[g].b, vr[g][:, slot, :, 0:64], prt[:, g * 768 + 512:g * 768 + 768].rearrange("p (h d) -> p h d", h=4), r=[prt.b])
            for pi in range(2):
                P.tr(pT.b, pT[:, 512 + pi * 128:512 + (pi + 1) * 128], prt[:, 2304 + pi * 128:2304 + (pi + 1) * 128], ident[:, :], r=[prt.b, ident.b])
            P.evac(mqT.b, mqT[0:64, 0, :, :], pT[0:64, 512:768].rearrange("p (k t) -> p k t", k=2), r=[pT.b])
            P.evac(mqT.b, mqT[64:128, 1, :, :], pT[64:128, 512:768].rearrange("p (k t) -> p k t", k=2), r=[pT.b])

            unitsA, unitsB = [], []
            MI = [[0, None, 1], [2, 3, 4], [5, 6, 7]]
            if not samp:
                for g in range(3):
                    nb = DIL[g][0] // 128
                    for o in range(min(t, nb) + 1):
                        slot = (t - o) % NTg[g]
                        mi = MI[g][0] if o == 0 else (MI[g][2] if o == nb else MI[g][1])
                        for h in range(4):
                            unitsA.append((kt[g][:, h // 2, slot * 128:(slot + 1) * 128], qT[:, h % 2, g, h // 2, :], mi,
                                           vr[g][:, slot, h, :], pU[0:65, h * 128:(h + 1) * 128], 128, [kt[g].b, qT.b, vr[g].b]))
                for mt in range(2):
                    for h in range(4):
                        unitsB.append((memKT[:, h // 2, mt * 128:(mt + 1) * 128], mqT[:, h % 2, h // 2, :], None,
                                       memV[:, mt, h, :], pU[0:65, 512 + h * 128:512 + (h + 1) * 128], 128, [memKT.b, mqT.b, memV.b]))
                for i, u in enumerate(unitsA):
                    attn_unit(*u, flags=(i == 0, i == len(unitsA) - 1))
                for i, u in enumerate(unitsB):
                    attn_unit(*u, flags=(i == 0, i == len(unitsB) - 1))
            else:
                nA = 12 + NS * 12
                ia = 0
                for g in range(3):
                    for h in range(4):
                        attn_unit(ktS[:, g, h // 2, :], qT[:, h % 2, g, h // 2, :], 8, vS[:, g, h, :],
                                  pU[0:65, h * 128:(h + 1) * 128], 128, [ktS.b, qT.b, vS.b], flags=(ia == 0, False))
                        ia += 1
                for s in range(NS):
                    for g in range(3):
                        W, dil = DIL[g]
                        cx.dma("sp", kin[:, :], self.cdil[g][s, 0:W:dil, 0].rearrange("r h d -> r (h d)"), w=[kin.b])
                        cx.dma("sp", vin[:, :], self.cdil[g][s, 0:W:dil, 1].rearrange("r h d -> r (h d)"), w=[vin.b])
                        for pi in range(2):
                            P.tr(pT.b, pT[:, pi * 128:(pi + 1) * 128], kin[:, pi * 128:(pi + 1) * 128], ident[:, :], r=[kin.b, ident.b])
                        P.evac(KsT.b, KsT[:, :, :], pT[:, 0:256].rearrange("p (k t) -> p k t", k=2), r=[pT.b])
                        P.evac(Vs.b, Vs[:, :, 0:64], vin[:, :].rearrange("p (h d) -> p h d", h=4), r=[vin.b])
                        for h in range(4):
                            ia += 1
                            attn_unit(KsT[:, h // 2, :], qT[:, h % 2, g, h // 2, s:s + 1], None, Vs[:, h, :],
                                      pU[0:65, h * 128 + s:h * 128 + s + 1], 1, [KsT.b, qT.b, Vs.b], flags=(False, ia == nA))
                ib = 0
                for s in range(NS):
                    for mt in range(2):
                        cx.dma("sp", mks[:, :], self.cmem[0, s, mt * 128:(mt + 1) * 128].rearrange("m a h d -> m (a h d)"), w=[mks.b])
                        P.mem_tile(mks, mt, memKTs, memVs, pT)
                    for mt in range(2):
                        for h in range(4):
                            ib += 1
                            attn_unit(memKTs[:, h // 2, mt * 128:(mt + 1) * 128], mqT[:, h % 2, h // 2, s:s + 1], None,
                                      memVs[:, mt, h, :], pU[0:65, 512 + h * 128 + s:512 + h * 128 + s + 1], 1, [memKTs.b, mqT.b, memVs.b],
                                      flags=(ib == 1, ib == NS * 8))
            for hf in range(2):
                hs = slice(hf * 512, (hf + 1) * 512)
                P.ts("dve", rden.b, rden[64:65, hs], pU[64:65, hs], 1e-20, None, ALU.max, None, r=[pU.b])
                cx.op("dve", lambda e, hs=hs: e.reciprocal(out=rden[64:65, hs], in_=rden[64:65, hs]), r=[rden.b], w=[rden.b])
                P.mm(pT.b, pT[0:64, hs], self.sel64[:, :], rden[:, hs], r=[self.sel64.b, rden.b])
                P.copy("act", Usb.b, Usb[:, hs], pU[0:64, hs], r=[pU.b])
                P.tt("dve", oT.b, oT[:, 4 * hf:4 * hf + 4, :].rearrange("p h t -> p (h t)"), Usb[:, hs], pT[0:64, hs], ALU.mult, r=[Usb.b, pT.b])
            if samp:
                P.memset("dve", oT.b, oT[:, :, NS:128], 0.0)
            for hf in range(2):
                for hh in range(8):
                    P.mm(pP.b, pP[:, hf * 512:(hf + 1) * 512], oT[:, hh, :], wout[:, hh, hf * 512:(hf + 1) * 512], r=[oT.b, wout.b], start=(hh == 0), stop=(hh == 7))
            for hf in range(2):
                hs = slice(hf * 512, (hf + 1) * 512)
                P.stt(zt.b, zt[:, hs], xi[:, hs], ALPHA, pP[:, hs], ALU.mult, ALU.add, r=[xi.b, pP.b])
            P.ln_tile(zt, zt, lng, lnb, lnt)
            cx.dma("pool", self.x1[t * 128:(t + 1) * 128, :], zt[:, :], r=[zt.b], w=[self.x1b[t]])
            if self.debug:
                cx.dma("pool", self.dbg_x1[t * 128:(t + 1) * 128, :], zt[:, :], r=[zt.b])
            for kc in range(8):
                P.tr(pT.b, pT[:, kc * 128:(kc + 1) * 128], zt[:, kc * 128:(kc + 1) * 128], ident[:, :], r=[zt.b, ident.b])
            P.evac8(x1T_, pT)
            cx.dma("pool", self.x1T[:, :, t * 128:(t + 1) * 128].rearrange("k p t -> p k t"), x1T_[:, :, :], r=[x1T_.b], w=[self.x1b[t]])
        P.phase_end()

    def blocks(self):
        nt = self.NT + 1
        BT = min(13, nt)
        return [list(range(b0, min(nt, b0 + BT))) for b0 in range(0, nt, BT)], BT

    def peer_p1(self, L):
        P, cx, NT = self, self.cx, self.NT
        blks, BT = self.blocks()
        if not hasattr(self, "wgt"):
            self.wgt = [self.dscr("wgt%d" % b_, [128, 128, BT * 128], BF16) for b_ in range(len(blks))]
            self.wgtb = [Buf("wgt%d" % t) for t in range(NT + 1)]
        P.phase_begin()
        wq = P.sbp([128, 8, 1024], F32)
        for kc in range(8):
            cx.dma("sp", wq[:, kc, :], self.peer_wq[L, kc * 128:(kc + 1) * 128, :], w=[wq.b])
        kT = P.sbp([128, 8, 128], F32)
        cx.dma("sp", kT[:, :, :], self.peer_keysT[L], w=[kT.b])
        pQ, pS = P.psp([128, 1024]), P.psp([128, 2048])
        pW = [P.psp([128, 1024], BF16) for _ in range(2)]
        xin = [P.sbp([128, D]) for _ in range(2)]
        xT = [P.sbp([128, 8, 128], F32)]
        qpz = P.sbp([128, 2, 8, 128], F32)
        P.memset("pool", qpz.b, qpz[:, :, :, :], 0.0)
        ssb = P.sbp([128, 8, 2, 128])
        work = P.sbp([128, 8, 2, 128])
        tops = P.sbp([128, 8, 2, 16])
        cand = P.sbp([128, 8, 256])
        cw = P.sbp([128, 8, 256])
        v24 = P.sbp([128, 8, 24])
        thr, nthr, Z, lnZ, off, nlz = [P.sbp([128, 8]) for _ in range(6)]
        e16 = P.sbp([128, 16])
        bsb = P.sbp([128, 8, 128])
        pen = P.sbp([128, 8, 128])
        Tm = [P.sbp([128, 32, 128]) for _ in range(2)]
        Eb = [P.sbp([128, 4096], BF16) for _ in range(2)]
        Wh = [P.sbp([128, 4096], BF16) for _ in range(2)]
        Wacc = [P.sbp([128, 4096], BF16) for _ in range(2)]
        WT = [P.sbp([128, 32, 128], BF16) for _ in range(2)]

        def load(t):
            cx.dma("sp", xin[t % 2][:, :], self.x1[t * 128:(t + 1) * 128, :], r=[self.x1b[t]], w=[xin[t % 2].b])

        load(0)
        k = 0
        for t in range(NT + 1):
            if t + 1 <= NT:
                load(t + 1)
            x = xT[0]
            for kc in range(8):
                P.tr(pS.b, pS[:, kc * 128:(kc + 1) * 128], xin[t % 2][:, kc * 128:(kc + 1) * 128], self.ident[:, :], r=[xin[t % 2].b, self.ident.b])
            P.evac8(x, pS)
            for h in range(8):
                for kc in range(8):
                    P.mm(pQ.b, pQ[:, h * 128:(h + 1) * 128], wq[:, kc, h * 128:(h + 1) * 128], x[:, kc, :], r=[wq.b, x.b], start=(kc == 0), stop=(kc == 7))
            for b0 in (0, 4):
                P.evac(qpz.b, qpz[0:64, 0, b0:b0 + 4, :], pQ[0:64, b0 * 128:(b0 + 4) * 128].rearrange("p (k t) -> p k t", k=4), r=[pQ.b])
                P.evac(qpz.b, qpz[64:128, 1, b0:b0 + 4, :], pQ[64:128, b0 * 128:(b0 + 4) * 128].rearrange("p (k t) -> p k t", k=4), r=[pQ.b])
            for h in range(8):
                for c in range(2):
                    P.mm(pS.b, pS[:, (2 * h + c) * 128:(2 * h + c + 1) * 128], qpz[:, c, h, :], kT[:, h, :], r=[qpz.b, kT.b])
            for bk in range(4):
                P.evac(ssb.b, ssb[:, 2 * bk:2 * bk + 2, :, :].rearrange("p h c k -> p (h c k)"), pS[:, bk * 512:(bk + 1) * 512], r=[pS.b])
            for h in range(8):
                for c in range(2):
                    cx.op("dve", lambda e, h=h, c=c: e.max(out=tops[:, h, c, 0:8], in_=ssb[:, h, c, :]), r=[ssb.b], w=[tops.b])
                    cx.op("dve", lambda e, h=h, c=c: e.match_replace(out=work[:, h, c, :], in_to_replace=tops[:, h, c, 0:8], in_values=ssb[:, h, c, :], imm_value=-1e30), r=[ssb.b, tops.b], w=[work.b])
                    cx.op("dve", lambda e, h=h, c=c: e.max(out=tops[:, h, c, 8:16], in_=work[:, h, c, :]), r=[work.b], w=[tops.b])
            P.tt("dve", cand.b, cand[:, :, :].rearrange("p h (a b) -> p h a b", a=16),
                 tops[:, :, 0, :].unsqueeze(3).broadcast_to([128, 8, 16, 16]), tops[:, :, 1, :].unsqueeze(2).broadcast_to([128, 8, 16, 16]), ALU.add, r=[tops.b])
            for h in range(8):
                cx.op("dve", lambda e, h=h: e.max(out=v24[:, h, 0:8], in_=cand[:, h, :]), r=[cand.b], w=[v24.b])
                cx.op("dve", lambda e, h=h: e.match_replace(out=cw[:, h, :], in_to_replace=v24[:, h, 0:8], in_values=cand[:, h, :], imm_value=-1e30), r=[cand.b, v24.b], w=[cw.b])
                cx.op("dve", lambda e, h=h: e.max(out=v24[:, h, 8:16], in_=cw[:, h, :]), r=[cw.b], w=[v24.b])
                cx.op("dve", lambda e, h=h: e.match_replace(out=cw[:, h, :], in_to_replace=v24[:, h, 8:16], in_values=cw[:, h, :], imm_value=-1e30), r=[cw.b, v24.b], w=[cw.b])
                cx.op("dve", lambda e, h=h: e.max(out=v24[:, h, 16:24], in_=cw[:, h, :]), r=[cw.b], w=[v24.b])
            P.tt("dve", thr.b, thr[:, :], v24[:, :, 15], v24[:, :, 16], ALU.add, r=[v24.b])
            P.ts("dve", thr.b, thr[:, :], thr[:, :], 0.5, None, ALU.mult, None, r=[thr.b])
            P.ts("dve", nthr.b, nthr[:, :], thr[:, :], -1.0, None, ALU.mult, None, r=[thr.b])
            for h in range(8):
                P.act(e16.b, e16[:, :], v24[:, h, 0:16], AF.Exp, r=[v24.b, nthr.b], bias=nthr[:, h:h + 1], scale=1.0, accum_out=Z[:, h:h + 1])
            P.act(lnZ.b, lnZ[:, :], Z[:, :], AF.Ln, r=[Z.b, e16.b])
            P.tt("dve", off.b, off[:, :], thr[:, :], lnZ[:, :], ALU.add, r=[thr.b, lnZ.b])
            P.ts("dve", nlz.b, nlz[:, :], lnZ[:, :], -1.0, None, ALU.mult, None, r=[lnZ.b])
            P.tt("dve", bsb.b, bsb[:, :, :], ssb[:, :, 0, :], off[:, :].unsqueeze(2).broadcast_to([128, 8, 128]), ALU.subtract, r=[ssb.b, off.b])
            for c in range(2):
                P.tt("dve", pen.b, pen[:, :, :], ssb[:, :, c, :], tops[:, :, c, 15:16].broadcast_to([128, 8, 128]), ALU.is_ge, r=[ssb.b, tops.b])
                P.ts("dve", pen.b, pen[:, :, :], pen[:, :, :], 1.0, 1.0e30, ALU.subtract, ALU.mult, r=[pen.b])
                if c == 0:
                    P.tt("dve", bsb.b, bsb[:, :, :], bsb[:, :, :], pen[:, :, :], ALU.add, r=[bsb.b, pen.b])
                else:
                    P.tt("dve", ssb.b, ssb[:, :, 1, :], ssb[:, :, 1, :], pen[:, :, :], ALU.add, r=[ssb.b, pen.b])
            blk = [bi for bi, bl in enumerate(blks) if t in bl][0]
            pos = t - blks[blk][0]
            for ic in range(4):
                wa = Wacc[ic % 2]
                for h in range(8):
                    k += 1
                    tm_, eb_, wh_ = Tm[k % 2], Eb[k % 2], Wh[k % 2]
                    P.tt("dve", tm_.b, tm_[:, :, :], ssb[:, h, 1, :].unsqueeze(1).broadcast_to([128, 32, 128]),
                         bsb[:, h, ic * 32:(ic + 1) * 32].unsqueeze(2).broadcast_to([128, 32, 128]), ALU.add, r=[ssb.b, bsb.b])
                    tf = tm_[:, :, :].rearrange("p i j -> p (i j)")
                    P.act(eb_.b, eb_[:, :], tf, AF.Exp, r=[tm_.b])
                    if h == 0:
                        P.stt(wa.b, wa[:, :], tf, nlz[:, h:h + 1], eb_[:, :], ALU.is_ge, ALU.mult, r=[tm_.b, eb_.b, nlz.b])
                    else:
                        P.stt(wh_.b, wh_[:, :], tf, nlz[:, h:h + 1], eb_[:, :], ALU.is_ge, ALU.mult, r=[tm_.b, eb_.b, nlz.b])
                        P.tt("pool", wa.b, wa[:, :], wa[:, :], wh_[:, :], ALU.add, r=[wa.b, wh_.b])
                wt = WT[ic % 2]
                for i8 in range(4):
                    pw = pW[i8 % 2]
                    for j in range(8):
                        i = i8 * 8 + j
                        P.tr(pw.b, pw[:, j * 128:(j + 1) * 128], wa[:, i * 128:(i + 1) * 128], self.identb[:, :], r=[wa.b, self.identb.b])
                    P.evac(wt.b, wt[:, i8 * 8:(i8 + 1) * 8, :], pw[:, :].rearrange("p (k t) -> p k t", k=8), r=[pw.b])
                cx.dma("pool", self.wgt[blk][ic * 32:(ic + 1) * 32, :, pos * 128:(pos + 1) * 128].rearrange("i j t -> j i t"), wt[:, :, :], r=[wt.b], w=[self.wgtb[t]])
        P.phase_end()

    def peer_p2(self, L, final):
        P, cx, NT = self, self.cx, self.NT
        blks, BT = self.blocks()
        NCH = self.peer_uT.shape[1]
        G = min(8, NCH)
        P.phase_begin()
        lng, lnb = P.sbp([128, D]), P.sbp([128, D])
        P.bcast_row(lng, self.ln_g[L, 1:2, :])
        P.bcast_row(lnb, self.ln_b[L, 1:2, :])
        acc = P.sbp([128, BT, 1024])
        xTb = P.sbp([128, 8, BT * 128], BF16)
        A = P.sbp([128, G, BT * 128], BF16)
        vbf = P.sbp([128, G, 1024], BF16)
        ust = [P.sbp([128, 8, 128]) for _ in range(2)]
        ubf = [P.sbp([128, 8, 128], BF16) for _ in range(2)]
        vst = [P.sbp([128, 1024]) for _ in range(2)]
        actb = [P.sbp([128, BT * 128], BF16) for _ in range(2)]
        wT = [P.sbp([128, BT * 128], BF16) for _ in range(2)]
        lnt = (P.sbp([128, 2, 6]), P.sbp([128, 2]), P.sbp([128, 1]))
        xo = [P.sbp([128, 8, 128], BF16) for _ in range(2)]
        pH = P.psp([128, 2048])
        pO = P.psp([128, 2048])
        Ob = [Buf("po%d" % i) for i in range(4)]
        oc = 0
        for bi, tiles in enumerate(blks):
            nb = len(tiles)
            ntok = nb * 128
            t0 = tiles[0]
            for tt, t in enumerate(tiles):
                cx.dma("sp", acc[:, tt, :], self.x1[t * 128:(t + 1) * 128, :], r=[self.x1b[t]], w=[acc.b])
            cx.dma("sp", xTb[:, :, 0:ntok], self.x1T[:, :, t0 * 128:t0 * 128 + ntok].rearrange("k p t -> p k t"), r=[self.x1b[t] for t in tiles], w=[xTb.b])
            for tt in range(nb):
                P.act(acc.b, acc[:, tt, :], acc[:, tt, :], AF.Copy, r=[acc.b], scale=ALPHA)
            ncb = (ntok + 511) // 512
            cw_ = ntok // ncb
            assert cw_ * ncb == ntok and cw_ <= 512
            for g0 in range(0, NCH, G):
                for gi in range(G):
                    i = g0 + gi
                    us, ub, vs_, ab, wt = ust[i % 2], ubf[i % 2], vst[i % 2], actb[i % 2], wT[i % 2]
                    cx.dma("sp", us[:, :, :], self.peer_uT[L, i], w=[us.b])
                    cx.dma("sp", vs_[:, :], self.peer_v[L, i * 128:(i + 1) * 128, :], w=[vs_.b])
                    cx.dma("sp", wt[:, 0:ntok], self.wgt[bi][i, :, 0:ntok], r=[self.wgtb[t] for t in tiles], w=[wt.b])
                    P.copy("pool", ub.b, ub[:, :, :], us[:, :, :], r=[us.b])
                    P.copy("pool", vbf.b, vbf[:, gi, :], vs_[:, :], r=[vs_.b])
                    for cb in range(ncb):
                        for kc in range(8):
                            P.mm(pH.b, pH[:, cb * 512:cb * 512 + cw_], ub[:, kc, :], xTb[:, kc, cb * cw_:(cb + 1) * cw_], r=[ub.b, xTb.b], start=(kc == 0), stop=(kc == 7))
                    for cb in range(ncb):
                        P.act(ab.b, ab[:, cb * cw_:(cb + 1) * cw_], pH[:, cb * 512:cb * 512 + cw_], AF.Gelu_apprx_tanh, r=[pH.b])
                    P.tt("dve", A.b, A[:, gi, 0:ntok], ab[:, 0:ntok], wt[:, 0:ntok], ALU.mult, r=[ab.b, wt.b])
                for tt in range(nb):
                    for hf in range(2):
                        ob = Ob[oc % 4]
                        po = pO[:, (oc % 4) * 512:(oc % 4 + 1) * 512]
                        oc += 1
                        for gi in range(G):
                            P.mm(ob, po, A[:, gi, tt * 128:(tt + 1) * 128], vbf[:, gi, hf * 512:(hf + 1) * 512], r=[A.b, vbf.b], start=(gi == 0), stop=(gi == G - 1))
                        P.tt("dve", acc.b, acc[:, tt, hf * 512:(hf + 1) * 512], acc[:, tt, hf * 512:(hf + 1) * 512], po, ALU.add, r=[acc.b, ob])
            for tt, t in enumerate(tiles):
                class _V:
                    def __init__(s, ap, b):
                        s.ap, s.b = ap, b

                    def __getitem__(s, k):
                        return s.ap[k]
                zv = _V(acc[:, tt, :], acc.b)
                P.ln_tile(zv, zv, lng, lnb, lnt)
                if final:
                    dst = self.y_s[:, :] if t == NT else self.y_p[t * 128:(t + 1) * 128, :]
                    cx.dma("pool", dst, acc[:, tt, :], r=[acc.b])
                else:
                    cx.dma("pool", self.x1[t * 128:(t + 1) * 128, :], acc[:, tt, :], r=[acc.b], w=[self.x1b[t]])
                    x_o = xo[t % 2]
                    for kc in range(8):
                        P.tr(pH.b, pH[:, kc * 128:(kc + 1) * 128], acc[:, tt, kc * 128:(kc + 1) * 128], self.ident[:, :], r=[acc.b, self.ident.b])
                    P.evac8(x_o, pH)
                    cx.dma("pool", self.x1T[:, :, t * 128:(t + 1) * 128].rearrange("k p t -> p k t"), x_o[:, :, :], r=[x_o.b], w=[self.x1b[t]])
                if self.debug:
                    cx.dma("pool", self.dbg_x1[t * 128:(t + 1) * 128, :], acc[:, tt, :], r=[acc.b])
        P.phase_end()


def host_l1_tables():
    half = HD // 2
    inv = (10000.0 ** (-np.arange(half, dtype=np.float64) / half))
    cpos = np.arange(512) * 16 + 31
    d = np.arange(128) % 64
    ang = cpos[None, :] * inv[d % 32][:, None]
    cosC = np.cos(ang.astype(np.float32)).astype(np.float32)
    sn = np.sin(ang.astype(np.float32)).astype(np.float32)
    sinC = np.where((d < 32)[:, None], -sn, sn).astype(np.float32)
    qi = np.arange(128)[:, None]
    u = np.arange(1024)[None, :] - 512
    mrel = np.where(16 * u + 31 <= qi, 0.0, NEGM).astype(ml_dtypes.bfloat16)
    ci = np.arange(128)[:, None]
    x = np.arange(2304)[None, :]
    mtc = np.where(16 * ci + 31 <= x - 128, 0.0, NEGM).astype(ml_dtypes.bfloat16)
    bp = np.arange(264)[None, :] - 128
    cur = (qi >= 64).astype(np.int64)
    valid = bp <= cur
    forced = (bp == cur) | (bp == cur - 1)
    kp = (valid & ~forced).astype(np.float32)
    ad = np.where(~valid, -1.0e30, np.where(forced, 1.0e9, 0.0)).astype(np.float32)
    selg = np.zeros((36, 36, 64), np.float32)
    for r in range(36):
        selg[r, r, :] = 1.0
    return dict(c_cosC=cosC, c_sinC=sinC, c_mrel=mrel, c_mtc=mtc, c_kp=kp, c_ad=ad, c_selg=selg,
                c_riota=np.arange(128, dtype=np.float32).reshape(128, 1))


NKT = 65
NBK = 136


def l1_context(self):
    P = self
    C = type("C", (), {})()
    C.skT = P.sbp([128, NKT * 128], BF16)
    C.svx = P.sbp([128, NKT, 2, 65], BF16)
    C.ckT = P.sbp([128, 512], BF16)
    C.cvx = P.sbp([128, 4, 2, 65], BF16)
    C.wkT = P.sbp([128, 5 * 128], BF16)
    C.wvx = P.sbp([128, 5, 2, 65], BF16)
    for tb in (C.svx, C.cvx, C.wvx):
        P.memset("pool", tb.b, tb[:, :, :, :], 1.0)
    P.memset("pool", C.skT.b, C.skT[:, :], 0.0)
    C.wkS = P.dscr("wkS", [NKT, 128, 128], BF16)
    C.wvS = P.dscr("wvS", [NKT, 128, 130], BF16)
    C.wsb = [Buf("ws%d" % i) for i in range(NKT)]
    C.rowsS = P.sbp([128, 768])
    C.qzS = P.sbp([128, 2, 6, 128], BF16)
    C.mqzS = P.sbp([128, 2, 2, 128], BF16)
    C.gTS = P.sbp([36, 128])
    C.oTS = P.sbp([64, 16, 128], BF16)
    C.xinS = P.sbp([128, D])
    P.memset("pool", C.oTS.b, C.oTS[:, :, :], 0.0)
    return C


def l1_kstage(self, C, unit):
    P, cx, NT, T = self, self.cx, self.NT, self.T
    P.phase_begin()
    ident = self.ident
    C.ckraw = P.sbp([128, NKT * 128], BF16)
    C.cvraw = P.sbp([128, NKT * 128], BF16)
    for tb in (C.ckraw, C.cvraw):
        P.memset("pool", tb.b, tb[:, :], 0.0)
    pT, pP, pH, pK = P.psp([128, 1024]), P.psp([128, 1024]), P.psp([128, 512]), P.psp([128, 1024])
    st = [P.sbp([128, 1024]) for _ in range(2)]
    w1z = P.sbp([128, 2, 2, 32, 128], BF16)
    P.memset("pool", w1z.b, w1z[:, :, :, :, :].rearrange("p a b l k -> p (a b l k)"), 0.0)
    for kind in range(2):
        for h in range(2):
            for l0 in range(0, 32, 8):
                s_ = st[(l0 // 8) % 2]
                cx.dma("sp", s_[h * 64:(h + 1) * 64, :].rearrange("p (l k) -> p l k", l=8),
                       self.cmp_w1[kind, l0 * 64:(l0 + 8) * 64, :].rearrange("(l d) k -> d l k", d=64), w=[s_.b])
                P.evac(w1z.b, w1z[h * 64:(h + 1) * 64, kind, h, l0:l0 + 8, :], s_[h * 64:(h + 1) * 64, :].rearrange("p (l k) -> p l k", l=8), r=[s_.b])
    w1n = P.sbp([128, 2, 16, 128], BF16)
    pe = P.sbp([128, 2, 16], BF16)
    pst = P.sbp([128, 16])
    for kind in range(2):
        for c0 in (0, 8):
            s_ = st[(c0 // 8) % 2]
            cx.dma("sp", s_[:, :].rearrange("p (c k) -> p c k", c=8), self.cmp_w1[kind, c0 * 128:(c0 + 8) * 128, :].rearrange("(c p) k -> p c k", p=128), w=[s_.b])
            P.evac(w1n.b, w1n[:, kind, c0:c0 + 8, :], s_[:, :].rearrange("p (c k) -> p c k", c=8), r=[s_.b])
        cx.dma("sp", pst[:, :], self.cmp_pe[kind], w=[pst.b])
        P.evac(pe.b, pe[:, kind, :], pst[:, :], r=[pst.b])
    w2z = P.sbp([128, 2, 2, 128], BF16)
    P.memset("pool", w2z.b, w2z[:, :, :, :], 0.0)
    w2v = P.sbp([128, 64], BF16)
    w2s = P.sbp([128, 64])
    for ri, src in enumerate((self.cmp_w2[0], self.cmp_w2r[0])):
        cx.dma("sp", w2s[:, :], src, w=[w2s.b])
        for h in range(2):
            P.evac(w2z.b, w2z[:, ri, h, h * 64:(h + 1) * 64], w2s[:, :], r=[w2s.b])
    cx.dma("sp", w2s[:, :], self.cmp_w2[1], w=[w2s.b])
    P.evac(w2v.b, w2v[:, :], w2s[:, :], r=[w2s.b])
    cosC, sinC = P.sbp([128, 512]), P.sbp([128, 512])
    cx.dma("sp", cosC[:, :], self.c_cosC[:, :], w=[cosC.b])
    cx.dma("sp", sinC[:, :], self.c_sinC[:, :], w=[sinC.b])
    rows = [P.sbp([128, 768]) for _ in range(2)]
    self.marks.append(("kw", cx.ninstr))

    def store_tile(t, rw, win):
        kinds = [0, 1, 2] + ([4] if win else [])
        for j, kd in enumerate(kinds):
            P.tr(pT.b, pT[:, j * 128:(j + 1) * 128], rw[:, kd * 128:(kd + 1) * 128], ident[:, :], r=[rw.b, ident.b])
        sl = slice(t * 128, (t + 1) * 128)
        P.copy("dve", C.ckraw.b, C.ckraw[:, sl], pT[:, 0:128], r=[pT.b])
        P.copy("dve", C.cvraw.b, C.cvraw[:, sl], pT[:, 128:256], r=[pT.b])
        P.copy("dve", C.skT.b, C.skT[:, sl], pT[:, 256:384], r=[pT.b])
        P.evac(C.svx.b, C.svx[:, t, :, 0:64], rw[:, 384:512].rearrange("p (h d) -> p h d", h=2), r=[rw.b])
        if win:
            ws = t % 5
            P.evac(C.wkT.b, C.wkT[:, ws * 128:(ws + 1) * 128], pT[:, 384:512], r=[pT.b])
            P.evac(C.wvx.b, C.wvx[:, ws, :, 0:64], rw[:, 640:768].rearrange("p (h d) -> p h d", h=2), r=[rw.b])
            if unit == "p":
                cx.dma("pool", C.wkS[t], C.wkT[:, ws * 128:(ws + 1) * 128], r=[C.wkT.b], w=[C.wsb[t]])
                cx.dma("pool", C.wvS[t].rearrange("p (h d) -> p h d", h=2), C.wvx[:, ws, :, :], r=[C.wvx.b], w=[C.wsb[t]])

    if unit == "p":
        wkv = P.sbp([128, 8, 768], BF16)
        P.load_w_bf16(wkv, self.w_kv_b, D, 768, st)
        xT = [P.sbp([128, 8, 128], BF16) for _ in range(2)]
        cs, sn = P.sbp([128, 32]), P.sbp([128, 32])
        tm = [P.sbp([128, 2, 2, 32]) for _ in range(4)]
        for t in range(NT + 1):
            x, rw = xT[t % 2], rows[t % 2]
            if t == NT:
                rw = C.rowsS
            cx.dma("sp", x[:, :, :], self.x1T[:, :, t * 128:(t + 1) * 128].rearrange("k p t -> p k t"), r=[self.x1b[t]], w=[x.b])
            cx.dma("sp", cs[:, :], self.c_cos[t * 128:(t + 1) * 128, :], w=[cs.b])
            cx.dma("sp", sn[:, :], self.c_sin[t * 128:(t + 1) * 128, :], w=[sn.b])
            for cb, (c0, c1) in enumerate(((0, 512), (512, 768))):
                hb = pP[:, cb * 512:cb * 512 + (c1 - c0)]
                for kc in range(8):
                    P.mm(pP.b, hb, x[:, kc, :], wkv[:, kc, c0:c1], r=[x.b, wkv.b], start=(kc == 0), stop=(kc == 7))
                P.evac(rw.b, rw[:, c0:c1], hb, r=[pP.b])
            v5 = rw[:, :].rearrange("p (k h x i) -> p k h x i", k=6, h=2, x=2)
            x1v, x2v = v5[:, 2:6:2, :, 0, :], v5[:, 2:6:2, :, 1, :]
            cb_ = cs[:, :].unsqueeze(1).unsqueeze(1).broadcast_to([128, 2, 2, 32])
            sb_ = sn[:, :].unsqueeze(1).unsqueeze(1).broadcast_to([128, 2, 2, 32])
            P.tt("dve", tm[0].b, tm[0][:, :, :, :], x1v, cb_, ALU.mult, r=[rw.b, cs.b])
            P.tt("pool", tm[1].b, tm[1][:, :, :, :], x2v, sb_, ALU.mult, r=[rw.b, sn.b])
            P.tt("dve", tm[2].b, tm[2][:, :, :, :], x1v, sb_, ALU.mult, r=[rw.b, sn.b])
            P.tt("pool", tm[3].b, tm[3][:, :, :, :], x2v, cb_, ALU.mult, r=[rw.b, cs.b])
            P.tt("dve", rw.b, x1v, tm[0][:, :, :, :], tm[1][:, :, :, :], ALU.subtract, r=[tm[0].b, tm[1].b])
            P.tt("dve", rw.b, x2v, tm[3][:, :, :, :], tm[2][:, :, :, :], ALU.add, r=[tm[2].b, tm[3].b])
            if t < NT:
                cx.dma("pool", self.nsa_kv_p[t * 128:(t + 1) * 128].rearrange("t k h d -> t (k h d)"), rw[:, 0:512], r=[rw.b])
                nW = min(512, T) // 128
                if t >= NT - nW:
                    r0 = 128 * (t - (NT - nW))
                    cx.dma("pool", self.win_p[r0:r0 + 128].rearrange("t k h d -> t (k h d)"), rw[:, 512:768], r=[rw.b])
                store_tile(t, rw, True)
            else:
                cx.dma("pool", self.nsa_kv_s[:, :].rearrange("t k h d -> t (k h d)"), rw[:, 0:512], r=[rw.b], w=[C.rowsS.b])
                for s in range(NS):
                    cx.dma("pool", self.win_s[s, 0:511].rearrange("t k h d -> t (k h d)"), self.cwin[s, 1:512].rearrange("t k h d -> t (k h d)"))
                cx.dma("pool", self.win_s[0:NS, 511].rearrange("s k h d -> s (k h d)"), rw[0:NS, 512:768], r=[rw.b])
    else:
        s = unit
        pti = P.sbp([128, 64], I32)
        ptf = P.sbp([128, 64])
        idx = P.sbp([128, 64], mybir.dt.uint32)
        rio = P.sbp([128, 1])
        cx.dma("sp", pti[:, :], self.ptab[s:s + 1, :].broadcast_to([128, 64]), w=[pti.b])
        cx.dma("sp", rio[:, :], self.c_riota[:, :], w=[rio.b])
        P.copy("dve", ptf.b, ptf[:, :], pti[:, :], r=[pti.b])
        P.ts("dve", ptf.b, ptf[:, :], ptf[:, :], 128.0, rio[:, 0:1], ALU.mult, ALU.add, r=[ptf.b, rio.b])
        P.copy("dve", idx.b, idx[:, :], ptf[:, :], r=[ptf.b])
        poolrows = self.pool.rearrange("g r k h d -> (g r) (k h d)")
        for t in range(NKT):
            rw = rows[t % 2]
            win = t >= NKT - 5
            if t < NKT - 1 and (cx.limit is not None and cx.ninstr >= cx.limit):
                pass
            elif t < NKT - 1:
                E = cx.E["pool"]
                cx._deps(E, [idx.b], [rw.b], is_dma=True)
                Q = cx.dq["pool"]
                slot = Q["next"]
                Q["next"] = (slot + 1) % cx.NDQ
                ent = Q["sems"][slot]
                if ent[1] > 0:
                    cx._wait(E, (ent[0], 16 * ent[1], "dma"))
                ins = self.nc.gpsimd.indirect_dma_start(out=rw[:, 0:512], out_offset=None, in_=poolrows,
                                                        in_offset=bass.IndirectOffsetOnAxis(ap=idx[:, t:t + 1], axis=0))
                ent[1] += 1
                ins.then_inc(ent[0], 16)
                cx._mark((ent[0], 16 * ent[1], "dma"), [idx.b], [rw.b])
                cx.ninstr += 1
                if win:
                    cx.dma("sp", rw[:, 512:768], self.cwin[s, (t - (NKT - 5)) * 128:(t - (NKT - 5) + 1) * 128].rearrange("t k h d -> t (k h d)"), w=[rw.b])
            else:
                P.memset("dve", rw.b, rw[:, :], 0.0)
                cx.dma("sp", rw[0:1, :], C.rowsS[s:s + 1, :], r=[C.rowsS.b], w=[rw.b])
            store_tile(t, rw, win)
    self.marks.append(("kt", cx.ninstr))
    bias = P.sbp([128, 1])
    Hs = [P.sbp([128, 512], BF16) for _ in range(2)]
    ck, ckr = P.sbp([128, 512]), P.sbp([128, 512])
    for kind in range(2):
        raw = C.ckraw if kind == 0 else C.cvraw
        for ch in range(16):
            P.mm(pK.b, pK[:, 0:1], w1n[:, kind, ch, :], pe[:, kind, ch:ch + 1], r=[w1n.b, pe.b], start=(ch == 0), stop=(ch == 15))
        P.copy("dve", bias.b, bias[:, :], pK[:, 0:1], r=[pK.b])
        for h in range(2):
            for l in range(32):
                P.mm(pH.b, pH[:, :], w1z[:, kind, h, l, :], raw[:, l:l + 16 * 511 + 1:16], r=[w1z.b, raw.b], start=(l == 0), stop=(l == 31))
            P.act(Hs[h].b, Hs[h][:, :], pH[:, :], AF.Gelu_apprx_tanh, r=[pH.b, bias.b], bias=bias[:, 0:1], scale=1.0)
        if kind == 0:
            for ri in range(2):
                for h in range(2):
                    P.mm(pK.b, pK[:, ri * 512:(ri + 1) * 512], w2z[:, ri, h, :], Hs[h][:, :], r=[w2z.b, Hs[h].b], start=(h == 0), stop=(h == 1))
            P.tt("dve", ck.b, ck[:, :], pK[:, 0:512], cosC[:, :], ALU.mult, r=[pK.b, cosC.b])
            P.tt("dve", ckr.b, ckr[:, :], pK[:, 512:1024], sinC[:, :], ALU.mult, r=[pK.b, sinC.b])
            P.tt("dve", C.ckT.b, C.ckT[:, :], ck[:, :], ckr[:, :], ALU.add, r=[ck.b, ckr.b])
        else:
            for h in range(2):
                for ct in range(4):
                    j = h * 4 + ct
                    P.mm(pK.b, pK[:, j * 64:(j + 1) * 64], Hs[h][:, ct * 128:(ct + 1) * 128], w2v[:, :], r=[Hs[h].b, w2v.b])
            for h in range(2):
                P.evac(C.cvx.b, C.cvx[:, :, h, 0:64], pK[:, h * 256:(h + 1) * 256].rearrange("p (c d) -> p c d", c=4), r=[pK.b])
    P.phase_end()


Model.l1_context = l1_context
Model.l1_kstage = l1_kstage


def l1_astage(self, C, unit):
    P, cx, NT, T = self, self.cx, self.NT, self.T
    P.phase_begin()
    ident, identb, masks = self.ident, self.identb, self.masks
    st = [P.sbp([128, 1024]) for _ in range(2)]
    pU, pM = P.psp([128, 1024]), P.psp([128, 1024])
    pS = [P.psp([128, 1024]) for _ in range(2)]
    memKT, memV = P.sbp([128, 2, 256], BF16), P.sbp([128, 2, 4, 65], BF16)
    P.memset("pool", memV.b, memV[:, :, :, :], 1.0)
    if unit == "p":
        P.phase_begin()
        tmpx, tmpT = P.sbp([128, D]), P.sbp([128, 8, 128], BF16)
        P.mem_kv(1, st, memKT, memV, tmpx, pS[0], pM, tmpT)
        P.phase_end()
    wout = P.sbp([64, 16, 1024], BF16)
    for hh in range(16):
        s_ = st[hh % 2]
        cx.dma("sp", s_[0:64, 0:1024], self.w_out_b[hh * 64:(hh + 1) * 64, :], w=[s_.b])
        P.evac(wout.b, wout[:, hh, :], s_[0:64, 0:1024], r=[s_.b])
    lng, lnb = P.sbp([128, D]), P.sbp([128, D])
    P.bcast_row(lng, self.ln_g[1, 0:1, :])
    P.bcast_row(lnb, self.ln_b[1, 0:1, :])
    mrel, mtc = P.sbp([128, 1024], BF16), P.sbp([128, 2304], BF16)
    kp, ad = P.sbp([128, 264]), P.sbp([128, 264])
    selg = P.sbp([36, 36, 64])
    for dst, src in ((mrel, self.c_mrel), (mtc, self.c_mtc), (kp, self.c_kp), (ad, self.c_ad)):
        cx.dma("sp", dst[:, :], src[:, :], w=[dst.b])
    cx.dma("sp", selg[:, :, :], self.c_selg[:, :, :], w=[selg.b])
    PT = [P.sbp([128, 768], BF16) for _ in range(3)]
    Pc = P.sbp([128, 512])
    den, rdn, thr = P.sbp([128, 1]), P.sbp([128, 1]), P.sbp([128, 1])
    impp = P.sbp([128, 520])
    P.memset("pool", impp.b, impp[:, :], 0.0)
    rA, rB = P.sbp([128, 128]), P.sbp([128, 128])
    score, work = P.sbp([128, NBK]), P.sbp([128, NBK])
    P.memset("pool", score.b, score[:, :], 0.0)
    m16 = P.sbp([128, 16])
    selb = P.sbp([128, NBK], BF16)
    selE = [P.sbp([128, 128], BF16) for _ in range(2)]
    rd = P.sbp([128, 768])
    P.memset("pool", rd.b, rd[:, :], 0.0)
    Usb, t1, t2 = P.sbp([64, 768]), P.sbp([64, 768]), P.sbp([64, 768])
    oacc = P.sbp([64, 2, 768])
    oT = P.sbp([64, 16, 128], BF16)
    zt = P.sbp([128, D])
    x1T_ = P.sbp([128, 8, 128], BF16)
    lnt = (P.sbp([128, 2, 6]), P.sbp([128, 2]), P.sbp([128, 1]))
    mks = P.sbp([128, 512])
    sct, pct = [0], [0]

    def attend(tq, ncol, qz, mqz, gT, qsel, ocols, memK, memVv):
        np_ = ncol
        halves = [(0, 3), (3, 6)] if ncol == 128 else [(0, 6)]
        N = 6 * ncol

        def bc(ap2, nh):
            return ap2.unsqueeze(1).broadcast_to([ap2.shape[0], nh, ncol])

        def s_unit(k_ap, kvh, extra, v_ap, first, last, rd_):
            ps = pS[sct[0] % 2]
            sct[0] += 1
            pt = PT[pct[0] % 3]
            pct[0] += 1
            for hi, (i0, i1) in enumerate(halves):
                nh = i1 - i0
                out = ps[:, hi * 512:hi * 512 + nh * ncol]
                P.mm(ps.b, out, k_ap, qz[:, kvh, i0:i1, :], r=rd_ + [C.qb], start=True, stop=(len(extra) == 0))
                for ei, (l_ap, r_fn, rb) in enumerate(extra):
                    P.mm(ps.b, out, l_ap, r_fn(nh), r=rb, start=False, stop=(ei == len(extra) - 1))
                P.act(pt.b, pt[:, i0 * ncol:i1 * ncol], out, AF.Exp, r=[ps.b], scale=0.125)
            for hi, (i0, i1) in enumerate(halves):
                nh = i1 - i0
                P.mm(pU.b, pU[0:65, hi * 512:hi * 512 + nh * ncol], v_ap, pt[:, i0 * ncol:i1 * ncol], r=[pt.b] + rd_,
                     start=(first and hi == 0) or (first and hi == 1), stop=last)

        def finish_branch(kvh, r, firstb):
            for hi, (i0, i1) in enumerate(halves):
                nh = i1 - i0
                w = nh * ncol
                us = slice(hi * 512, hi * 512 + w)
                fs = slice(i0 * ncol, i1 * ncol)
                P.ts("dve", rd.b, rd[64:65, fs], pU[64:65, us], 1e-20, None, ALU.max, None, r=[pU.b])
                cx.op("dve", lambda e, fs=fs: e.reciprocal(out=rd[64:65, fs], in_=rd[64:65, fs]), r=[rd.b], w=[rd.b])
                P.mm(pM.b, pM[0:64, 0:w], self.sel64[:, :], rd[:, fs], r=[self.sel64.b, rd.b])
                P.copy("act", Usb.b, Usb[:, fs], pU[0:64, us], r=[pU.b])
                P.tt("dve", t1.b, t1[:, fs], Usb[:, fs], pM[0:64, 0:w], ALU.mult, r=[Usb.b, pM.b])
                for i in range(i0, i1):
                    row = (kvh * 6 + i) * 3 + r
                    P.mm(pM.b, pM[0:64, 512 + (i - i0) * ncol:512 + (i - i0 + 1) * ncol], selg[:, row, :], gT, r=[selg.b, C.gb])
                if firstb:
                    P.tt("dve", oacc.b, oacc[:, kvh, fs], t1[:, fs], pM[0:64, 512:512 + w], ALU.mult, r=[t1.b, pM.b])
                else:
                    P.tt("dve", t2.b, t2[:, fs], t1[:, fs], pM[0:64, 512:512 + w], ALU.mult, r=[t1.b, pM.b])
                    P.tt("dve", oacc.b, oacc[:, kvh, fs], oacc[:, kvh, fs], t2[:, fs], ALU.add, r=[oacc.b, t2.b])

        s0 = 128 * tq
        for kvh in range(2):
            for i in range(6):
                P.mm(pM.b, pM[0:np_, 0:512], qz[:, kvh, i, :], C.ckT[:, :], r=[C.qb, C.ckT.b], start=True, stop=False)
                P.mm(pM.b, pM[0:np_, 0:512], identb[0:np_, 0:np_], mrel[0:np_, 512 - 8 * tq:1024 - 8 * tq], r=[identb.b, mrel.b], start=False, stop=True)
                P.act(Pc.b, Pc[0:np_, :], pM[0:np_, 0:512], AF.Exp, r=[pM.b], scale=0.125, accum_out=den[0:np_, :])
                P.ts("dve", rdn.b, rdn[0:np_, :], den[0:np_, :], 1e-20, None, ALU.max, None, r=[den.b, Pc.b])
                cx.op("dve", lambda e: e.reciprocal(out=rdn[0:np_, :], in_=rdn[0:np_, :]), r=[rdn.b], w=[rdn.b])
                if i == 0:
                    P.ts("dve", impp.b, impp[0:np_, 1:513], Pc[0:np_, :], rdn[0:np_, 0:1], None, ALU.mult, None, r=[Pc.b, rdn.b])
                else:
                    P.stt(impp.b, impp[0:np_, 1:513], Pc[0:np_, :], rdn[0:np_, 0:1], impp[0:np_, 1:513], ALU.mult, ALU.add, r=[Pc.b, rdn.b, impp.b])
            cx.op("dve", lambda e: e.tensor_reduce(out=rA[0:np_, :], in_=impp[0:np_, 1:513].rearrange("p (b m) -> p b m", m=4), axis=AX.X, op=ALU.add), r=[impp.b], w=[rA.b])
            cx.op("dve", lambda e: e.tensor_reduce(out=rB[0:np_, :], in_=impp[0:np_, 0:512].rearrange("p (b m) -> p b m", m=4), axis=AX.X, op=ALU.add), r=[impp.b], w=[rB.b])
            P.tt("dve", score.b, score[0:np_, 0:128], rA[0:np_, :], rB[0:np_, :], ALU.add, r=[rA.b, rB.b])
            P.memset("dve", score.b, score[0:np_, 128:NBK], 0.0)
            o_ = 128 - 2 * tq
            P.tt("dve", score.b, score[0:np_, :], score[0:np_, :], kp[0:np_, o_:o_ + NBK], ALU.mult, r=[score.b, kp.b])
            P.tt("dve", score.b, score[0:np_, :], score[0:np_, :], ad[0:np_, o_:o_ + NBK], ALU.add, r=[score.b, ad.b])
            P.memset("dve", score.b, score[0:np_, 0:1], 1.0e9)
            cx.op("dve", lambda e: e.max(out=m16[0:np_, 0:8], in_=score[0:np_, :]), r=[score.b], w=[m16.b])
            cx.op("dve", lambda e: e.match_replace(out=work[0:np_, :], in_to_replace=m16[0:np_, 0:8], in_values=score[0:np_, :], imm_value=-1e30), r=[score.b, m16.b], w=[work.b])
            cx.op("dve", lambda e: e.max(out=m16[0:np_, 8:16], in_=work[0:np_, :]), r=[work.b], w=[m16.b])
            P.ts("dve", thr.b, thr[0:np_, :], m16[0:np_, 15:16], -1e29, None, ALU.max, None, r=[m16.b])
            P.ts("dve", work.b, work[0:np_, :], score[0:np_, :], thr[0:np_, 0:1], 1.0, ALU.is_ge, ALU.subtract, r=[score.b, thr.b])
            P.ts("dve", selb.b, selb[0:np_, :], work[0:np_, :], -NEGM, None, ALU.mult, None, r=[work.b])
            cts = [ct for ct in range(4) if s0 - 2048 * ct >= 0]
            for j, ct in enumerate(cts):
                dl = s0 - 2048 * ct
                extra = []
                if dl < 2176:
                    extra.append((identb[:, :], (lambda nh, dl=dl: bc(mtc[:, dl + 128:dl + 128 + ncol], nh)), [identb.b, mtc.b]))
                s_unit(C.ckT[:, ct * 128:(ct + 1) * 128], kvh, extra, C.cvx[:, ct, kvh, :], j == 0, j == len(cts) - 1, [C.ckT.b, C.cvx.b])
            finish_branch(kvh, 0, True)
            for kt in range(tq + 1):
                se = selE[(kt + kvh) % 2]
                P.copy("dve", se.b, se[0:np_, :].rearrange("p (a b) -> p a b", a=2), selb[0:np_, 2 * kt:2 * kt + 2].unsqueeze(2).broadcast_to([np_, 2, 64]), r=[selb.b])
                extra = [(se[0:np_, :], (lambda nh: bc(qsel, nh)), [se.b, identb.b])]
                if kt == tq:
                    extra.append((identb[:, :], (lambda nh: bc(masks[:, 0, 0:ncol], nh)), [identb.b, masks.b]))
                s_unit(C.skT[:, kt * 128:(kt + 1) * 128], kvh, extra, C.svx[:, kt, kvh, :], kt == 0, kt == tq, [C.skT.b, C.svx.b])
            finish_branch(kvh, 1, False)
            kts = list(range(max(0, tq - 4), tq + 1))
            for j, kt in enumerate(kts):
                extra = []
                if kt == tq:
                    extra.append((identb[:, :], (lambda nh: bc(masks[:, 0, 0:ncol], nh)), [identb.b, masks.b]))
                elif kt == tq - 4:
                    extra.append((identb[:, :], (lambda nh: bc(masks[:, 1, 0:ncol], nh)), [identb.b, masks.b]))
                ws = kt % 5
                s_unit(C.wkT[:, ws * 128:(ws + 1) * 128], kvh, extra, C.wvx[:, ws, kvh, :], j == 0, j == len(kts) - 1, [C.wkT.b, C.wvx.b])
            finish_branch(kvh, 2, False)
            P.copy("act", ocols[0], ocols[1](kvh), oacc[:, kvh, 0:N].rearrange("p (i t) -> p i t", i=6), r=[oacc.b])
        first = True
        for mt in range(2):
            for h in range(4):
                ps = pS[sct[0] % 2]
                sct[0] += 1
                pt = PT[pct[0] % 3]
                pct[0] += 1
                P.mm(ps.b, ps[:, 0:ncol], memK[:, h // 2, mt * 128:(mt + 1) * 128], mqz[:, h % 2, h // 2, :], r=[C.qb])
                P.act(pt.b, pt[:, 0:ncol], ps[:, 0:ncol], AF.Exp, r=[ps.b], scale=0.125)
                P.mm(pU.b, pU[0:65, h * ncol:(h + 1) * ncol], memVv[:, mt, h, :], pt[:, 0:ncol], r=[pt.b], start=first, stop=(mt == 1 and h == 3))
                first = False
        w = 4 * ncol
        P.ts("dve", rd.b, rd[64:65, 0:w], pU[64:65, 0:w], 1e-20, None, ALU.max, None, r=[pU.b])
        cx.op("dve", lambda e: e.reciprocal(out=rd[64:65, 0:w], in_=rd[64:65, 0:w]), r=[rd.b], w=[rd.b])
        P.mm(pM.b, pM[0:64, 0:w], self.sel64[:, :], rd[:, 0:w], r=[self.sel64.b, rd.b])
        P.copy("act", Usb.b, Usb[:, 0:w], pU[0:64, 0:w], r=[pU.b])
        P.tt("dve", ocols[0], ocols[2], Usb[:, 0:w].rearrange("p (h t) -> p h t", h=4), pM[0:64, 0:w].rearrange("p (h t) -> p h t", h=4), ALU.mult, r=[Usb.b, pM.b])

    def epilogue(t, oT_, xin):
        for hf in range(2):
            for hh in range(16):
                P.mm(pM.b, pM[:, hf * 512:(hf + 1) * 512], oT_[:, hh, :], wout[:, hh, hf * 512:(hf + 1) * 512], r=[oT_.b, wout.b], start=(hh == 0), stop=(hh == 15))
        for hf in range(2):
            hs = slice(hf * 512, (hf + 1) * 512)
            P.stt(zt.b, zt[:, hs], xin[:, hs], ALPHA, pM[:, hs], ALU.mult, ALU.add, r=[xin.b, pM.b])
        P.ln_tile(zt, zt, lng, lnb, lnt)
        cx.dma("pool", self.x1[t * 128:(t + 1) * 128, :], zt[:, :], r=[zt.b], w=[self.x1b[t]])
        if self.debug:
            cx.dma("pool", self.dbg_x1[t * 128:(t + 1) * 128, :], zt[:, :], r=[zt.b])
        for kc in range(8):
            P.tr(pM.b, pM[:, kc * 128:(kc + 1) * 128], zt[:, kc * 128:(kc + 1) * 128], ident[:, :], r=[zt.b, ident.b])
        P.evac8(x1T_, pM)
        cx.dma("pool", self.x1T[:, :, t * 128:(t + 1) * 128].rearrange("k p t -> p k t"), x1T_[:, :, :], r=[x1T_.b], w=[self.x1b[t]])

    if unit == "p":
        win = P.sbp([128, 8, 1060], BF16)
        P.load_w_bf16(win, self.w_in_b, D, 1060, st)
        xin = [P.sbp([128, D])]
        xT = P.sbp([128, 8, 128], BF16)
        pr = P.sbp([128, 1060])
        cs, sn = P.sbp([128, 32]), P.sbp([128, 32])
        tm = [P.sbp([128, 12, 32]) for _ in range(4)]
        qb = P.sbp([128, 6, 2, 64])
        gates = P.sbp([128, 36])
        qz = P.sbp([128, 2, 6, 128], BF16)
        mqz = P.sbp([128, 2, 2, 128], BF16)
        gT = P.sbp([36, 128])
        P.memset("pool", qz.b, qz[:, :, :, :], 0.0)
        P.memset("pool", mqz.b, mqz[:, :, :, :], 0.0)
        P.memset("pool", C.qzS.b, C.qzS[:, :, :, :], 0.0)
        P.memset("pool", C.mqzS.b, C.mqzS[:, :, :, :], 0.0)
        for t in range(NT + 1):
            samp = (t == NT)
            xi = C.xinS if samp else xin[0]
            qz_, mqz_, gT_ = (C.qzS, C.mqzS, C.gTS) if samp else (qz, mqz, gT)
            cx.dma("sp", xi[:, :], self.x1[t * 128:(t + 1) * 128, :], r=[self.x1b[t]], w=[xi.b])
            cx.dma("sp", xT[:, :, :], self.x1T[:, :, t * 128:(t + 1) * 128].rearrange("k p t -> p k t"), r=[self.x1b[t]], w=[xT.b])
            cx.dma("sp", cs[:, :], self.c_cos[t * 128:(t + 1) * 128, :], w=[cs.b])
            cx.dma("sp", sn[:, :], self.c_sin[t * 128:(t + 1) * 128, :], w=[sn.b])
            for cb, (c0, c1) in enumerate(((0, 512), (512, 1024), (1024, 1060))):
                hb = pM[:, (cb % 2) * 512:(cb % 2) * 512 + (c1 - c0)]
                for kc in range(8):
                    P.mm(pM.b, hb, xT[:, kc, :], win[:, kc, c0:c1], r=[xT.b, win.b], start=(kc == 0), stop=(kc == 7))
                P.evac(pr.b, pr[:, c0:c1], hb, r=[pM.b])
            v4 = pr[:, 0:768].rearrange("p (h x i) -> p h x i", h=12, x=2)
            x1v, x2v = v4[:, :, 0, :], v4[:, :, 1, :]
            cb_ = cs[:, :].unsqueeze(1).broadcast_to([128, 12, 32])
            sb_ = sn[:, :].unsqueeze(1).broadcast_to([128, 12, 32])
            P.tt("dve", tm[0].b, tm[0][:, :, :], x1v, cb_, ALU.mult, r=[pr.b, cs.b])
            P.tt("pool", tm[1].b, tm[1][:, :, :], x2v, sb_, ALU.mult, r=[pr.b, sn.b])
            P.tt("dve", tm[2].b, tm[2][:, :, :], x1v, sb_, ALU.mult, r=[pr.b, sn.b])
            P.tt("pool", tm[3].b, tm[3][:, :, :], x2v, cb_, ALU.mult, r=[pr.b, cs.b])
            qv = qb[:, :, :, :].rearrange("p i k (x j) -> p k i x j", x=2)
            for kv in range(2):
                P.tt("dve", qb.b, qv[:, kv, :, 0, :], tm[0][:, kv * 6:(kv + 1) * 6, :], tm[1][:, kv * 6:(kv + 1) * 6, :], ALU.subtract, r=[tm[0].b, tm[1].b])
                P.tt("dve", qb.b, qv[:, kv, :, 1, :], tm[3][:, kv * 6:(kv + 1) * 6, :], tm[2][:, kv * 6:(kv + 1) * 6, :], ALU.add, r=[tm[2].b, tm[3].b])
            P.act(gates.b, gates[:, :], pr[:, 768:804], AF.Sigmoid, r=[pr.b])
            for b0 in (0, 3):
                for i in range(b0, b0 + 3):
                    P.tr(pM.b, pM[:, (i - b0) * 128:(i - b0 + 1) * 128], qb[:, i, :, :].rearrange("p k d -> p (k d)"), ident[:, :], r=[qb.b, ident.b])
                P.evac(qz_.b, qz_[0:64, 0, b0:b0 + 3, :], pM[0:64, 0:384].rearrange("p (k t) -> p k t", k=3), r=[pM.b])
                P.evac(qz_.b, qz_[64:128, 1, b0:b0 + 3, :], pM[64:128, 0:384].rearrange("p (k t) -> p k t", k=3), r=[pM.b])
            for pi in range(2):
                P.tr(pM.b, pM[:, 512 + pi * 128:512 + (pi + 1) * 128], pr[:, 804 + pi * 128:804 + (pi + 1) * 128], ident[:, :], r=[pr.b, ident.b])
            P.evac(mqz_.b, mqz_[0:64, 0, :, :], pM[0:64, 512:768].rearrange("p (k t) -> p k t", k=2), r=[pM.b])
            P.evac(mqz_.b, mqz_[64:128, 1, :, :], pM[64:128, 512:768].rearrange("p (k t) -> p k t", k=2), r=[pM.b])
            P.tr(pM.b, pM[0:36, 0:128], gates[:, :], ident[:, :], r=[gates.b, ident.b])
            P.copy("dve", gT_.b, gT_[:, :], pM[0:36, 0:128], r=[pM.b])
            if samp:
                break
            C.qb, C.gb = qz.b, gT.b
            ws = t % 5
            cx.dma("sp", C.wkT[:, ws * 128:(ws + 1) * 128], C.wkS[t], r=[C.wsb[t]], w=[C.wkT.b])
            cx.dma("sp", C.wvx[:, ws, :, :], C.wvS[t].rearrange("p (h d) -> p h d", h=2), r=[C.wsb[t]], w=[C.wvx.b])
            attend(t, 128, qz, mqz, gT[:, :], identb[:, 0:128], (oT.b, lambda kvh: oT[:, kvh * 6:(kvh + 1) * 6, :], oT[:, 12:16, :]), memKT, memV)
            epilogue(t, oT, xi)
    else:
        s = unit
        memKTs, memVs = memKT, memV
        for mt in range(2):
            cx.dma("sp", mks[:, :], self.cmem[1, s, mt * 128:(mt + 1) * 128].rearrange("m a h d -> m (a h d)"), w=[mks.b])
            P.mem_tile(mks, mt, memKTs, memVs, pM)
        C.qb, C.gb = C.qzS.b, C.gTS.b
        qzs = type("V", (), {"__getitem__": lambda _, k: C.qzS[k[0], k[1], k[2], s:s + 1]})()
        mqzs = type("V", (), {"__getitem__": lambda _, k: C.mqzS[k[0], k[1], k[2], s:s + 1]})()
        attend(NKT - 1, 1, qzs, mqzs, C.gTS[:, s:s + 1], identb[0:1, 0:1],
               (C.oTS.b, lambda kvh: C.oTS[:, kvh * 6:(kvh + 1) * 6, s:s + 1], C.oTS[:, 12:16, s:s + 1]), memKTs, memVs)
        if s == NS - 1:
            epilogue(NT, C.oTS, C.xinS)
    P.phase_end()


Model.l1_astage = l1_astage


def layer1(self):
    self.evac_dve = True
    self.marks = [("start", self.cx.ninstr)]
    self.phase_begin()
    C = self.l1_context()
    self.l1_kstage(C, "p")
    self.marks.append(("k_p", self.cx.ninstr))
    self.l1_astage(C, "p")
    self.marks.append(("a_p", self.cx.ninstr))
    for s in range(NS):
        self.l1_kstage(C, s)
        self.marks.append(("k_%d" % s, self.cx.ninstr))
        self.l1_astage(C, s)
        self.marks.append(("a_%d" % s, self.cx.ninstr))
    self.phase_end()
    self.evac_dve = False


Model.layer1 = layer1


def _core_inputs(inp, T, core, tabs):
    b = core % 2
    ss = slice(NS * core, NS * core + NS)
    d = {}
    d["xp"] = np.ascontiguousarray(inp["x_prompt"][b, :T])
    xs = np.zeros((128, D), np.float32)
    xs[:NS] = inp["x_sample"][ss, 0]
    d["xs"] = xs
    d["memp"] = np.ascontiguousarray(inp["mem_prompt"][b])
    for g in range(3):
        d["cdil%d" % g] = np.ascontiguousarray(inp["cache_dil_g%d" % g][0, ss])
    d["pool"] = inp["cache_nsa_kv"]
    d["cwin"] = np.ascontiguousarray(inp["cache_nsa_win"][ss])
    d["cmem"] = np.ascontiguousarray(inp["cache_mem_kv"][:, ss])
    d["ptab"] = np.ascontiguousarray(inp["page_table"][ss]).astype(np.int32)
    d.update(tabs)
    return d


def _shared_inputs(inp, T):
    d = {}
    d["w_in_a"] = np.ascontiguousarray(inp["w_in_a"][0])
    d["w_out_a"] = np.ascontiguousarray(inp["w_out_a"][0])
    d["w_in_b"] = np.ascontiguousarray(inp["w_in_b"][0])
    d["w_out_b"] = np.ascontiguousarray(inp["w_out_b"][0])
    d["w_mem_kv"] = np.ascontiguousarray(inp["w_mem_kv"])
    d["w_kv_b"] = np.ascontiguousarray(inp["w_kv_b"])
    d["cmp_pe"] = np.ascontiguousarray(inp["cmp_pe"].reshape(2, 16, 128).transpose(0, 2, 1))
    d["cmp_w1"] = np.ascontiguousarray(inp["cmp_w1"])
    w2 = inp["cmp_w2"]
    d["cmp_w2"] = np.ascontiguousarray(w2)
    d["cmp_w2r"] = np.ascontiguousarray(np.concatenate([w2[..., 32:], w2[..., :32]], -1))
    d["ln_g"] = np.ascontiguousarray(inp["ln_g"])
    d["ln_b"] = np.ascontiguousarray(inp["ln_b"])
    d["peer_wq"] = np.ascontiguousarray(inp["peer_wq"])
    d["peer_keysT"] = np.ascontiguousarray(inp["peer_keys"].transpose(0, 2, 4, 1, 3).reshape(2, 128, 8, 128))
    u = inp["peer_u"].reshape(2, 128, 128, 8, 128)
    d["peer_uT"] = np.ascontiguousarray(u.transpose(0, 1, 4, 3, 2))
    d["peer_v"] = np.ascontiguousarray(inp["peer_v"])
    c, s_ = rope_tables(T)
    d["c_cos"], d["c_sin"] = c, s_
    d["c_masks"] = host_masks()
    d["c_ident"] = np.eye(128, dtype=np.float32)
    d.update(host_l1_tables())
    return d


def build_model(T):
    m = Model(T)
    m.consts()
    m.layer0_attn()
    m.peer_p1(0)
    m.peer_p2(0, False)
    m.layer1()
    m.peer_p1(1)
    m.peer_p2(1, True)
    m.cx.finish()
    return m


def kernel(**inputs):
    inp = {k: np.asarray(v) for k, v in inputs.items()}
    T = inp["x_prompt"].shape[1]
    m = build_model(T)
    shared = _shared_inputs(inp, T)
    in_maps = [_core_inputs(inp, T, c, shared) for c in range(8)]
    res = run_bass_kernel_spmd(m.nc, in_maps, core_ids=list(range(8))).results
    B = inp["x_prompt"].shape[0]
    y_p = np.stack([res[b]["y_p"] for b in range(B)])
    y_s = np.concatenate([res[c]["y_s"][:NS] for c in range(8)])[:, None, :]
    outs = [y_p, y_s]
    for g in range(3):
        outs.append(np.stack([res[b]["dil_p%d" % g] for b in range(B)])[None])
    for g in range(3):
        outs.append(np.concatenate([res[c]["dil_s%d" % g] for c in range(8)])[None])
    outs.append(np.stack([res[b]["nsa_kv_p"] for b in range(B)]))
    outs.append(np.concatenate([res[c]["nsa_kv_s"][:NS] for c in range(8)])[:, None])
    outs.append(np.stack([res[b]["win_p"] for b in range(B)]))
    outs.append(np.concatenate([res[c]["win_s"] for c in range(8)]))
    outs.append(np.stack([res[b]["mem_p"] for b in range(B)], axis=1))
    return tuple(np.ascontiguousarray(o, dtype=np.float32) for o in outs)
```
